# Optimizing a Trainium2 kernel written in Bass

```python
import math
import jax
import jax.numpy as jnp
from jax import lax
import numpy as np

D_MODEL = 1024
BATCH = 16
SEQ = 2048
DEPTH = 2

GROUP_WIDTH = D_MODEL // 2
FOX_HEAD_DIM = 64
FOX_HEADS = GROUP_WIDTH // FOX_HEAD_DIM
FOX_BLOCK = 128
FOX_F_BIAS_INIT = 3.0
GDN_HEAD_DIM = 128
GDN_HEADS = GROUP_WIDTH // GDN_HEAD_DIM
GDN_CHUNK = 64
SHORT_CONV = 4
HGRN_EXPAND = 128
HGRN_HEAD_DIM = 128
HGRN_HEADS = GROUP_WIDTH // HGRN_HEAD_DIM
HGRN_CHUNK = 64
M2_HEAD_DIM = 64
M2_HEADS = GROUP_WIDTH // M2_HEAD_DIM
M2_GROUPS = 2
M2_STATE = 128
M2_CHUNK = 128
D_FF = 2816
FFN_CONV = 3
NORM_EPS = 1e-6

HGRN_KW = HGRN_HEADS * HGRN_EXPAND
M2_BC = M2_GROUPS * M2_STATE
EVEN_SPLITS = [GROUP_WIDTH, GROUP_WIDTH, GROUP_WIDTH, FOX_HEADS, 3 * GROUP_WIDTH, GDN_HEADS, GDN_HEADS]
EVEN_IN = sum(EVEN_SPLITS) + GROUP_WIDTH
ODD_SPLITS = [HGRN_KW, HGRN_KW, GROUP_WIDTH, GROUP_WIDTH, GROUP_WIDTH, GROUP_WIDTH + 2 * M2_BC]
ODD_IN = sum(ODD_SPLITS) + M2_HEADS

kernel_name = 'fox_gdn_hgrn2_ssd_hybrid'


def rmsnorm(x, gain):
    xf = x.astype(jnp.float32)
    y = xf * lax.rsqrt(jnp.mean(xf * xf, axis=-1, keepdims=True) + NORM_EPS)
    return (y * gain.astype(jnp.float32)).astype(x.dtype)


def l2norm(x):
    return x * lax.rsqrt(jnp.sum(x * x, axis=-1, keepdims=True) + NORM_EPS)


def causal_dwconv(x, w, b=None):
    K, C = w.shape
    y = lax.conv_general_dilated(x, w[:, None, :].astype(x.dtype), window_strides=(1,),
                                 padding=[(K - 1, 0)], dimension_numbers=('NWC', 'WIO', 'NWC'),
                                 feature_group_count=C)
    if b is not None:
        y = y + b.astype(y.dtype)
    return y


def _split(t, sizes):
    return jnp.split(t, [int(i) for i in np.cumsum(sizes)], axis=-1)


def _to_chunks(t, c):
    b, s = t.shape[:2]
    t = t.reshape((b, s // c, c) + t.shape[2:])
    return jnp.moveaxis(t, 3, 1)


def _from_chunks(t):
    t = jnp.moveaxis(t, 1, 3)
    return t.reshape((t.shape[0], t.shape[1] * t.shape[2]) + t.shape[3:])


def fox_attention(q, k, v, log_f):
    S_ = q.shape[1]
    scale = q.shape[-1] ** -0.5
    c = jnp.cumsum(log_f, axis=1).transpose(0, 2, 1)
    q, k, v = (t.transpose(0, 2, 1, 3) for t in (q, k, v))
    pos = jnp.arange(S_)
    outs = []
    for blk in range(S_ // FOX_BLOCK):
        lo, hi = blk * FOX_BLOCK, (blk + 1) * FOX_BLOCK
        s = jnp.einsum('bhqd,bhkd->bhqk', q[:, :, lo:hi], k[:, :, :hi]) * scale
        s = s + c[:, :, lo:hi, None] - c[:, :, None, :hi]
        s = jnp.where(pos[lo:hi, None] >= pos[None, :hi], s, -jnp.inf)
        p = jax.nn.softmax(s, axis=-1)
        outs.append(jnp.einsum('bhqk,bhkd->bhqd', p, v[:, :, :hi]))
    return jnp.concatenate(outs, axis=2).transpose(0, 2, 1, 3)


def gated_delta_rule(q, k, v, g, beta):
    B_, _, H, K = q.shape
    V = v.shape[-1]
    C = GDN_CHUNK
    q, k, v = _to_chunks(q * K ** -0.5, C), _to_chunks(k, C), _to_chunks(v, C)
    g, beta = _to_chunks(g, C), _to_chunks(beta, C)
    G = jnp.cumsum(g, axis=-1)
    tri = jnp.tril(jnp.ones((C, C), bool))
    gamma = jnp.exp(jnp.where(tri, G[..., :, None] - G[..., None, :], -jnp.inf))
    kb = k * beta[..., None]
    m = jnp.einsum('bhntk,bhnsk->bhnts', kb, k) * gamma
    m = jnp.where(jnp.tril(jnp.ones((C, C), bool), -1), m, 0.0) + jnp.eye(C, dtype=m.dtype)
    rhs = jnp.concatenate([v * beta[..., None], kb * jnp.exp(G)[..., None]], axis=-1)
    sol = lax.linalg.triangular_solve(m, rhs, left_side=True, lower=True)
    u0, w = sol[..., :V], sol[..., V:]
    a_qk = jnp.einsum('bhntk,bhnsk->bhnts', q, k) * gamma
    q_dec = q * jnp.exp(G)[..., None]
    k_dec = k * jnp.exp(G[..., -1:] - G)[..., None]
    g_last = jnp.exp(G[..., -1])
    xs = tuple(jnp.moveaxis(t, 2, 0) for t in (u0, w, a_qk, q_dec, k_dec, g_last))

    def step(S, inp):
        u0_n, w_n, a_n, q_n, k_n, gl_n = inp
        u = u0_n - jnp.einsum('bhck,bhkv->bhcv', w_n, S)
        o = jnp.einsum('bhck,bhkv->bhcv', q_n, S) + jnp.einsum('bhts,bhsv->bhtv', a_n, u)
        S = gl_n[..., None, None] * S + jnp.einsum('bhck,bhcv->bhkv', k_n, u)
        return S, o

    _, o = lax.scan(step, jnp.zeros((B_, H, K, V), jnp.float32), xs)
    return _from_chunks(jnp.moveaxis(o, 0, 2))


def hgrn2_recurrence(q, k, v, log_f):
    B_, _, H, K = q.shape
    V = v.shape[-1]
    C = HGRN_CHUNK
    q, k, v, log_f = (_to_chunks(t, C) for t in (q, k, v, log_f))
    G = jnp.cumsum(log_f, axis=3)
    tri = jnp.tril(jnp.ones((C, C), bool))[..., None]
    xs = tuple(jnp.moveaxis(t, 2, 0) for t in (q, k, v, G))

    def step(S, inp):
        q_n, k_n, v_n, G_n = inp
        dec = jnp.exp(jnp.where(tri, G_n[:, :, :, None, :] - G_n[:, :, None, :, :], -jnp.inf))
        a = jnp.einsum('bhtk,bhsk,bhtsk->bhts', q_n, k_n, dec)
        o = jnp.einsum('bhtk,bhkv->bhtv', q_n * jnp.exp(G_n), S) + jnp.einsum('bhts,bhsv->bhtv', a, v_n)
        G_last = G_n[:, :, -1]
        S = jnp.exp(G_last)[..., None] * S + jnp.einsum(
            'bhsk,bhsv->bhkv', k_n * jnp.exp(G_last[:, :, None] - G_n), v_n)
        return S, o

    _, o = lax.scan(step, jnp.zeros((B_, H, K, V), jnp.float32), xs)
    return _from_chunks(jnp.moveaxis(o, 0, 2))


def ssd_scan(x, dt, A, Bm, Cm):
    B_, _, H, P = x.shape
    N = Bm.shape[-1]
    C = M2_CHUNK
    X = _to_chunks(x * dt[..., None], C)
    Bc, Cc = _to_chunks(Bm, C), _to_chunks(Cm, C)
    A_cs = jnp.cumsum(_to_chunks(dt * A, C), axis=-1)
    tri = jnp.tril(jnp.ones((C, C), bool))
    L = jnp.exp(jnp.where(tri, A_cs[..., :, None] - A_cs[..., None, :], -jnp.inf))
    cb = jnp.einsum('bhnld,bhnsd->bhnls', Cc, Bc)
    y_diag = jnp.einsum('bhnls,bhnsp->bhnlp', cb * L, X)
    states = jnp.einsum('bhnsd,bhns,bhnsp->bhnpd', Bc, jnp.exp(A_cs[..., -1:] - A_cs), X)

    def step(h, inp):
        st, dec = inp
        return dec[..., None, None] * h + st, h

    _, prev = lax.scan(step, jnp.zeros((B_, H, P, N), jnp.float32),
                       (jnp.moveaxis(states, 2, 0), jnp.moveaxis(jnp.exp(A_cs[..., -1]), 2, 0)))
    prev = jnp.moveaxis(prev, 0, 2)
    y_off = jnp.einsum('bhnld,bhnpd,bhnl->bhnlp', Cc, prev, jnp.exp(A_cs))
    return _from_chunks(y_diag + y_off)


def fox_gdn_mixer(h, w_in, fox_f_bias, gdn_conv_w, gdn_A_log, gdn_dt_bias, gdn_norm_gain, w_out):
    B_, S_, _ = h.shape
    f32 = jnp.float32
    hd = lambda t, n: t.reshape(B_, S_, n, -1).astype(f32)
    fq, fk, fv, ff, g_qkv, g_a, g_b, g_gate = _split(h @ w_in, EVEN_SPLITS)
    log_f = jax.nn.log_sigmoid((ff + fox_f_bias).astype(f32))
    o_fox = fox_attention(hd(fq, FOX_HEADS), hd(fk, FOX_HEADS), hd(fv, FOX_HEADS), log_f)
    gq, gk, gv = jnp.split(jax.nn.silu(causal_dwconv(g_qkv, gdn_conv_w)), 3, axis=-1)
    beta = jax.nn.sigmoid(g_b.astype(f32))
    g = -jnp.exp(gdn_A_log.astype(f32)) * jax.nn.softplus(g_a.astype(f32) + gdn_dt_bias.astype(f32))
    o = gated_delta_rule(l2norm(hd(gq, GDN_HEADS)), l2norm(hd(gk, GDN_HEADS)), hd(gv, GDN_HEADS), g, beta)
    o_gdn = rmsnorm(o, gdn_norm_gain) * jax.nn.silu(hd(g_gate, GDN_HEADS))
    mixed = jnp.concatenate([o_fox.reshape(B_, S_, -1), o_gdn.reshape(B_, S_, -1)], axis=-1)
    return mixed.astype(h.dtype) @ w_out


def hgrn2_mamba2_mixer(h, w_in, hgrn_lb_logits, hgrn_norm_gain, m2_conv_w, m2_conv_b,
                       m2_A_log, m2_dt_bias, m2_D, m2_norm_gain, w_out, layer):
    B_, S_, _ = h.shape
    f32 = jnp.float32
    hd = lambda t, n: t.reshape(B_, S_, n, -1).astype(f32)
    hq, hf, hi, hg, mz, mxbc, mdt = _split(h @ w_in, ODD_SPLITS)
    lb_all = jax.nn.softmax(hgrn_lb_logits.astype(f32), axis=0)
    lb_all = jnp.cumsum(lb_all, axis=0) - lb_all[0]
    lb = lb_all[layer].reshape(HGRN_HEADS, HGRN_EXPAND)
    zf = hd(hf, HGRN_HEADS)
    f = lb + (1.0 - lb) * jax.nn.sigmoid(zf)
    k = (1.0 - lb) * jax.nn.sigmoid(-zf)
    o = hgrn2_recurrence(jax.nn.silu(hd(hq, HGRN_HEADS)), k, hd(hi, HGRN_HEADS), jnp.log(f))
    o_hgrn = rmsnorm(o, hgrn_norm_gain) * jax.nn.silu(hd(hg, HGRN_HEADS))
    xbc = jax.nn.silu(causal_dwconv(mxbc, m2_conv_w, m2_conv_b)).astype(f32)
    xs, Bs, Cs = jnp.split(xbc, [GROUP_WIDTH, GROUP_WIDTH + M2_BC], axis=-1)
    rep = M2_HEADS // M2_GROUPS
    Bs = jnp.repeat(Bs.reshape(B_, S_, M2_GROUPS, M2_STATE), rep, axis=2)
    Cs = jnp.repeat(Cs.reshape(B_, S_, M2_GROUPS, M2_STATE), rep, axis=2)
    xs = xs.reshape(B_, S_, M2_HEADS, M2_HEAD_DIM)
    dt = jax.nn.softplus(mdt.astype(f32) + m2_dt_bias.astype(f32))
    A = -jnp.exp(m2_A_log.astype(f32))
    y = ssd_scan(xs, dt, A, Bs, Cs) + m2_D.astype(f32)[:, None] * xs
    y = y.reshape(B_, S_, GROUP_WIDTH) * jax.nn.silu(mz.astype(f32))
    y = rmsnorm(y.reshape(B_, S_, M2_GROUPS, -1), m2_norm_gain.reshape(M2_GROUPS, -1))
    mixed = jnp.concatenate([o_hgrn.reshape(B_, S_, -1), y.reshape(B_, S_, -1)], axis=-1)
    return mixed.astype(h.dtype) @ w_out


def conv_ffn(h, w_up, conv_w, conv_b, w_down):
    u = causal_dwconv(h @ w_up, conv_w, conv_b)
    gate, up = jnp.split(u, 2, axis=-1)
    return (jax.nn.silu(gate) * up) @ w_down


def _dt_bias(key, n):
    dt = jnp.exp(jax.random.uniform(key, (n,), jnp.float32, math.log(1e-3), math.log(1e-1)))
    return dt + jnp.log(-jnp.expm1(-dt))


def setup_inputs(seed: int = 0) -> dict:
    key = jax.random.key(seed)
    ks = iter(jax.random.split(key, 32))
    nrm = lambda shape, scale: jax.random.normal(next(ks), shape, jnp.float32) * scale
    W = GROUP_WIDTH
    return {
        'x': nrm((BATCH, SEQ, D_MODEL), 1.0),
        'norm_gains': 1.0 + nrm((DEPTH, 4, D_MODEL), 0.02),
        'w_out': nrm((DEPTH, D_MODEL, D_MODEL), D_MODEL ** -0.5),
        'ffn_w_up': nrm((DEPTH, D_MODEL, 2 * D_FF), D_MODEL ** -0.5),
        'ffn_conv_w': nrm((DEPTH, FFN_CONV, 2 * D_FF), FFN_CONV ** -0.5),
        'ffn_conv_b': nrm((DEPTH, 2 * D_FF), 0.02),
        'ffn_w_down': nrm((DEPTH, D_FF, D_MODEL), D_FF ** -0.5),
        'even_w_in': nrm((D_MODEL, EVEN_IN), D_MODEL ** -0.5),
        'fox_f_bias': FOX_F_BIAS_INIT + nrm((FOX_HEADS,), 0.1),
        'gdn_conv_w': nrm((SHORT_CONV, 3 * W), SHORT_CONV ** -0.5),
        'gdn_A_log': jnp.log(jax.random.uniform(next(ks), (GDN_HEADS,), jnp.float32, 1.0, 16.0)),
        'gdn_dt_bias': _dt_bias(next(ks), GDN_HEADS),
        'gdn_norm_gain': 1.0 + nrm((GDN_HEAD_DIM,), 0.02),
        'odd_w_in': nrm((D_MODEL, ODD_IN), D_MODEL ** -0.5),
        'hgrn_lb_logits': nrm((DEPTH, HGRN_KW), 0.1),
        'hgrn_norm_gain': 1.0 + nrm((HGRN_HEAD_DIM,), 0.02),
        'm2_conv_w': nrm((SHORT_CONV, W + 2 * M2_BC), SHORT_CONV ** -0.5),
        'm2_conv_b': nrm((W + 2 * M2_BC,), 0.02),
        'm2_A_log': jnp.log(jax.random.uniform(next(ks), (M2_HEADS,), jnp.float32, 1.0, 16.0)),
        'm2_dt_bias': _dt_bias(next(ks), M2_HEADS),
        'm2_D': 1.0 + nrm((M2_HEADS,), 0.1),
        'm2_norm_gain': 1.0 + nrm((W,), 0.02),
    }


def reference(x, norm_gains, w_out, ffn_w_up, ffn_conv_w, ffn_conv_b, ffn_w_down,
              even_w_in, fox_f_bias, gdn_conv_w, gdn_A_log, gdn_dt_bias, gdn_norm_gain,
              odd_w_in, hgrn_lb_logits, hgrn_norm_gain, m2_conv_w, m2_conv_b,
              m2_A_log, m2_dt_bias, m2_D, m2_norm_gain):
    for layer in range(DEPTH):
        g = norm_gains[layer]
        hn = rmsnorm(x, g[0])
        if layer % 2 == 0:
            mix = fox_gdn_mixer(hn, even_w_in, fox_f_bias, gdn_conv_w, gdn_A_log, gdn_dt_bias,
                                gdn_norm_gain, w_out[layer])
        else:
            mix = hgrn2_mamba2_mixer(hn, odd_w_in, hgrn_lb_logits, hgrn_norm_gain, m2_conv_w,
                                     m2_conv_b, m2_A_log, m2_dt_bias, m2_D, m2_norm_gain,
                                     w_out[layer], layer)
        x = x + rmsnorm(mix, g[1])
        hn = rmsnorm(x, g[2])
        x = x + rmsnorm(conv_ffn(hn, ffn_w_up[layer], ffn_conv_w[layer], ffn_conv_b[layer],
                                 ffn_w_down[layer]), g[3])
    return x
```

```python
import contextlib
import numpy as np
import concourse.bass as bass
import concourse.mybir as mybir
from concourse.bass_utils import run_bass_kernel_spmd

F32 = mybir.dt.float32
BF16 = mybir.dt.bfloat16
ALU = mybir.AluOpType
AF = mybir.ActivationFunctionType
AX = mybir.AxisListType

PE, ACT, DVE, POOL, SP = "tensor", "scalar", "vector", "gpsimd", "sync"
EPOCH = 30000
NDSEM = 8

D = 1024
DFF = 2816
NFF = 22
EPS = 1e-6
NCORES = 8


import heapq


class V:
    def __init__(self, ap):
        self.ap = ap

    def __getitem__(self, k):
        return self.ap if (isinstance(k, slice) and k == slice(None)) else self.ap[k]


class Res:
    __slots__ = ("lw", "rd")

    def __init__(self):
        self.lw = None
        self.rd = []


class _Rec:
    class _I:
        def then_inc(self, *a, **k):
            return self

    def __init__(self):
        self.calls = []

    def __getattr__(self, name):
        def f(*a, **k):
            self.calls.append((name, a, k))
            return _Rec._I()
        return f


def _free(ap):
    n = 1
    for d in ap.shape[1:]:
        n *= d
    return n


def _cost(eng, fn, is_dma):
    r = _Rec()
    try:
        fn(r)
    except Exception:
        return 500.0
    c = 0.0
    for name, a, k in r.calls:
        out = k.get("out", a[0] if a else None)
        try:
            n = _free(out)
        except Exception:
            n = 128
        if is_dma:
            try:
                byts = n * out.shape[0] * (4 if out.dtype == F32 else 2)
            except Exception:
                byts = 65536
            c += 2000.0 + byts / 150.0
        elif eng == PE:
            lhsT = k.get("lhsT", a[1] if len(a) > 1 else None)
            f32 = False
            try:
                f32 = (lhsT is not None and lhsT.dtype == F32)
            except Exception:
                pass
            if name == "transpose":
                c += 110.0
            else:
                c += (max(n, 64) / 2.4 + 25.0) * (4.0 if f32 else 1.0)
        elif eng == ACT:
            c += 230.0 + 0.85 * n
        elif eng == DVE:
            c += 120.0 + 1.05 * n
        else:
            c += 150.0 + 1.7 * n
    return max(c, 60.0)


class Op:
    __slots__ = ("eng", "fn", "deps", "is_dma", "sig", "ticket", "dslot", "dtarget", "cost", "seg", "nobar", "is_bar", "fin", "start", "tag")

    def __init__(self, eng, fn, is_dma):
        self.eng = eng
        self.fn = fn
        self.deps = set()
        self.is_dma = is_dma
        self.sig = False
        self.ticket = None
        self.nobar = False
        self.is_bar = False
        self.fin = 0.0


SYNC_LAT = 500.0
SCHEDULE = True
PRIO_CP = False


class Prog:
    def __init__(self, nc):
        self.nc = nc
        self.ops = []
        self.seg = 0
        self.bar = None

    def op(self, eng, fn, reads=(), writes=(), dma=False, nobar=False):
        i = len(self.ops)
        o = Op(eng, fn, dma)
        o.cost = _cost(eng, fn, dma)
        o.seg = self.seg
        import sys as _sys
        fr = _sys._getframe(1)
        if fr.f_code.co_name in ("dma", "proj_fm", "proj_tm", "_proj_fm", "_proj_tm", "rstd_from_ss"):
            fr = fr.f_back
        o.tag = f"{fr.f_code.co_name}:{fr.f_lineno}"
        o.start = 0.0
        o.nobar = nobar
        for r in reads:
            if r.lw is not None:
                o.deps.add(r.lw)
        for w in writes:
            if w.lw is not None:
                o.deps.add(w.lw)
            o.deps.update(w.rd)
        for r in reads:
            r.rd.append(i)
        for w in writes:
            w.lw = i
            w.rd = []
        if not nobar and self.bar is not None:
            o.deps.add(self.bar)
        o.deps.discard(i)
        self.ops.append(o)
        return i

    def dma(self, q, out, in_, reads=(), writes=(), nobar=False):
        return self.op(q, lambda e: e.dma_start(out=out, in_=in_), reads, writes, dma=True, nobar=nobar)

    def barrier(self, scratch_ap):
        i = len(self.ops)
        o = Op(POOL, lambda e: e.memset(scratch_ap, 0.0), False)
        o.cost = 100.0
        o.seg = self.seg
        o.is_bar = True
        if self.bar is not None:
            o.deps.add(self.bar)
        self.ops.append(o)
        self.bar = i
        self.seg += 1
        return i

    def schedule(self):
        ops = self.ops
        order = {}
        eng_free = {}
        self.seg_stats = []
        segs = {}
        for i, o in enumerate(ops):
            segs.setdefault(o.seg, []).append(i)
        tglob = 0.0
        for sg in sorted(segs):
            idxs = segs[sg]
            inseg = set(idxs)
            bar_i = None
            body = []
            for i in idxs:
                if ops[i].is_bar:
                    bar_i = i
                else:
                    body.append(i)
            if not SCHEDULE:
                for i in body:
                    order.setdefault(ops[i].eng, []).append(i)
            else:
                ndeps = {}
                users = {}
                for i in body:
                    c = 0
                    for j in ops[i].deps:
                        if j in inseg and not ops[j].is_bar:
                            c += 1
                            users.setdefault(j, []).append(i)
                    ndeps[i] = c
                bl = {}
                for i in reversed(body):
                    m = 0.0
                    for u in users.get(i, ()):
                        lat = 0.0 if (ops[u].eng == ops[i].eng == PE) else SYNC_LAT
                        m = max(m, bl[u] + lat)
                    bl[i] = m + ops[i].cost
                ready = {}
                rtime = {}
                for i in body:
                    if ndeps[i] == 0:
                        rtime[i] = tglob
                        heapq.heappush(ready.setdefault(ops[i].eng, []), (tglob, i))
                for e in ready:
                    eng_free.setdefault(e, tglob)
                remaining = len(body)
                while remaining:
                    best = None
                    for e, hp in ready.items():
                        if not hp:
                            continue
                        ef = max(eng_free.get(e, tglob), tglob)
                        cand = None
                        avail = [x for x in hp if x[0] <= ef]
                        if avail:
                            if PRIO_CP:
                                ci = min(avail, key=lambda x: (-bl[x[1]], x[1]))
                            else:
                                ci = min(avail, key=lambda x: x[1])
                            cand = (ef, ci[1], e, ci)
                        else:
                            ci = hp[0]
                            cand = (ci[0], ci[1], e, ci)
                        if best is None or cand[:2] < best[:2]:
                            best = cand
                    start, i, e, item = best
                    hp = ready[e]
                    hp.remove(item)
                    heapq.heapify(hp)
                    o = ops[i]
                    if o.is_dma:
                        eng_free[e] = start + (6000.0 if e == POOL else 70.0)
                        o.fin = start + o.cost
                    else:
                        eng_free[e] = start + o.cost
                        o.fin = start + o.cost
                    order.setdefault(e, []).append(i)
                    o.start = start
                    remaining -= 1
                    for u in users.get(i, ()):
                        ndeps[u] -= 1
                        lat = 0.0 if (ops[u].eng == e and e == PE) else SYNC_LAT
                        rtime[u] = max(rtime.get(u, tglob), o.fin + lat)
                        if ndeps[u] == 0:
                            heapq.heappush(ready.setdefault(ops[u].eng, []), (rtime[u], u))
                if body:
                    t_prev = tglob
                    tglob = max([tglob] + [ops[i].fin for i in body])
                    busy = {}
                    for i in body:
                        if not ops[i].is_dma:
                            busy[ops[i].eng] = busy.get(ops[i].eng, 0.0) + ops[i].cost
                    self.seg_stats.append((sg, tglob - t_prev, busy))
            if bar_i is not None:
                b = ops[bar_i]
                last = {}
                for i in body:
                    o = ops[i]
                    if o.nobar:
                        continue
                    if o.is_dma:
                        b.deps.add(i)
                for e, lst in order.items():
                    for i in reversed(lst):
                        if ops[i].seg != sg:
                            break
                        if not ops[i].is_dma and not ops[i].nobar:
                            b.deps.add(i)
                            break
                order.setdefault(POOL, []).append(bar_i)
                eng_free[POOL] = tglob + 100.0
                tglob += 300.0
        self.est_ns = tglob
        return order

    def emit(self, final_wait_ops=()):
        nc = self.nc
        ops = self.ops
        per_eng = self.schedule()
        for i, o in enumerate(ops):
            for j in o.deps:
                d = ops[j]
                if d.is_dma:
                    continue
                if d.eng == PE and o.eng == PE and not o.is_dma:
                    continue
                d.sig = True
        for j in final_wait_ops:
            if not ops[j].is_dma:
                ops[j].sig = True
        cnt = {}
        dcnt = {}
        for eng, lst in per_eng.items():
            for i in lst:
                o = ops[i]
                if o.is_dma:
                    k = dcnt.get(eng, 0)
                    dcnt[eng] = k + 1
                    o.dslot = k % NDSEM
                    o.dtarget = 16 * (k // NDSEM + 1)
                elif o.sig:
                    cnt[eng] = cnt.get(eng, 0) + 1
                    o.ticket = cnt[eng]
        with contextlib.ExitStack() as st:
            sems = {}
            for eng, n in cnt.items():
                for ep in range((n + EPOCH - 1) // EPOCH + 1):
                    sems[(eng, ep)] = st.enter_context(nc.semaphore(f"s_{eng}_{ep}"))
            dsems = {}
            for q in dcnt:
                for s in range(NDSEM):
                    dsems[(q, s)] = st.enter_context(nc.semaphore(f"d_{q}_{s}"))
            block = st.enter_context(nc.Block())

            def run_engine(engname):
                def body(e):
                    waited = {}
                    for i in per_eng.get(engname, []):
                        o = ops[i]
                        need = {}
                        for j in o.deps:
                            d = ops[j]
                            if d.is_dma:
                                key = ("d", d.eng, d.dslot)
                                need[key] = max(need.get(key, 0), d.dtarget)
                            else:
                                if d.eng == PE and o.eng == PE and not o.is_dma:
                                    continue
                                ep = (d.ticket - 1) // EPOCH
                                key = ("c", d.eng, ep)
                                need[key] = max(need.get(key, 0), d.ticket - ep * EPOCH)
                        if o.is_dma and o.dtarget > 16:
                            key = ("d", o.eng, o.dslot)
                            need[key] = max(need.get(key, 0), o.dtarget - 16)
                        for key, v in need.items():
                            if waited.get(key, 0) >= v:
                                continue
                            waited[key] = v
                            s = dsems[(key[1], key[2])] if key[0] == "d" else sems[(key[1], key[2])]
                            e.wait_ge(s, v)
                        ins = o.fn(e)
                        if o.is_dma:
                            ins.then_inc(dsems[(o.eng, o.dslot)], 16)
                        elif o.sig:
                            ep = (o.ticket - 1) // EPOCH
                            ins.then_inc(sems[(o.eng, ep)], 1)
                    if engname == SP:
                        for j in final_wait_ops:
                            d = ops[j]
                            if d.is_dma:
                                e.wait_ge(dsems[(d.eng, d.dslot)], d.dtarget)
                            else:
                                ep = (d.ticket - 1) // EPOCH
                                e.wait_ge(sems[(d.eng, ep)], d.ticket - ep * EPOCH)
                return body

            for engname in (SP, ACT, POOL, DVE, PE):
                if engname in per_eng or engname == SP:
                    getattr(block, engname)(run_engine(engname))


def _kp(w):
    n = w.shape[1]
    return np.ascontiguousarray(w.reshape(8, 128, n).transpose(1, 0, 2).reshape(128, 8 * n))


def _pc(v):
    return np.ascontiguousarray(v.reshape(-1, 128).T)


class Pack:
    def __init__(self):
        self.parts = []
        self.off = {}
        self.n = 0

    def add(self, name, arr):
        arr = np.asarray(arr, np.float32)
        assert arr.shape[0] == 128, (name, arr.shape)
        arr = arr.reshape(128, -1)
        self.off[name] = (self.n, arr.shape[1])
        self.parts.append(arr)
        self.n += arr.shape[1]

    def cat(self):
        return np.ascontiguousarray(np.concatenate(self.parts, axis=1))


EVEN_GROUPS = {}
for j in range(4):
    EVEN_GROUPS[f"fq{j}"] = (128 * j, 128)
    EVEN_GROUPS[f"fk{j}"] = (512 + 128 * j, 128)
EVEN_GROUPS["fv"] = (1024, 512)
EVEN_GROUPS["ff"] = (1536, 8)
for h in range(4):
    EVEN_GROUPS[f"gq{h}"] = (1544 + 128 * h, 128)
    EVEN_GROUPS[f"gk{h}"] = (2056 + 128 * h, 128)
    EVEN_GROUPS[f"gv{h}"] = (2568 + 128 * h, 128)
EVEN_GROUPS["gab"] = (3080, 8)
EVEN_GROUPS["gg"] = (3088, 512)
ODD_GROUPS = {}
for h in range(4):
    ODD_GROUPS[f"hq{h}"] = (128 * h, 128)
    ODD_GROUPS[f"hf{h}"] = (512 + 128 * h, 128)
ODD_GROUPS["hi"] = (1024, 512)
ODD_GROUPS["hg"] = (1536, 512)
ODD_GROUPS["mz"] = (2048, 512)
for c in range(8):
    ODD_GROUPS[f"mx{c}"] = (2560 + 128 * c, 128)
ODD_GROUPS["mdt"] = (3584, 8)


def weight_layout():
    lay = []
    for l in range(2):
        groups = EVEN_GROUPS if l == 0 else ODD_GROUPS
        for name, (c0, n) in groups.items():
            lay.append((f"in{l}_{name}", 8 * n))
        lay.append((f"out{l}", 8 * 1024))
        for j in range(NFF):
            lay.append((f"up{l}_{j}", 8 * 256))
        for hf in range(2):
            lay.append((f"down{l}_{hf}", NFF * 512))
    return lay


def host_pack(inp):
    W = Pack()
    for l in range(2):
        groups = EVEN_GROUPS if l == 0 else ODD_GROUPS
        w_in = inp["even_w_in"] if l == 0 else inp["odd_w_in"]
        for name, (c0, n) in groups.items():
            W.add(f"in{l}_{name}", _kp(w_in[:, c0:c0 + n]))
        W.add(f"out{l}", _kp(inp["w_out"][l]))
        wu = inp["ffn_w_up"][l]
        for j in range(NFF):
            t = np.concatenate([wu[:, j * 128:(j + 1) * 128], wu[:, DFF + j * 128:DFF + (j + 1) * 128]], axis=1)
            W.add(f"up{l}_{j}", _kp(t))
        wd = inp["ffn_w_down"][l]
        wd = wd.reshape(NFF, 128, 1024).transpose(1, 0, 2)
        for hf in range(2):
            W.add(f"down{l}_{hf}", wd[:, :, hf * 512:(hf + 1) * 512])
    Pp = Pack()
    rep = lambda v: np.broadcast_to(np.asarray(v, np.float32).reshape(1, -1), (128, np.asarray(v).size))
    for l in range(2):
        cw = inp["ffn_conv_w"][l]
        Pp.add(f"fcw{l}", np.stack([_pc(cw[k]) for k in range(3)], axis=2))
        Pp.add(f"fcb{l}", _pc(inp["ffn_conv_b"][l]))
    Pp.add("gcw", np.stack([_pc(inp["gdn_conv_w"][k]) for k in range(4)], axis=2))
    Pp.add("mcw", np.stack([_pc(inp["m2_conv_w"][k]) for k in range(4)], axis=2))
    Pp.add("mcb", _pc(inp["m2_conv_b"]))
    Pp.add("lb0", _pc(inp["hgrn_lb_logits"][0]))
    Pp.add("lb1", _pc(inp["hgrn_lb_logits"][1]))
    Pp.add("foxb", np.broadcast_to(inp["fox_f_bias"].reshape(8, 1), (8, 1)).repeat(16, axis=0).reshape(128, 1))
    Pp.add("foxb8", np.concatenate([inp["fox_f_bias"].reshape(8, 1), np.zeros((120, 1), np.float32)], axis=0))
    Pp.add("gAlog", rep(inp["gdn_A_log"]))
    Pp.add("gdtb", rep(inp["gdn_dt_bias"]))
    Pp.add("mAlog", rep(inp["m2_A_log"]))
    Pp.add("mdtb", rep(inp["m2_dt_bias"]))
    Pp.add("mD", rep(inp["m2_D"]))
    Pp.add("gng", rep(inp["gdn_norm_gain"]))
    Pp.add("hng", rep(inp["hgrn_norm_gain"]))
    Pp.add("mng", rep(inp["m2_norm_gain"]))
    G = np.broadcast_to(inp["norm_gains"].reshape(1, 8, 1024), (128, 8, 1024))
    return W, Pp, np.ascontiguousarray(G.reshape(128, 8 * 1024))


class Builder:
    def __init__(self, S, NSEQ, woff, poff, wtot, ptot, layers=(0, 1), mixers=True, dbg=None):
        self.S, self.NSEQ = S, NSEQ
        self.NT = S // 128
        self.NB = S // 512
        self.woff, self.poff = woff, poff
        self.layers = layers
        self.mixers = mixers
        self.dbg = dbg or {}
        self.uid = 0
        nc = self.nc = bass.Bass("TRN2", target_bir_lowering=False)
        self.P = Prog(nc)
        self.x_d = nc.dram_tensor("x", [NSEQ, S, D], F32, kind="ExternalInput").ap()
        self.w_d = nc.dram_tensor("wbig", [128, wtot], F32, kind="ExternalInput").ap()
        self.pp_d = nc.dram_tensor("pp", [128, ptot], F32, kind="ExternalInput").ap()
        self.gn_d = nc.dram_tensor("gains", [128, 8 * 1024], F32, kind="ExternalInput").ap()
        self.y_d = nc.dram_tensor("y", [NSEQ, S, D], F32, kind="ExternalOutput").ap()
        self.wbf = nc.dram_tensor("wbf", [128, wtot], BF16, kind="Internal").ap()
        self.xspill = nc.dram_tensor("xspill", [14, 128, D], F32, kind="Internal").ap()
        self.wtot, self.ptot = wtot, ptot
        self.wres = {name: Res() for name in woff}
        self.wcast_done = set()
        self.out_dmas = []
        self.dbg_out = {}

    def sb(self, st, shape, dt, name="t"):
        self.uid += 1
        return st.enter_context(self.nc.sbuf_tensor(f"{name}_{self.uid}", list(shape), dt))

    def ps(self, st, shape, dt, name="p"):
        self.uid += 1
        return st.enter_context(self.nc.psum_tensor(f"{name}_{self.uid}", list(shape), dt))

    def wseg(self, name):
        o, n = self.woff[name]
        return self.wbf[:, o:o + n], self.wres[name]

    def pcol(self, name):
        o, n = self.poff[name]
        return self.PP[:, o:o + n]

    def dbg_dump(self, key, ap_sbuf, shape, res):
        if key not in self.dbg:
            return
        dt = ap_sbuf.dtype
        t = self.nc.dram_tensor(f"dbg_{key}", list(shape), dt, kind="ExternalOutput").ap()
        self.out_dmas.append(self.P.dma(SP, t, ap_sbuf, reads=res))

    def cast_weights(self, prefixes):
        P = self.P
        for name, (o, n) in self.woff.items():
            if not any(name.startswith(p) for p in prefixes):
                continue
            if name in self.wcast_done:
                continue
            self.wcast_done.add(name)
            r = self.wres[name]
            c = 0
            while c < n:
                m = min(8192, n - c)
                P.dma(POOL, self.wbf[:, o + c:o + c + m], self.w_d[:, o + c:o + c + m], writes=[r], nobar=True)
                c += m

    def rstd_from_ss(self, ss_ap, rs_ap, res_ss, res_rs, n):
        P = self.P
        P.op(ACT, lambda e: e.activation(out=rs_ap, in_=ss_ap, func=AF.Ln, scale=1.0 / n, bias=self.eps_t[:, 0:1]), [res_ss], [res_rs])
        P.op(ACT, lambda e: e.activation(out=rs_ap, in_=rs_ap, func=AF.Exp, scale=-0.5), [res_rs], [res_rs])

    def prenorm_tile(self, st_unused, t, gain_idx, dst_ap, dst_res):
        P = self.P
        X, rX = self.X, self.rX
        k = t % 2
        junk = self.junk_bf[0]
        ss, rss = self.ss[k], self.r_ss[k]
        hn, rhn = self.hn[k], self.r_hn[k]
        pt, rpt = self.pt[k], self.r_pt[k]
        gB, rgB = self.gainB[gain_idx % 2], self.r_gainB[gain_idx % 2]
        P.op(ACT, lambda e: e.activation(out=hn[:], in_=X[:, t, :], func=AF.Square, accum_out=ss[:, 0:1]), [rX[t]], [rss, rhn])
        self.rstd_from_ss(ss[:, 0:1], ss[:, 1:2], rss, rss, D)
        P.op(DVE, lambda e: e.scalar_tensor_tensor(out=hn[:], in0=X[:, t, :], scalar=ss[:, 1:2], in1=gB[:], op0=ALU.mult, op1=ALU.mult), [rX[t], rss, rgB], [rhn])

        def tr(e):
            ins = None
            for c in range(8):
                ins = e.transpose(pt[:, c, :], hn[:, c * 128:(c + 1) * 128], self.ident_bf[:])
            return ins
        P.op(PE, tr, [rhn], [rpt])
        P.op(ACT, lambda e: e.activation(out=dst_ap, in_=pt[:], func=AF.Copy), [rpt], [dst_res])

    def load_gain(self, l, i):
        idx = l * 4 + i
        gB, rgB = self.gainB[idx % 2], self.r_gainB[idx % 2]
        self.P.dma(SP, gB[:], self.gn_d[:, idx * 1024:(idx + 1) * 1024], writes=[rgB])
        return idx

    def postnorm_residual(self, t, pacc, rpacc, gain_idx, junk=None, evac=None):
        P = self.P
        X, rX = self.X, self.rX
        k = t % 2
        ss, rss = self.ss[k], self.r_ss[k]
        tmp, rtmp = self.tmp32[0], self.r_tmp32[0]
        jb, rjb = junk if junk is not None else (tmp, rtmp)
        gB, rgB = self.gainB[gain_idx % 2], self.r_gainB[gain_idx % 2]
        if evac is not None:
            cp, rcp = evac
            P.op(DVE, lambda e: e.tensor_copy(out=cp[:], in_=pacc[:]), [], [rcp, rpacc])
            P.op(ACT, lambda e: e.activation(out=jb[:], in_=cp[:], func=AF.Square, accum_out=ss[:, 0:1]), [rcp], [rss, rjb])
            self.rstd_from_ss(ss[:, 0:1], ss[:, 1:2], rss, rss, D)
            P.op(DVE, lambda e: e.scalar_tensor_tensor(out=tmp[:], in0=cp[:], scalar=ss[:, 1:2], in1=gB[:], op0=ALU.mult, op1=ALU.mult), [rss, rgB, rcp], [rtmp])
            P.op(POOL, lambda e: e.tensor_tensor(out=X[:, t, :], in0=X[:, t, :], in1=tmp[:], op=ALU.add), [rtmp, rX[t]], [rX[t]])
            return
        P.op(ACT, lambda e: e.activation(out=jb[:], in_=pacc[:], func=AF.Square, accum_out=ss[:, 0:1]), [], [rss, rjb, rpacc])
        self.rstd_from_ss(ss[:, 0:1], ss[:, 1:2], rss, rss, D)
        P.op(DVE, lambda e: e.scalar_tensor_tensor(out=tmp[:], in0=pacc[:], scalar=ss[:, 1:2], in1=gB[:], op0=ALU.mult, op1=ALU.mult), [rss, rgB], [rtmp, rpacc])
        P.op(POOL, lambda e: e.tensor_tensor(out=X[:, t, :], in0=X[:, t, :], in1=tmp[:], op=ALU.add), [rtmp, rX[t]], [rX[t]])

    def ffn_phase(self, l):
        P = self.P
        S, NT, NB = self.S, self.NT, self.NB
        with contextlib.ExitStack() as st:
            HTb = self.sb(st, [128, 8, 512], BF16, "HTb"); rHTb = [Res() for _ in range(4)]
            AT = self.sb(st, [128, NFF, 512], BF16, "AT"); rAT = [Res() for _ in range(NFF)]
            Wd = [self.sb(st, [128, NFF, 512], BF16, "Wd") for _ in range(2)]; rWd = [Res(), Res()]
            NWU = 4
            Wu = [self.sb(st, [128, 8, 256], BF16, "Wu") for _ in range(NWU)]; rWu = [Res() for _ in range(NWU)]
            U = [[self.sb(st, [128, 514], F32, "U") for _ in range(2)] for _ in range(2)]
            rU = [[Res(), Res()], [Res(), Res()]]
            Y = [[self.sb(st, [128, 512], F32, "Y") for _ in range(2)] for _ in range(2)]
            rY = [[Res(), Res()], [Res(), Res()]]
            HALO = self.sb(st, [128, 2 * NFF, 2], F32, "HALO"); rH = [Res() for _ in range(2 * NFF)]
            pu = [[self.ps(st, [128, 512], F32, "pu") for _ in range(2)] for _ in range(2)]
            rpu = [[Res(), Res()], [Res(), Res()]]
            pacc = [self.ps(st, [128, 1024], F32, "pacc") for _ in range(1)]; rpacc = [Res()]
            pcp = [self.sb(st, [128, 1024], F32, "pcp") for _ in range(2)]; rpcp = [Res(), Res()]
            cwo, _ = self.poff[f"fcw{l}"]
            cbo, _ = self.poff[f"fcb{l}"]
            PP = self.PP
            g_pre = self.load_gain(l, 2)
            nxt = [x for x in self.layers if x > l]
            if nxt:
                self.cast_weights([f"in{nxt[0]}_"])
            P.op(POOL, lambda e: e.memset(HALO[:], 0.0), [], rH)
            for hf in range(2):
                seg, rs = self.wseg(f"down{l}_{hf}")
                P.dma(SP, Wd[hf][:], seg.rearrange("p (j c) -> p j c", j=NFF), reads=[rs], writes=[rWd[hf]])
            for b in range(NB):
                for tt in range(4):
                    t = b * 4 + tt
                    self.prenorm_tile(None, t, g_pre, HTb[:, :, tt * 128:(tt + 1) * 128], rHTb[tt])
                if b == 0:
                    g_post = self.load_gain(l, 3)
                for j in range(NFF):
                    k = j % 2
                    kw = j % NWU
                    seg, rs = self.wseg(f"up{l}_{j}")
                    P.dma(SP, Wu[kw][:], seg.rearrange("p (k c) -> p k c", k=8), reads=[rs], writes=[rWu[kw]])
                    for gu in range(2):
                        def mm(e, gu=gu, k=k, kw=kw):
                            ins = None
                            for kk in range(8):
                                ins = e.matmul(pu[gu][k][:], lhsT=Wu[kw][:, kk, gu * 128:(gu + 1) * 128], rhs=HTb[:, kk, :],
                                               start=(kk == 0), stop=(kk == 7))
                            return ins
                        P.op(PE, mm, [rWu[kw]] + rHTb, [rpu[gu][k]])
                        u, ru = U[gu][k], rU[gu][k]
                        y, ry = Y[gu][k], rY[gu][k]
                        hidx = gu * NFF + j
                        fidx = gu * NFF + j
                        w0 = PP[:, cwo + fidx * 3 + 0:cwo + fidx * 3 + 1]
                        w1 = PP[:, cwo + fidx * 3 + 1:cwo + fidx * 3 + 2]
                        w2 = PP[:, cwo + fidx * 3 + 2:cwo + fidx * 3 + 3]
                        bb = PP[:, cbo + fidx:cbo + fidx + 1]
                        P.op(POOL, lambda e, u=u, hidx=hidx: e.tensor_copy(out=u[:, 0:2], in_=HALO[:, hidx, :]), [rH[hidx]], [ru])
                        if gu == 0:
                            P.op(ACT, lambda e, u=u, gu=gu, k=k: e.activation(out=u[:, 2:514], in_=pu[gu][k][:], func=AF.Copy), [], [ru, rpu[gu][k]])
                        else:
                            P.op(DVE, lambda e, u=u, gu=gu, k=k: e.tensor_copy(out=u[:, 2:514], in_=pu[gu][k][:]), [], [ru, rpu[gu][k]])
                        P.op(ACT, lambda e, y=y, gu=gu, k=k, w2=w2, bb=bb: e.activation(out=y[:], in_=pu[gu][k][:], func=AF.Identity, scale=w2, bias=bb), [], [ry, rpu[gu][k]])
                        P.op(POOL, lambda e, u=u, hidx=hidx: e.tensor_copy(out=HALO[:, hidx, :], in_=u[:, 512:514]), [ru], [rH[hidx]])
                        P.op(DVE, lambda e, y=y, u=u, w0=w0: e.scalar_tensor_tensor(out=y[:], in0=u[:, 0:512], scalar=w0, in1=y[:], op0=ALU.mult, op1=ALU.add), [ru, ry], [ry])
                        P.op(DVE, lambda e, y=y, u=u, w1=w1: e.scalar_tensor_tensor(out=y[:], in0=u[:, 1:513], scalar=w1, in1=y[:], op0=ALU.mult, op1=ALU.add), [ru, ry], [ry])
                    yg, yu = Y[0][k], Y[1][k]
                    P.op(ACT, lambda e, yg=yg: e.activation(out=yg[:], in_=yg[:], func=AF.Silu), [rY[0][k]], [rY[0][k]])
                    P.op(POOL, lambda e, yg=yg, yu=yu, j=j: e.tensor_tensor(out=AT[:, j, :], in0=yg[:], in1=yu[:], op=ALU.mult), [rY[0][k], rY[1][k]], [rAT[j]])
                for tt in range(4):
                    t = b * 4 + tt
                    JG = 6
                    for hf in range(2):
                        for j0 in range(0, NFF, JG):
                            j1 = min(NFF, j0 + JG)
                            def mmd(e, tt=tt, hf=hf, j0=j0, j1=j1):
                                ins = None
                                for j in range(j0, j1):
                                    ins = e.matmul(pacc[0][:, hf * 512:(hf + 1) * 512], lhsT=AT[:, j, tt * 128:(tt + 1) * 128], rhs=Wd[hf][:, j, :],
                                                   start=(j == 0), stop=(j == NFF - 1))
                                return ins
                            P.op(PE, mmd, rAT[j0:j1] + [rWd[hf]], [rpacc[0]])
                    self.postnorm_residual(t, pacc[0], rpacc[0], g_post, evac=(pcp[t % 2], rpcp[t % 2]))

    def outproj_phase(self, l, MT, rMT, st):
        P = self.P
        Wo = self.sb(st, [128, 8, 1024], BF16, "Wo"); rWo = Res()
        paccs = [self.ps(st, [128, 1024], F32, "pacc_o") for _ in range(2)]; rpaccs = [Res(), Res()]
        seg, rs = self.wseg(f"out{l}")
        P.dma(SP, Wo[:], seg.rearrange("p (k c) -> p k c", k=8), reads=[rs], writes=[rWo])
        g_post = self.load_gain(l, 1)
        for t in range(self.NT):
            pacc, rpacc = paccs[t % 2], rpaccs[t % 2]
            def mm(e, t=t, pacc=pacc):
                ins = None
                for hf in range(2):
                    for kk in range(8):
                        ins = e.matmul(pacc[:, hf * 512:(hf + 1) * 512], lhsT=MT[:, kk, t * 128:(t + 1) * 128], rhs=Wo[:, kk, hf * 512:(hf + 1) * 512],
                                       start=(kk == 0), stop=(kk == 7))
                return ins
            P.op(PE, mm, [rWo, rMT[t]], [rpacc])
            self.postnorm_residual(t, pacc, rpacc, g_post, junk=(self.hn[t % 2], self.r_hn[t % 2]))

    def mixer_phase(self, l):
        P = self.P
        S, NT = self.S, self.NT
        with contextlib.ExitStack() as st:
            HT = self.sb(st, [128, 8, S], BF16, "HT"); rHT = [Res() for _ in range(NT)]
            MT = self.sb(st, [128, 8, S], BF16, "MT"); rMT = [Res() for _ in range(NT)]
            g_pre = self.load_gain(l, 0)
            for t in range(NT):
                self.prenorm_tile(None, t, g_pre, HT[:, :, t * 128:(t + 1) * 128], rHT[t])
            self.dbg_dump(f"HT{l}", HT[:], [128, 8, S], rHT)
            self.cast_weights([f"out{l}", f"up{l}_", f"down{l}_"])
            if not self.mixers:
                for t in range(NT):
                    P.op(DVE, lambda e, t=t: e.tensor_copy(out=MT[:, :, t * 128:(t + 1) * 128], in_=HT[:, :, t * 128:(t + 1) * 128]), [rHT[t]], [rMT[t]])
            elif l == 0:
                only = getattr(self, "only", "both")
                self.gdn_nx = 14 if NT >= 14 else (7 if NT >= 7 else 0)
                self.gdn_xsub = [[] for _ in range(self.gdn_nx)]
                self.gdn_spill_ops = []
                for t in range(self.gdn_nx):
                    self.gdn_spill_ops.append(P.dma(SP, self.xspill[t], self.X[:, t, :], reads=[self.rX[t]]))
                if only != "both":
                    for t in range(NT):
                        P.op(POOL, lambda e, t=t: e.memset(MT[:, :, t * 128:(t + 1) * 128], 0.0), [], [rMT[t]])
                if only in ("both", "first"):
                    with contextlib.ExitStack() as st2:
                        self.fox(HT, rHT, MT, rMT, st2)
                    P.barrier(self.bar_scratch[:])
                if only in ("both", "second"):
                    with contextlib.ExitStack() as st2:
                        self.gdn(HT, rHT, MT, rMT, st2)
                    for t in range(self.gdn_nx):
                        P.dma(SP, self.X[:, t, :], self.xspill[t], reads=[], writes=[self.rX[t]] + self.gdn_xsub[t])
                    P.barrier(self.bar_scratch[:])
            else:
                only = getattr(self, "only", "both")
                if only != "both":
                    for t in range(NT):
                        P.op(POOL, lambda e, t=t: e.memset(MT[:, :, t * 128:(t + 1) * 128], 0.0), [], [rMT[t]])
                if only in ("both", "first"):
                    with contextlib.ExitStack() as st2:
                        self.hgrn(HT, rHT, MT, rMT, st2)
                    P.barrier(self.bar_scratch[:])
                if only in ("both", "second"):
                    with contextlib.ExitStack() as st2:
                        self.ssd(HT, rHT, MT, rMT, st2)
                    P.barrier(self.bar_scratch[:])
            self.dbg_dump(f"MT{l}", MT[:], [128, 8, S], rMT)
            with contextlib.ExitStack() as st2:
                self.outproj_phase(l, MT, rMT, st2)


    def build(self):
        P = self.P
        nc = self.nc
        S, NT = self.S, self.NT
        with contextlib.ExitStack() as st:
            self.X = self.sb(st, [128, NT, D], F32, "X"); self.rX = [Res() for _ in range(NT)]
            self.PP = self.sb(st, [128, self.ptot], F32, "PP"); self.rPP = Res()
            self.gainB = [self.sb(st, [128, D], F32, "gainB") for _ in range(2)]; self.r_gainB = [Res(), Res()]
            self.junk_bf = [self.sb(st, [128, D], BF16, "junk") for _ in range(1)]
            self.hn = [self.sb(st, [128, D], BF16, "hn") for _ in range(2)]; self.r_hn = [Res(), Res()]
            self.tmp32 = [self.sb(st, [128, D], F32, "tmp32") for _ in range(1)]; self.r_tmp32 = [Res()]
            self.ss = [self.sb(st, [128, 2], F32, "ss") for _ in range(2)]; self.r_ss = [Res(), Res()]
            self.ident_bf = self.sb(st, [128, 128], BF16, "ident"); r_id = Res()
            self.ident_f = self.sb(st, [128, 128], F32, "identf")
            self.ones_f = self.sb(st, [128, 128], F32, "onesf")
            self.ones_bf = self.sb(st, [128, 128], BF16, "onesbf")
            self.tri_incl = self.sb(st, [128, 128], F32, "tri_incl")
            self.tri_incl_bf = self.sb(st, [128, 128], BF16, "tri_incl_bf")
            self.neg_incl = self.sb(st, [128, 128], F32, "neg_incl")
            self.neg_strict = self.sb(st, [128, 128], F32, "neg_strict")
            self.eps_t = self.sb(st, [128, 1], F32, "eps")
            self.bar_scratch = self.sb(st, [128, 1], F32, "bar")
            self.pt = [self.ps(st, [128, 8, 128], BF16, "pt") for _ in range(2)]; self.r_pt = [Res(), Res()]
            rc = self.rconst = Res()
            P.op(POOL, lambda e: e.memset(self.ident_bf[:], 0.0), [], [rc], nobar=True)
            P.op(POOL, lambda e: e.affine_select(out=self.ident_bf[:], in_=self.ident_bf[:], pattern=[[-1, 128]], compare_op=ALU.not_equal, fill=1.0, base=0, channel_multiplier=1), [rc], [rc], nobar=True)
            P.op(POOL, lambda e: e.memset(self.ident_f[:], 0.0), [rc], [rc], nobar=True)
            P.op(POOL, lambda e: e.affine_select(out=self.ident_f[:], in_=self.ident_f[:], pattern=[[-1, 128]], compare_op=ALU.not_equal, fill=1.0, base=0, channel_multiplier=1), [rc], [rc], nobar=True)
            P.op(POOL, lambda e: e.memset(self.ones_f[:], 1.0), [rc], [rc], nobar=True)
            P.op(POOL, lambda e: e.memset(self.ones_bf[:], 1.0), [rc], [rc], nobar=True)
            P.op(POOL, lambda e: e.memset(self.tri_incl[:], 1.0), [rc], [rc], nobar=True)
            P.op(POOL, lambda e: e.affine_select(out=self.tri_incl[:], in_=self.tri_incl[:], pattern=[[1, 128]], compare_op=ALU.is_ge, fill=0.0, base=0, channel_multiplier=-1), [rc], [rc], nobar=True)
            P.op(POOL, lambda e: e.tensor_copy(out=self.tri_incl_bf[:], in_=self.tri_incl[:]), [rc], [rc], nobar=True)
            P.op(POOL, lambda e: e.memset(self.neg_incl[:], 0.0), [rc], [rc], nobar=True)
            P.op(POOL, lambda e: e.affine_select(out=self.neg_incl[:], in_=self.neg_incl[:], pattern=[[1, 128]], compare_op=ALU.is_ge, fill=-30000.0, base=0, channel_multiplier=-1), [rc], [rc], nobar=True)
            P.op(POOL, lambda e: e.memset(self.neg_strict[:], 0.0), [rc], [rc], nobar=True)
            P.op(POOL, lambda e: e.affine_select(out=self.neg_strict[:], in_=self.neg_strict[:], pattern=[[1, 128]], compare_op=ALU.is_gt, fill=-30000.0, base=0, channel_multiplier=-1), [rc], [rc], nobar=True)
            P.op(POOL, lambda e: e.memset(self.eps_t[:], EPS), [rc], [rc], nobar=True)
            P.dma(SP, self.PP[:], self.pp_d, writes=[self.rPP])
            P.barrier(self.bar_scratch[:])
            self.cast_weights([f"in{self.layers[0]}_"])
            for s in range(self.NSEQ):
                for t in range(NT):
                    P.dma(SP, self.X[:, t, :], self.x_d[s, t * 128:(t + 1) * 128, :], writes=[self.rX[t]])
                for l in self.layers:
                    self.mixer_phase(l)
                    P.barrier(self.bar_scratch[:])
                    self.ffn_phase(l)
                    P.barrier(self.bar_scratch[:])
                for t in range(NT):
                    self.out_dmas.append(P.dma(SP, self.y_d[s, t * 128:(t + 1) * 128, :], self.X[:, t, :], reads=[self.rX[t]]))
            P.emit(final_wait_ops=self.out_dmas)
        return nc


def make_builder(S, NSEQ, layers=(0, 1), mixers=True, dbg=None, poff=None, ptot=None):
    woff = {}
    n = 0
    for name, sz in weight_layout():
        woff[name] = (n, sz)
        n += sz
    return Builder(S, NSEQ, woff, poff, n, ptot, layers=layers, mixers=mixers, dbg=dbg)


def kernel(**inputs):
    inp = {k: np.asarray(v) for k, v in inputs.items()}
    W, Pp, G = host_pack(inp)
    wbig = W.cat()
    pp = Pp.cat()
    B, S, _ = inp["x"].shape
    nseq = B // NCORES
    b = Builder(S, nseq, W.off, Pp.off, W.n, Pp.n)
    nc = b.build()
    x = np.ascontiguousarray(inp["x"], dtype=np.float32)
    in_maps = [{"x": x[c * nseq:(c + 1) * nseq], "wbig": wbig, "pp": pp, "gains": G} for c in range(NCORES)]
    res = run_bass_kernel_spmd(nc, in_maps, core_ids=list(range(NCORES)))
    out = np.concatenate([r["y"] for r in res.results], axis=0)
    return out.astype(np.float32)


def _fox(self, HT, rHT, MT, rMT, st):
    P = self.P
    S, NT, NB = self.S, self.NT, self.NB
    sb, ps = self.sb, self.ps
    Vaug = sb(st, [128, NT, 2, 128], BF16, "Vaug"); rV = [Res() for _ in range(NT)]
    qz = [sb(st, [128, S], BF16, "qz") for _ in range(2)]; rqz = [[Res() for _ in range(NB)] for _ in range(2)]
    kT = sb(st, [128, S], BF16, "kT"); rk = [Res() for _ in range(NB)]
    Wq = sb(st, [128, 8, 128], BF16, "Wq"); rWq = Res()
    Wk = sb(st, [128, 8, 128], BF16, "Wk"); rWk = Res()
    Wv = sb(st, [128, 8, 128], BF16, "Wv"); rWv = Res()
    Wf = sb(st, [128, 8, 8], BF16, "Wf"); rWf = Res()
    ebuf = sb(st, [8, 512], F32, "ebuf"); reb = Res()
    CTb = [sb(st, [8, 512], F32, "CTb") for _ in range(2)]; rCT = [Res(), Res()]
    CrefB = sb(st, [8, NB, 128], F32, "CrefB"); rCr = Res()
    negb = sb(st, [8, 1], F32, "negb"); rnegb = Res()
    Ctok = sb(st, [128, NT, 8], F32, "Ctok"); rCtok = Res()
    nb = sb(st, [128, NB, NT, 8], F32, "nb"); rnb = Res()
    PT = [sb(st, [128, 512], BF16, "PT") for _ in range(3)]; rPT = [Res() for _ in range(3)]
    rec = sb(st, [64, 512], F32, "rec"); rrec = Res()
    pS = [ps(st, [128, 512], F32, "pS") for _ in range(2)]; rpS = [Res(), Res()]
    po = [ps(st, [128, 512], F32, "po") for _ in range(2)]; rpo = [Res(), Res()]
    pm = ps(st, [128, 512], F32, "pm"); rpm = Res()
    pC = ps(st, [128, NT * 8 + NB * 8], F32, "pC"); rpC = Res()
    PP = self.PP
    fo, _ = self.poff["foxb8"]

    seg, rs = self.wseg("in0_ff")
    P.dma(SP, Wf[:], seg.rearrange("p (k c) -> p k c", k=8), reads=[rs], writes=[rWf])
    P.op(DVE, lambda e: e.tensor_scalar(out=negb[:], in0=PP[0:8, fo:fo + 1], scalar1=-1.0, scalar2=None, op0=ALU.mult), [self.rPP], [rnegb])
    P.op(POOL, lambda e: e.memset(Vaug[:, :, :, 64:128], 1.0), [], rV)
    for hh_ in range(2):
        P.op(POOL, lambda e, hh_=hh_: e.memset(qz[hh_][:], 0.0), [], rqz[hh_])
    for b in range(NB):
        def mmf(e, b=b):
            ins = None
            for kk in range(8):
                ins = e.matmul(pm[0:8, :], lhsT=Wf[:, kk, :], rhs=HT[:, kk, b * 512:(b + 1) * 512], start=(kk == 0), stop=(kk == 7))
            return ins
        P.op(PE, mmf, [rWf] + rHT[b * 4:(b + 1) * 4], [rpm])
        P.op(ACT, lambda e: e.activation(out=ebuf[:], in_=pm[0:8, :], func=AF.Exp, bias=negb[:, 0:1], scale=-1.0), [rpm, rnegb], [reb])
        P.op(ACT, lambda e: e.activation(out=ebuf[:], in_=ebuf[:], func=AF.Ln, bias=1.0), [reb], [reb])
        cur, prev = CTb[b % 2], CTb[(b + 1) % 2]
        rcur, rprev = rCT[b % 2], rCT[(b + 1) % 2]
        P.op(POOL, lambda e, cur=cur: e.memset(cur[:], 1.0), [], [rcur])
        if b == 0:
            P.op(DVE, lambda e, cur=cur: e.tensor_tensor_scan(out=cur[:], data0=cur[:], data1=ebuf[:], initial=0.0, op0=ALU.mult, op1=ALU.add), [reb, rcur], [rcur])
        else:
            P.op(DVE, lambda e, cur=cur, prev=prev: e.tensor_tensor_scan(out=cur[:], data0=cur[:], data1=ebuf[:], initial=prev[:, 511:512], op0=ALU.mult, op1=ALU.add), [reb, rcur, rprev], [rcur])
        P.op(DVE, lambda e, cur=cur, b=b: e.tensor_copy(out=CrefB[:, b, :], in_=cur[:, 256:257].to_broadcast([8, 128])), [rcur], [rCr])
        def mmc(e, cur=cur, b=b):
            ins = None
            for tt in range(4):
                t = b * 4 + tt
                ins = e.matmul(pC[:, t * 8:(t + 1) * 8], lhsT=cur[0:8, tt * 128:(tt + 1) * 128], rhs=self.ident_f[0:8, 0:8], start=True, stop=True)
            ins = e.matmul(pC[:, NT * 8 + b * 8:NT * 8 + (b + 1) * 8], lhsT=CrefB[0:8, b, :], rhs=self.ident_f[0:8, 0:8], start=True, stop=True)
            return ins
        P.op(PE, mmc, [rcur, rCr, self.rconst], [rpC])
    P.op(ACT, lambda e: e.activation(out=Ctok[:], in_=pC[:, 0:NT * 8].rearrange("p (t h) -> p t h", h=8), func=AF.Copy), [rpC], [rCtok])
    for i in range(NB):
        P.op(DVE, lambda e, i=i: e.tensor_tensor(out=nb[:, i, :, :], in0=Ctok[:], in1=pC[:, NT * 8 + i * 8:NT * 8 + (i + 1) * 8].unsqueeze(1).to_broadcast([128, NT, 8]), op=ALU.subtract), [rCtok, rpC], [rnb])
    self.dbg_dump("foxC", Ctok[:], [128, NT, 8], [rCtok])

    for pr in range(4):
        seg, rs = self.wseg(f"in0_fq{pr}")
        P.dma(SP, Wq[:], seg.rearrange("p (k c) -> p k c", k=8), reads=[rs], writes=[rWq])
        seg, rs = self.wseg(f"in0_fk{pr}")
        P.dma(SP, Wk[:], seg.rearrange("p (k c) -> p k c", k=8), reads=[rs], writes=[rWk])
        seg, rs = self.wseg("in0_fv")
        P.dma(SP, Wv[:], seg.rearrange("p (k c) -> p k c", k=8)[:, :, pr * 128:(pr + 1) * 128], reads=[rs], writes=[rWv])
        for b in range(NB):
            for which in ("q", "k"):
                W_, rW_ = (Wq, rWq) if which == "q" else (Wk, rWk)
                pb, rpb = pS[b % 2], rpS[b % 2]
                def mmp(e, W_=W_, pb=pb, b=b):
                    ins = None
                    for kk in range(8):
                        ins = e.matmul(pb[:], lhsT=W_[:, kk, :], rhs=HT[:, kk, b * 512:(b + 1) * 512], start=(kk == 0), stop=(kk == 7))
                    return ins
                P.op(PE, mmp, [rW_] + rHT[b * 4:(b + 1) * 4], [rpb])
                if which == "q":
                    P.op(ACT, lambda e, pb=pb, b=b: e.activation(out=qz[0][0:64, b * 512:(b + 1) * 512], in_=pb[0:64, :], func=AF.Copy), [], [rqz[0][b], rpb])
                    P.op(ACT, lambda e, pb=pb, b=b: e.activation(out=qz[1][64:128, b * 512:(b + 1) * 512], in_=pb[64:128, :], func=AF.Copy), [], [rqz[1][b], rpb])
                else:
                    P.op(DVE, lambda e, pb=pb, b=b: e.tensor_copy(out=kT[:, b * 512:(b + 1) * 512], in_=pb[:]), [], [rk[b], rpb])
        for t in range(NT):
            def mmv(e, t=t):
                ins = None
                for kk in range(8):
                    ins = e.matmul(pm[:, 0:128], lhsT=HT[:, kk, t * 128:(t + 1) * 128], rhs=Wv[:, kk, :], start=(kk == 0), stop=(kk == 7))
                return ins
            P.op(PE, mmv, [rWv, rHT[t]], [rpm])
            P.op(DVE, lambda e, t=t: e.tensor_copy(out=Vaug[:, t, :, 0:64], in_=pm[:, 0:128].rearrange("p (a b) -> p a b", a=2)), [rpm], [rV[t]])
        for hh in range(2):
            h = 2 * pr + hh
            base = hh * 64
            cnt = 0
            for i in range(NB):
                njb = 4 * i + 4
                pacc, rpacc = po[i % 2], rpo[i % 2]
                items = []
                for j in range(njb):
                    r = j - 4 * i
                    c0 = 128 * r if r > 0 else 0
                    items.append((j, r, c0))

                def emit_qk(j, r, c0, slot, i=i, base=base, h=h, hh=hh):
                    pb, rpb = pS[slot % 2], rpS[slot % 2]
                    pt_, rpt_ = PT[slot % 3], rPT[slot % 3]
                    P.op(PE, lambda e: e.matmul(pb[:, c0:512], lhsT=kT[:, j * 128:(j + 1) * 128],
                                                rhs=qz[hh][:, i * 512 + c0:(i + 1) * 512], start=True, stop=True),
                         [rk[j // 4], rqz[hh][i]], [rpb])
                    P.op(ACT, lambda e: e.activation(out=pt_[:, c0:512], in_=pb[:, c0:512], func=AF.Exp, bias=nb[:, i, j, h:h + 1], scale=0.125),
                         [rpb, rnb], [rpt_])
                    if r >= 0:
                        P.op(POOL, lambda e: e.tensor_tensor(out=pt_[:, c0:c0 + 128], in0=pt_[:, c0:c0 + 128], in1=self.tri_incl_bf[:], op=ALU.mult),
                             [rpt_, self.rconst], [rpt_])

                def emit_pv(j, r, c0, slot, first, last, pacc=pacc, rpacc=rpacc, hh=hh):
                    pt_, rpt_ = PT[slot % 3], rPT[slot % 3]
                    P.op(PE, lambda e: e.matmul(pacc[:, c0:512], lhsT=Vaug[:, j, hh, :], rhs=pt_[:, c0:512], start=first, stop=last),
                         [rV[j], rpt_], [rpacc])

                for idx, (j, r, c0) in enumerate(items):
                    emit_qk(j, r, c0, cnt + idx)
                    if idx >= 1:
                        pj, pr_, pc0 = items[idx - 1]
                        emit_pv(pj, pr_, pc0, cnt + idx - 1, idx - 1 == 0, False)
                pj, pr_, pc0 = items[-1]
                emit_pv(pj, pr_, pc0, cnt + len(items) - 1, len(items) == 1, True)
                cnt += len(items)
                P.op(DVE, lambda e, pacc=pacc: e.reciprocal(out=rec[:], in_=pacc[64:128, :]), [rpacc], [rrec])
                P.op(DVE, lambda e, pacc=pacc, i=i, base=base, pr=pr: e.tensor_tensor(out=MT[base:base + 64, pr, i * 512:(i + 1) * 512], in0=pacc[0:64, :], in1=rec[:], op=ALU.mult),
                     [rpacc, rrec], rMT[i * 4:(i + 1) * 4])


Builder.fox = _fox


def _proj_fm(self, W_, rW_, HT, rHT, b, pb, rpb):
    def mm(e):
        ins = None
        for kk in range(8):
            ins = e.matmul(pb[:], lhsT=W_[:, kk, :], rhs=HT[:, kk, b * 512:(b + 1) * 512], start=(kk == 0), stop=(kk == 7))
        return ins
    self.P.op(PE, mm, [rW_] + rHT[b * 4:(b + 1) * 4], [rpb])


def _proj_tm(self, W_ap, rW_, HT, rHT, t, pb_ap, rpb):
    def mm(e):
        ins = None
        for kk in range(8):
            ins = e.matmul(pb_ap, lhsT=HT[:, kk, t * 128:(t + 1) * 128], rhs=W_ap[:, kk, :], start=(kk == 0), stop=(kk == 7))
        return ins
    self.P.op(PE, mm, [rW_, rHT[t]], [rpb])


def _load_w(self, name, W_, rW_, cols=None, ncol=None):
    seg, rs = self.wseg(name)
    n = self.woff[name][1] // 8
    src = seg.rearrange("p (k c) -> p k c", k=8)
    if cols is not None:
        src = src[:, :, cols:cols + ncol]
    self.P.dma(SP, W_[:], src, reads=[rs], writes=[rW_])


Builder.proj_fm = _proj_fm
Builder.proj_tm = _proj_tm
Builder.load_w = _load_w


def _hgrn(self, HT, rHT, MT, rMT, st):
    P = self.P
    S, NT, NB = self.S, self.NT, self.NB
    sb, ps = self.sb, self.ps
    PP = self.PP
    Wq = sb(st, [128, 8, 128], BF16, "Wq"); rWq = Res()
    Wf = sb(st, [128, 8, 128], BF16, "Wf"); rWf = Res()
    Wg = sb(st, [128, 8, 128], BF16, "Wg"); rWg = Res()
    Wi = sb(st, [128, 8, 512], BF16, "Wi"); rWi = Res()
    vtok = sb(st, [128, NT, 512], BF16, "vtok"); rvt = [Res() for _ in range(NT)]
    qtT = sb(st, [128, S], BF16, "qtT"); rqt = [Res() for _ in range(NB)]
    ktT = sb(st, [128, S], BF16, "ktT"); rkt = [Res() for _ in range(NB)]
    qhT = sb(st, [128, S], BF16, "qhT"); rqh = [Res() for _ in range(NB)]
    class _V:
        def __init__(self, ap):
            self.ap = ap
        def __getitem__(self, k):
            return self.ap if (isinstance(k, slice) and k == slice(None)) else self.ap[k]
    A1 = _V(self.tmp32[0][:, 0:512]); rA1 = self.r_tmp32[0]
    A2 = _V(self.tmp32[0][:, 512:1024]); rA2 = self.r_tmp32[0]
    A3 = _V(self.gainB[0][:, 0:512]); rA3 = self.r_gainB[0]
    A4 = _V(self.gainB[0][:, 512:1024]); rA4 = self.r_gainB[0]
    E1 = _V(self.gainB[1][:, 0:512]); rE1 = self.r_gainB[1]
    E2 = _V(self.gainB[1][:, 512:1024]); rE2 = self.r_gainB[1]
    QS = sb(st, [128, 512], F32, "QS"); rQS = Res()
    RESET = sb(st, [128, 512], F32, "RESET"); rRS = Res()
    SC = sb(st, [128, NT, 3], F32, "SC"); rSC = Res()
    lbt = sb(st, [128, 4, 4], F32, "lbt"); rlb = Res()
    ktok = [sb(st, [128, 128], BF16, "ktok") for _ in range(2)]; rktok = [Res(), Res()]
    ATm = [sb(st, [128, 128], BF16, "ATm") for _ in range(2)]; rATm = [Res(), Res()]
    Sst = sb(st, [128, 128], F32, "Sst"); rS = Res()
    Sbf = sb(st, [128, 128], BF16, "Sbf"); rSbf = Res()
    yb = [sb(st, [128, 128], F32, "yb") for _ in range(2)]; ryb = [Res(), Res()]
    sg = [sb(st, [128, 128], F32, "sg") for _ in range(2)]; rsg = [Res(), Res()]
    ym = [sb(st, [128, 128], BF16, "ym") for _ in range(2)]; rym = [Res(), Res()]
    ssn = [sb(st, [128, 2], F32, "ssn") for _ in range(2)]; rssn = [Res(), Res()]
    pA = [ps(st, [128, 512], F32, "pA") for _ in range(2)]; rpA = [Res(), Res()]
    pv = ps(st, [128, 512], F32, "pv"); rpv = Res()
    pG = ps(st, [128, 512], F32, "pG"); rpG = Res()
    pKl = [pA[0][:, 0:128], pA[1][:, 0:128], pv[:, 0:128], pG[:, 0:128]]
    rpK = [rpA[0], rpA[1], rpv, rpG]
    l0o, _ = self.poff["lb0"]; l1o, _ = self.poff["lb1"]
    hgo, _ = self.poff["hng"]

    P.op(DVE, lambda e: e.tensor_tensor(out=lbt[:, :, 0], in0=PP[:, l1o:l1o + 4], in1=PP[:, l0o:l0o + 4], op=ALU.subtract), [self.rPP], [rlb])
    P.op(ACT, lambda e: e.activation(out=lbt[:, :, 0], in_=lbt[:, :, 0], func=AF.Exp, scale=-1.0), [rlb], [rlb])
    P.op(DVE, lambda e: e.tensor_scalar(out=lbt[:, :, 0], in0=lbt[:, :, 0], scalar1=1.0, scalar2=None, op0=ALU.add), [rlb], [rlb])
    P.op(DVE, lambda e: e.reciprocal(out=lbt[:, :, 0], in_=lbt[:, :, 0]), [rlb], [rlb])
    P.op(DVE, lambda e: e.tensor_scalar(out=lbt[:, :, 1], in0=lbt[:, :, 0], scalar1=-1.0, scalar2=1.0, op0=ALU.mult, op1=ALU.add), [rlb], [rlb])
    P.op(DVE, lambda e: e.tensor_scalar(out=lbt[:, :, 3], in0=lbt[:, :, 1], scalar1=0.5, scalar2=None, op0=ALU.mult), [rlb], [rlb])
    P.op(DVE, lambda e: e.tensor_tensor(out=lbt[:, :, 2], in0=lbt[:, :, 0], in1=lbt[:, :, 3], op=ALU.add), [rlb], [rlb])
    P.op(POOL, lambda e: e.memset(RESET[:], 1.0), [], [rRS])
    P.op(POOL, lambda e: e.memset(RESET[:].rearrange("p (c t) -> p c t", c=4)[:, :, 0:1], 0.0), [rRS], [rRS])
    self.load_w("in1_hi", Wi, rWi)
    for t in range(NT):
        self.proj_tm(Wi, rWi, HT, rHT, t, pv[:], rpv)
        P.op(ACT, lambda e, t=t: e.activation(out=vtok[:, t, :], in_=pv[:], func=AF.Copy), [rpv], [rvt[t]])

    def head_block(h, b):
        lb = lbt[:, h, 0:1]; oml = lbt[:, h, 1:2]
        self.proj_fm(Wf, rWf, HT, rHT, b, pA[0], rpA[0])
        self.proj_fm(Wq, rWq, HT, rHT, b, pA[1], rpA[1])
        P.op(ACT, lambda e: e.activation(out=A1[:], in_=pA[0][:], func=AF.Tanh, scale=0.5), [], [rA1, rpA[0]])
        P.op(ACT, lambda e: e.activation(out=QS[:], in_=pA[1][:], func=AF.Silu), [], [rQS, rpA[1]])
        P.op(DVE, lambda e: e.tensor_scalar(out=A1[:], in0=A1[:], scalar1=lbt[:, h, 3:4], scalar2=lbt[:, h, 2:3], op0=ALU.mult, op1=ALU.add), [rA1, rlb], [rA1])
        P.op(POOL, lambda e: e.tensor_scalar(out=A2[:], in0=A1[:], scalar1=-1.0, scalar2=1.0, op0=ALU.mult, op1=ALU.add), [rA1], [rA2])
        P.op(ACT, lambda e: e.activation(out=A1[:], in_=A1[:], func=AF.Ln), [rA1, rA2], [rA1])
        P.op(DVE, lambda e: e.tensor_tensor_scan(out=A3[:], data0=RESET[:], data1=A1[:], initial=0.0, op0=ALU.mult, op1=ALU.add), [rA1, rRS], [rA3])
        A3v = A3[:].rearrange("p (c t) -> p c t", c=4)
        A4v = A4[:].rearrange("p (c t) -> p c t", c=4)
        P.op(DVE, lambda e: e.tensor_tensor(out=A4v, in0=A3v, in1=A3v[:, :, 63:64].to_broadcast([128, 4, 128]), op=ALU.subtract), [rA3], [rA4])
        P.op(ACT, lambda e: e.activation(out=E1[:], in_=A4[:], func=AF.Exp), [rA4], [rE1])
        P.op(ACT, lambda e: e.activation(out=E2[:], in_=A4[:], func=AF.Exp, scale=-1.0), [rA4], [rE2])
        P.op(ACT, lambda e: e.activation(out=SC[:, b * 4:(b + 1) * 4, 0], in_=A3v[:, :, 127], func=AF.Exp), [rA3], [rSC])
        P.op(ACT, lambda e: e.activation(out=SC[:, b * 4:(b + 1) * 4, 1], in_=A4v[:, :, 127], func=AF.Exp), [rA4], [rSC])
        P.op(ACT, lambda e: e.activation(out=SC[:, b * 4:(b + 1) * 4, 2], in_=A3v[:, :, 63], func=AF.Exp), [rA3], [rSC])
        P.op(DVE, lambda e: e.tensor_tensor(out=qtT[:, b * 512:(b + 1) * 512], in0=QS[:], in1=E1[:], op=ALU.mult), [rQS, rE1], [rqt[b]])
        P.op(DVE, lambda e: e.tensor_tensor(out=ktT[:, b * 512:(b + 1) * 512], in0=A2[:], in1=E2[:], op=ALU.mult), [rA2, rE2], [rkt[b]])
        P.op(POOL, lambda e: e.tensor_tensor(out=qhT[:, b * 512:(b + 1) * 512].rearrange("p (c t) -> p c t", c=4),
                                             in0=qtT[:, b * 512:(b + 1) * 512].rearrange("p (c t) -> p c t", c=4),
                                             in1=SC[:, b * 4:(b + 1) * 4, 2:3].to_broadcast([128, 4, 128]), op=ALU.mult), [rqt[b], rSC], [rqh[b]])

    def head_chunk(h, n):
        k = n % 2
        b = n // 4
        cs = slice(n * 128, (n + 1) * 128)
        pt, rpt = self.pt[k], self.r_pt[k]
        P.op(PE, lambda e: e.transpose(pt[:, 0, :], ktT[:, cs], self.ident_bf[:]), [rkt[b]], [rpt])
        P.op(DVE, lambda e: e.tensor_copy(out=ktok[k][:], in_=pt[:, 0, :]), [rpt], [rktok[k]])
        P.op(PE, lambda e: e.matmul(pKl[0], lhsT=ktT[:, cs], rhs=qtT[:, cs], start=True, stop=True), [rkt[b], rqt[b]], [rpK[0]])
        P.op(DVE, lambda e: e.tensor_tensor(out=ATm[k][:], in0=pKl[0], in1=self.tri_incl[:], op=ALU.mult), [self.rconst], [rATm[k], rpK[0]])
        P.op(PE, lambda e: e.matmul(pKl[1], lhsT=ktok[k][:], rhs=vtok[:, n, h * 128:(h + 1) * 128], start=True, stop=True), [rktok[k], rvt[n]], [rpK[1]])
        def mmo(e):
            ins = None
            if n > 0:
                ins = e.matmul(pKl[2], lhsT=qhT[:, cs], rhs=Sbf[:], start=True, stop=False)
            ins = e.matmul(pKl[2], lhsT=ATm[k][:], rhs=vtok[:, n, h * 128:(h + 1) * 128], start=(n == 0), stop=True)
            return ins
        P.op(PE, mmo, [rqh[b], rSbf, rATm[k], rvt[n]], [rpK[2]])
        if n == 0:
            P.op(DVE, lambda e: e.tensor_scalar(out=Sst[:], in0=pKl[1], scalar1=SC[:, n, 1:2], scalar2=None, op0=ALU.mult), [rSC], [rS, rpK[1]])
        else:
            P.op(DVE, lambda e: e.tensor_scalar(out=Sst[:], in0=Sst[:], scalar1=SC[:, n, 0:1], scalar2=None, op0=ALU.mult), [rS, rSC], [rS])
            P.op(DVE, lambda e: e.scalar_tensor_tensor(out=Sst[:], in0=pKl[1], scalar=SC[:, n, 1:2], in1=Sst[:], op0=ALU.mult, op1=ALU.add), [rSC, rS], [rS, rpK[1]])
        if n < NT - 1:
            P.op(ACT, lambda e: e.activation(out=Sbf[:], in_=Sst[:], func=AF.Copy), [rS], [rSbf])
        self.proj_tm(Wg, rWg, HT, rHT, n, pKl[3], rpK[3])
        P.op(ACT, lambda e: e.activation(out=sg[k][:], in_=pKl[3], func=AF.Exp, scale=-1.0), [], [rsg[k], rpK[3]])
        P.op(ACT, lambda e: e.activation(out=sg[k][:], in_=sg[k][:], func=AF.Ln, bias=1.0), [rsg[k]], [rsg[k]])
        P.op(ACT, lambda e: e.activation(out=sg[k][:], in_=sg[k][:], func=AF.Exp, scale=-1.0), [rsg[k]], [rsg[k]])
        P.op(DVE, lambda e: e.tensor_tensor(out=sg[k][:], in0=pKl[3], in1=sg[k][:], op=ALU.mult), [rsg[k]], [rsg[k], rpK[3]])
        P.op(ACT, lambda e: e.activation(out=yb[k][:], in_=pKl[2], func=AF.Square, accum_out=ssn[k][:, 0:1]), [], [ryb[k], rssn[k], rpK[2]])
        self.rstd_from_ss(ssn[k][:, 0:1], ssn[k][:, 1:2], rssn[k], rssn[k], 128)
        P.op(DVE, lambda e: e.scalar_tensor_tensor(out=yb[k][:], in0=pKl[2], scalar=ssn[k][:, 1:2], in1=PP[:, hgo:hgo + 128], op0=ALU.mult, op1=ALU.mult), [rssn[k], ryb[k]], [ryb[k], rpK[2]])
        P.op(POOL, lambda e: e.tensor_tensor(out=ym[k][:], in0=yb[k][:], in1=sg[k][:], op=ALU.mult), [ryb[k], rsg[k]], [rym[k]])
        P.op(PE, lambda e: e.transpose(pt[:, 1, :], ym[k][:], self.ident_bf[:]), [rym[k]], [rpt])
        P.op(ACT, lambda e: e.activation(out=MT[:, h, cs], in_=pt[:, 1, :], func=AF.Copy), [rpt], [rMT[n]])

    for h in range(4):
        self.load_w(f"in1_hq{h}", Wq, rWq)
        self.load_w(f"in1_hf{h}", Wf, rWf)
        self.load_w("in1_hg", Wg, rWg, cols=h * 128, ncol=128)
        for b in range(NB):
            head_block(h, b)
        for n in range(NT):
            head_chunk(h, n)


Builder.hgrn = _hgrn


def _ssd(self, HT, rHT, MT, rMT, st):
    P = self.P
    S, NT, NB = self.S, self.NT, self.NB
    sb, ps = self.sb, self.ps
    PP = self.PP
    XT = sb(st, [128, 8, 512], BF16, "XT"); rXT = [Res() for _ in range(8)]
    U = [sb(st, [128, 515], F32, "U") for _ in range(2)]; rU = [Res(), Res()]
    Y = [sb(st, [128, 512], F32, "Y") for _ in range(2)]; rY = [Res(), Res()]
    HALO = sb(st, [128, 8, 3], F32, "HALO"); rH = [Res() for _ in range(8)]
    Wx = [sb(st, [128, 8, 128], BF16, "Wx") for _ in range(2)]; rWx = [Res(), Res()]
    Wz = sb(st, [128, 8, 512], BF16, "Wz"); rWz = Res()
    Wdt = sb(st, [128, 8, 8], BF16, "Wdt"); rWdt = Res()
    sgt = sb(st, [128, 8], F32, "sgt"); rsgt = Res()
    eA = sb(st, [128, 8], F32, "eA"); reA = Res()
    dt = sb(st, [128, 8], F32, "dt"); rdt = Res()
    av = sb(st, [128, 8], F32, "av"); rav = Res()
    sc = sb(st, [128, 3, 8], F32, "sc"); rsc = Res()
    tmp8 = sb(st, [128, 8], F32, "tmp8"); rtmp8 = Res()
    class _V3:
        def __init__(self, ap):
            self.ap = ap
        def __getitem__(self, k):
            return self.ap if (isinstance(k, slice) and k == slice(None)) else self.ap[k]
    TA8 = _V3(self.gainB[0][:].rearrange("p (h c) -> p h c", h=8)); rTA = self.r_gainB[0]
    gT = sb(st, [128, 8, 128], BF16, "gT"); rgT = Res()
    ATt = sb(st, [128, 8, 128], BF16, "ATt"); rAT = Res()
    Btok = sb(st, [128, 2, 128], BF16, "Btok"); rBt = Res()
    Xd = sb(st, [128, 8, 64], BF16, "Xd"); rXd = Res()
    Xdec = sb(st, [128, 8, 64], BF16, "Xdec"); rXdec = Res()
    xsD = sb(st, [128, 8, 64], F32, "xsD"); rxsD = Res()
    ytmp = sb(st, [128, 8, 64], F32, "ytmp"); rytmp = Res()
    yy = sb(st, [128, 512], F32, "yy"); ryy = Res()
    sgz = _V3(self.tmp32[0][:, 0:512]); rsgz = self.r_tmp32[0]
    sq = _V3(self.tmp32[0][:, 512:1024]); rsq = self.r_tmp32[0]
    ybf = sb(st, [128, 512], BF16, "ybf"); rybf = Res()
    ssg = sb(st, [128, 2, 2], F32, "ssg"); rssg = Res()
    Sst = sb(st, [128, 8, 64], F32, "Sst"); rS = Res()
    Sbf = sb(st, [128, 512], BF16, "Sbf"); rSbf = Res()
    sgt_c = sb(st, [128, 128], F32, "sgt_c"); rsgt_c = Res()
    pLG = ps(st, [128, 4, 128], F32, "pLG"); rpLG = Res()
    pCB = ps(st, [128, 512], F32, "pCB"); rpCB = Res()
    p2 = ps(st, [128, 512], F32, "p2"); rp2 = Res()
    p3 = ps(st, [128, 512], F32, "p3"); rp3 = Res()
    p4 = ps(st, [128, 512], F32, "p4"); rp4 = Res()
    pz = ps(st, [128, 512], F32, "pz"); rpz = Res()
    cwo, _ = self.poff["mcw"]; cbo, _ = self.poff["mcb"]
    alo, _ = self.poff["mAlog"]; dbo, _ = self.poff["mdtb"]; mDo, _ = self.poff["mD"]; ngo, _ = self.poff["mng"]

    P.op(POOL, lambda e: e.memset(sgt_c[:], 1.0), [], [rsgt_c])
    P.op(POOL, lambda e: e.affine_select(out=sgt_c[:], in_=sgt_c[:], pattern=[[-1, 128]], compare_op=ALU.is_gt, fill=0.0, base=0, channel_multiplier=1), [rsgt_c], [rsgt_c])
    P.op(POOL, lambda e: e.memset(HALO[:], 0.0), [], rH)
    P.op(ACT, lambda e: e.activation(out=eA[:], in_=PP[:, alo:alo + 8], func=AF.Exp), [self.rPP], [reA])
    self.load_w("in1_mz", Wz, rWz)
    self.load_w("in1_mdt", Wdt, rWdt)

    def conv_chunk(b, c):
        k = c % 2
        self.load_w(f"in1_mx{c}", Wx[k], rWx[k])
        pb, rpb = (p2, rp2) if k == 0 else (p3, rp3)
        self.proj_fm(Wx[k], rWx[k], HT, rHT, b, pb, rpb)
        u, ru, y, ry = U[k], rU[k], Y[k], rY[k]
        P.op(POOL, lambda e: e.tensor_copy(out=u[:, 0:3], in_=HALO[:, c, :]), [rH[c]], [ru])
        P.op(ACT, lambda e: e.activation(out=u[:, 3:515], in_=pb[:], func=AF.Copy), [], [ru, rpb])
        P.op(POOL, lambda e: e.tensor_copy(out=HALO[:, c, :], in_=u[:, 512:515]), [ru], [rH[c]])
        w = [PP[:, cwo + c * 4 + i:cwo + c * 4 + i + 1] for i in range(4)]
        bb = PP[:, cbo + c:cbo + c + 1]
        P.op(DVE, lambda e: e.tensor_scalar(out=y[:], in0=u[:, 0:512], scalar1=w[0], scalar2=bb, op0=ALU.mult, op1=ALU.add), [ru], [ry])
        for i in range(1, 4):
            P.op(DVE, lambda e, i=i: e.scalar_tensor_tensor(out=y[:], in0=u[:, i:i + 512], scalar=w[i], in1=y[:], op0=ALU.mult, op1=ALU.add), [ru, ry], [ry])
        P.op(ACT, lambda e: e.activation(out=XT[:, c, :], in_=y[:], func=AF.Silu), [ry], [rXT[c]])

    def chunk(n):
        tt = n % 4
        cs = slice(tt * 128, (tt + 1) * 128)
        gs = slice(n * 128, (n + 1) * 128)
        pt0, rpt0 = self.pt[0], self.r_pt[0]
        pt1, rpt1 = self.pt[1], self.r_pt[1]
        self.proj_tm(Wdt, rWdt, HT, rHT, n, pCB[:, 256:264], rpCB)
        P.op(DVE, lambda e: e.tensor_tensor(out=dt[:], in0=pCB[:, 256:264], in1=PP[:, dbo:dbo + 8], op=ALU.add), [self.rPP], [rdt, rpCB])
        P.op(ACT, lambda e: e.activation(out=dt[:], in_=dt[:], func=AF.Exp), [rdt], [rdt])
        P.op(ACT, lambda e: e.activation(out=dt[:], in_=dt[:], func=AF.Ln, bias=1.0), [rdt], [rdt])
        P.op(DVE, lambda e: e.scalar_tensor_tensor(out=av[:], in0=dt[:], scalar=-1.0, in1=eA[:], op0=ALU.mult, op1=ALU.mult), [rdt, reA], [rav])
        def mma(e):
            e.matmul(pCB[:, 264:272], lhsT=self.tri_incl[:], rhs=av[:], start=True, stop=True)
            return e.matmul(pCB[:, 272:280], lhsT=self.ones_f[:], rhs=av[:], start=True, stop=True)
        P.op(PE, mma, [rav, self.rconst], [rpCB])
        P.op(ACT, lambda e: e.activation(out=sc[:, 0, :], in_=pCB[:, 264:272], func=AF.Exp), [], [rsc, rpCB])
        P.op(ACT, lambda e: e.activation(out=sc[:, 2, :], in_=pCB[:, 272:280], func=AF.Exp), [], [rsc, rpCB])
        P.op(ACT, lambda e: e.activation(out=tmp8[:], in_=pCB[:, 264:272], func=AF.Copy), [], [rtmp8, rpCB])
        P.op(DVE, lambda e: e.tensor_tensor(out=tmp8[:], in0=pCB[:, 272:280], in1=tmp8[:], op=ALU.subtract), [], [rtmp8, rpCB])
        P.op(ACT, lambda e: e.activation(out=sc[:, 1, :], in_=tmp8[:], func=AF.Exp), [rtmp8], [rsc])
        P.op(DVE, lambda e: e.tensor_tensor(out=TA8[:], in0=self.tri_incl[:].unsqueeze(1).to_broadcast([128, 8, 128]),
                                            in1=av[:].unsqueeze(2).to_broadcast([128, 8, 128]), op=ALU.mult), [rav, self.rconst], [rTA])
        for half in range(2):
            def mml(e, half=half):
                ins = None
                for hh in range(4):
                    h = half * 4 + hh
                    e.matmul(pLG[:, hh, :], lhsT=sgt_c[:], rhs=TA8[:, h, :], start=True, stop=False)
                    ins = e.matmul(pLG[:, hh, :], lhsT=self.ident_f[:], rhs=self.neg_incl[:], start=False, stop=True)
                return ins
            P.op(PE, mml, [rTA, rsgt_c, self.rconst], [rpLG])
            P.op(ACT, lambda e, half=half: e.activation(out=gT[:, half * 4:(half + 1) * 4, :], in_=pLG[:], func=AF.Exp), [], [rgT, rpLG])
        def mmcb(e):
            ins = None
            for g in range(2):
                ins = e.matmul(pCB[:, g * 128:(g + 1) * 128], lhsT=XT[:, 4 + g, cs], rhs=XT[:, 6 + g, cs], start=True, stop=True)
            return ins
        P.op(PE, mmcb, [rXT[4], rXT[5], rXT[6], rXT[7]], [rpCB])
        P.op(DVE, lambda e: e.tensor_tensor(out=ATt[:].rearrange("p (g a) l -> p g a l", g=2),
                                            in0=pCB[:, 0:256].rearrange("p (g l) -> p g l", g=2).unsqueeze(2).to_broadcast([128, 2, 4, 128]),
                                            in1=gT[:].rearrange("p (g a) l -> p g a l", g=2), op=ALU.mult), [rgT], [rAT, rpCB])
        def trx(e):
            ins = None
            for c in range(4):
                ins = e.transpose(pt0[:, c, :], XT[:, c, cs], self.ident_bf[:])
            return ins
        P.op(PE, trx, [rXT[0], rXT[1], rXT[2], rXT[3]], [rpt0])
        def trb(e):
            ins = None
            for g in range(2):
                ins = e.transpose(pt1[:, g, :], XT[:, 4 + g, cs], self.ident_bf[:])
            return ins
        P.op(PE, trb, [rXT[4], rXT[5]], [rpt1])
        P.op(ACT, lambda e: e.activation(out=Btok[:], in_=pt1[:, 0:2, :], func=AF.Copy), [], [rBt, rpt1])
        xs_ps = pt0[:, 0:4, :].rearrange("p c (a q) -> p (c a) q", q=64)
        P.op(DVE, lambda e: e.tensor_tensor(out=Xd[:], in0=xs_ps, in1=dt[:].unsqueeze(2).to_broadcast([128, 8, 64]), op=ALU.mult), [rdt], [rXd, rpt0])
        P.op(DVE, lambda e: e.tensor_tensor(out=xsD[:], in0=xs_ps, in1=PP[:, mDo:mDo + 8].unsqueeze(2).to_broadcast([128, 8, 64]), op=ALU.mult), [self.rPP], [rxsD, rpt0])
        P.op(POOL, lambda e: e.tensor_tensor(out=Xdec[:], in0=Xd[:], in1=sc[:, 1, :].unsqueeze(2).to_broadcast([128, 8, 64]), op=ALU.mult), [rXd, rsc], [rXdec])
        if n > 0:
            def mm2(e):
                ins = None
                for g in range(2):
                    ins = e.matmul(p2[:, g * 256:(g + 1) * 256], lhsT=XT[:, 6 + g, cs], rhs=Sbf[:, g * 256:(g + 1) * 256], start=True, stop=True)
                return ins
            P.op(PE, mm2, [rXT[6], rXT[7], rSbf], [rp2])
        def mm3(e):
            ins = None
            for h in range(8):
                ins = e.matmul(p3[:, h * 64:(h + 1) * 64], lhsT=ATt[:, h, :], rhs=Xd[:, h, :], start=True, stop=True)
            return ins
        P.op(PE, mm3, [rAT, rXd], [rp3])
        if n > 0:
            P.op(DVE, lambda e: e.tensor_tensor(out=ytmp[:], in0=p2[:].rearrange("p (h q) -> p h q", q=64), in1=sc[:, 0, :].unsqueeze(2).to_broadcast([128, 8, 64]), op=ALU.mult), [rsc], [rytmp, rp2])
            P.op(DVE, lambda e: e.tensor_tensor(out=yy[:], in0=p3[:], in1=ytmp[:].rearrange("p h q -> p (h q)"), op=ALU.add), [rytmp], [ryy, rp3])
        else:
            P.op(DVE, lambda e: e.tensor_copy(out=yy[:], in_=p3[:]), [], [ryy, rp3])
        P.op(POOL, lambda e: e.tensor_tensor(out=yy[:], in0=yy[:], in1=xsD[:].rearrange("p h q -> p (h q)"), op=ALU.add), [ryy, rxsD], [ryy])
        if n < NT - 1:
            def mm4(e):
                ins = None
                for g in range(2):
                    ins = e.matmul(p4[:, g * 256:(g + 1) * 256], lhsT=Btok[:, g, :], rhs=Xdec[:, g * 4:(g + 1) * 4, :].rearrange("p a q -> p (a q)"), start=True, stop=True)
                return ins
            P.op(PE, mm4, [rBt, rXdec], [rp4])
            if n == 0:
                P.op(DVE, lambda e: e.tensor_copy(out=Sst[:].rearrange("p h q -> p (h q)"), in_=p4[:]), [], [rS, rp4])
            else:
                P.op(DVE, lambda e: e.tensor_tensor(out=Sst[:], in0=Sst[:], in1=sc[:, 2, :].unsqueeze(2).to_broadcast([128, 8, 64]), op=ALU.mult), [rS, rsc], [rS])
                P.op(DVE, lambda e: e.tensor_tensor(out=Sst[:].rearrange("p h q -> p (h q)"), in0=p4[:], in1=Sst[:].rearrange("p h q -> p (h q)"), op=ALU.add), [rS], [rS, rp4])
            P.op(ACT, lambda e: e.activation(out=Sbf[:], in_=Sst[:].rearrange("p h q -> p (h q)"), func=AF.Copy), [rS], [rSbf])
        self.proj_tm(Wz, rWz, HT, rHT, n, pz[:], rpz)
        P.op(ACT, lambda e: e.activation(out=sgz[:], in_=pz[:], func=AF.Silu), [], [rsgz, rpz])
        P.op(POOL, lambda e: e.tensor_tensor(out=yy[:], in0=yy[:], in1=sgz[:], op=ALU.mult), [ryy, rsgz], [ryy])
        P.op(POOL, lambda e: e.tensor_tensor(out=sq[:], in0=yy[:], in1=yy[:], op=ALU.mult), [ryy], [rsq])
        P.op(DVE, lambda e: e.tensor_reduce(out=ssg[:, 0, :], in_=sq[:].rearrange("p (g f) -> p g f", g=2), axis=AX.X, op=ALU.add), [rsq], [rssg])
        self.rstd_from_ss(ssg[:, 0, :], ssg[:, 1, :], rssg, rssg, 256)
        P.op(DVE, lambda e: e.tensor_tensor(out=yy[:].rearrange("p (g f) -> p g f", g=2), in0=yy[:].rearrange("p (g f) -> p g f", g=2),
                                            in1=ssg[:, 1, :].unsqueeze(2).to_broadcast([128, 2, 256]), op=ALU.mult), [ryy, rssg], [ryy])
        P.op(POOL, lambda e: e.tensor_tensor(out=ybf[:], in0=yy[:], in1=PP[:, ngo:ngo + 512], op=ALU.mult), [ryy, self.rPP], [rybf])
        def try_(e):
            ins = None
            for c in range(4):
                ins = e.transpose(pt1[:, 4 + c, :], ybf[:, c * 128:(c + 1) * 128], self.ident_bf[:])
            return ins
        P.op(PE, try_, [rybf], [rpt1])
        P.op(ACT, lambda e: e.activation(out=MT[:, 4:8, gs], in_=pt1[:, 4:8, :], func=AF.Copy), [], [rMT[n], rpt1])

    for b in range(NB):
        for c in range(8):
            conv_chunk(b, c)
        for tt in range(4):
            chunk(b * 4 + tt)


Builder.ssd = _ssd


def _gdn(self, HT, rHT, MT, rMT, st):
    P = self.P
    S, NT, NB = self.S, self.NT, self.NB
    sb, ps = self.sb, self.ps
    PP = self.PP
    v4 = lambda ap: ap.rearrange("p (h c) -> p h c", h=4)
    YT = sb(st, [128, 12, 512], BF16, "YT"); rYT = [Res() for _ in range(12)]
    U = [sb(st, [128, 515], F32, "U") for _ in range(2)]; rU = [Res(), Res()]
    Y = [sb(st, [128, 512], F32, "Y") for _ in range(2)]; rY = [Res(), Res()]
    HALO = sb(st, [128, 12, 3], F32, "HALO"); rH = [Res() for _ in range(12)]
    Wc = [sb(st, [128, 8, 128], BF16, "Wc") for _ in range(2)]; rWc = [Res(), Res()]
    Wgg = sb(st, [128, 8, 512], BF16, "Wgg"); rWgg = Res()
    Wgab = sb(st, [128, 8, 8], BF16, "Wgab"); rWgab = Res()
    sm2 = [sb(st, [128, 16, 4], F32, "sm") for _ in range(3)]; rsm2 = [Res(), Res(), Res()]
    eAg = sb(st, [128, 4], F32, "eAg"); reAg = Res()
    sqq = sb(st, [128, 4, 128], BF16, "sqq"); rsqq = Res()
    sqk = sb(st, [128, 4, 128], BF16, "sqk"); rsqk = Res()
    TTt = sb(st, [128, 512], F32, "TT"); TT = TTt[:]; rTT = Res()
    offd = sb(st, [128, 128], BF16, "offd"); roffd = Res()
    gNs = sb(st, [128, 4, 128], BF16, "gNs"); rgNs = Res()
    ATt = sb(st, [128, 4, 128], BF16, "ATt"); rAT = Res()
    Xf = sb(st, [128, 4, 256], F32, "Xf"); rXf = Res()
    kD = sb(st, [128, 4, 128], BF16, "kD"); rkD = Res()
    u0 = sb(st, [128, 4, 128], F32, "u0"); ru0 = Res()
    wv = sb(st, [128, 4, 128], BF16, "wv"); rwv = Res()
    wT = sb(st, [128, 4, 128], BF16, "wT"); rwT = Res()
    ubf = sb(st, [128, 4, 128], BF16, "ubf"); rubf = Res()
    Sst = sb(st, [128, 4, 128], F32, "Sst"); rS = Res()
    Sbf = sb(st, [128, 4, 128], BF16, "Sbf"); rSbf = Res()
    sgt_c = sb(st, [128, 128], F32, "sgt_c"); rsgt_c = Res()
    o_t = self.tmp32[0][:, 0:512]; otmp = self.tmp32[0][:, 512:1024]; ro = self.r_tmp32[0]
    sq_t = Y[0][:]; rsq = rY[0]
    sgg = Y[1][:]; rsgg = rY[1]
    TG4 = U[0][:, 0:512]; DRA4 = U[1][:, 0:512]; rTG = rU[0]; rDRA = rU[1]
    gA = self.hn[0][:, 0:512]; gN = self.hn[0][:, 512:1024]; rgm = self.r_hn[0]
    ybf = self.hn[1][:, 0:512]; rybf = self.r_hn[1]
    Pc = [(self.gainB[0][:, 0:512], self.gainB[0][:, 512:1024], self.r_gainB[0], Res()),
          (self.gainB[1][:, 0:512], self.gainB[1][:, 512:1024], self.r_gainB[1], Res())]
    X = self.X
    xs = self.gdn_xsub
    def sub(t):
        r = Res(); xs[t].append(r); r.rd.append(self.gdn_spill_ops[t]); return r
    bf = lambda t: X[:, t, :].bitcast(BF16)
    r4 = lambda ap: ap.rearrange("p (h c) -> p h c", h=4)
    B0 = dict(sqq=sqq, rsqq=rsqq, sqk=sqk, rsqk=rsqk, TG4=TG4, DRA4=DRA4, rTG=rTG, rDRA=rDRA, gA=gA, rgm=rgm,
              gNs=gNs, rgNs=rgNs, ATt=ATt, rAT=rAT, Pc=Pc, TT=TT, rTT=rTT, Xf=Xf, rXf=rXf, kD=kD, rkD=rkD,
              u0=u0, ru0=ru0, wv=wv, rwv=rwv, wT=wT, rwT=rwT)
    def xset(o):
        return dict(sqq=V(r4(bf(o + 5)[:, 0:512])), rsqq=sub(o + 5), sqk=V(r4(bf(o + 5)[:, 512:1024])), rsqk=sub(o + 5),
              gA=bf(o + 5)[:, 1024:1536], rgm=sub(o + 5), gNs=V(r4(bf(o + 5)[:, 1536:2048])), rgNs=sub(o + 5),
              TG4=X[:, o + 4, 0:512], DRA4=X[:, o + 4, 512:1024], rTG=sub(o + 4), rDRA=sub(o + 4),
              ATt=V(r4(bf(o + 6)[:, 0:512])), rAT=sub(o + 6), kD=V(r4(bf(o + 6)[:, 512:1024])), rkD=sub(o + 6),
              wv=V(r4(bf(o + 6)[:, 1024:1536])), rwv=sub(o + 6), wT=V(r4(bf(o + 6)[:, 1536:2048])), rwT=sub(o + 6),
              Pc=[(X[:, o + 0, 0:512], X[:, o + 0, 512:1024], sub(o + 0), sub(o + 0)), (X[:, o + 1, 0:512], X[:, o + 1, 512:1024], sub(o + 1), sub(o + 1))],
              TT=X[:, o + 2, 0:512], rTT=sub(o + 2), u0=V(r4(X[:, o + 2, 512:1024])), ru0=sub(o + 2),
              Xf=V(X[:, o + 3, :].rearrange("p (h c) -> p h c", h=4)), rXf=sub(o + 3))
    NSET = 1 + self.gdn_nx // 7
    BUFS = [B0] + [xset(7 * i) for i in range(NSET - 1)]
    pa = ps(st, [128, 512], F32, "pa"); rpa = Res()
    pb = ps(st, [128, 512], F32, "pb"); rpb = Res()
    pcd = ps(st, [128, 1024], F32, "pcd"); rpcd = Res()
    pc_, pd_ = pcd[:, 0:512], pcd[:, 512:1024]
    pe = ps(st, [128, 512], F32, "pe"); rpe = Res()
    pg = ps(st, [128, 512], F32, "pg"); rpg = Res()
    cwo, _ = self.poff["gcw"]
    alo, _ = self.poff["gAlog"]; dbo, _ = self.poff["gdtb"]; ngo, _ = self.poff["gng"]
    GV, LNB, BETA, GC, GL, EG, EGLG, GLB, LNRK, RK, CR, CD, T1, T2, MSR, RSTD = range(16)
    P.op(POOL, lambda e: e.memset(offd[:], 1.0), [], [roffd])
    P.op(POOL, lambda e: e.affine_select(out=offd[:], in_=offd[:], pattern=[[-1, 128]], compare_op=ALU.not_equal, fill=0.0, base=0, channel_multiplier=1), [roffd], [roffd])

    P.op(POOL, lambda e: e.memset(sgt_c[:], 1.0), [], [rsgt_c])
    P.op(POOL, lambda e: e.affine_select(out=sgt_c[:], in_=sgt_c[:], pattern=[[-1, 128]], compare_op=ALU.is_gt, fill=0.0, base=0, channel_multiplier=1), [rsgt_c], [rsgt_c])
    P.op(POOL, lambda e: e.memset(HALO[:], 0.0), [], rH)
    P.op(ACT, lambda e: e.activation(out=eAg[:], in_=PP[:, alo:alo + 4], func=AF.Exp), [self.rPP], [reAg])
    self.load_w("in0_gg", Wgg, rWgg)
    self.load_w("in0_gab", Wgab, rWgab)

    def conv_chunk(b, c):
        k = c % 2
        name = ("gq", "gk", "gv")[c // 4] + str(c % 4)
        self.load_w(f"in0_{name}", Wc[k], rWc[k])
        pbk, rpbk = (pa, rpa) if k == 0 else (pb, rpb)
        self.proj_fm(Wc[k], rWc[k], HT, rHT, b, pbk, rpbk)
        u, ru, y, ry = U[k], rU[k], Y[k], rY[k]
        P.op(POOL, lambda e: e.tensor_copy(out=u[:, 0:3], in_=HALO[:, c, :]), [rH[c]], [ru])
        P.op(ACT, lambda e: e.activation(out=u[:, 3:515], in_=pbk[:], func=AF.Copy), [], [ru, rpbk])
        P.op(POOL, lambda e: e.tensor_copy(out=HALO[:, c, :], in_=u[:, 512:515]), [ru], [rH[c]])
        w = [PP[:, cwo + c * 4 + i:cwo + c * 4 + i + 1] for i in range(4)]
        P.op(DVE, lambda e: e.tensor_scalar(out=y[:], in0=u[:, 0:512], scalar1=w[0], scalar2=None, op0=ALU.mult), [ru], [ry])
        for i in range(1, 4):
            P.op(DVE, lambda e, i=i: e.scalar_tensor_tensor(out=y[:], in0=u[:, i:i + 512], scalar=w[i], in1=y[:], op0=ALU.mult, op1=ALU.add), [ru, ry], [ry])
        P.op(ACT, lambda e: e.activation(out=YT[:, c, :], in_=y[:], func=AF.Silu), [ry], [rYT[c]])

    def chunk(n):
        sm, rsm = sm2[n % len(BUFS)], rsm2[n % len(BUFS)]
        Bf = BUFS[n % len(BUFS)]
        sqq, rsqq, sqk, rsqk = Bf["sqq"], Bf["rsqq"], Bf["sqk"], Bf["rsqk"]
        TG4, DRA4, rTG, rDRA = Bf["TG4"], Bf["DRA4"], Bf["rTG"], Bf["rDRA"]
        gA, rgm, gNs, rgNs = Bf["gA"], Bf["rgm"], Bf["gNs"], Bf["rgNs"]
        ATt, rAT, Pc, TT, rTT = Bf["ATt"], Bf["rAT"], Bf["Pc"], Bf["TT"], Bf["rTT"]
        Xf, rXf, kD, rkD = Bf["Xf"], Bf["rXf"], Bf["kD"], Bf["rkD"]
        u0, ru0, wv, rwv, wT, rwT = Bf["u0"], Bf["ru0"], Bf["wv"], Bf["rwv"], Bf["wT"], Bf["rwT"]

        def bc(row):
            return sm[:, row, :].unsqueeze(2).to_broadcast([128, 4, 128])
        tt = n % 4
        cs = slice(tt * 128, (tt + 1) * 128)
        gs = slice(n * 128, (n + 1) * 128)
        pt0, rpt0 = self.pt[0], self.r_pt[0]
        pt1, rpt1 = self.pt[1], self.r_pt[1]
        rq = [rYT[h] for h in range(4)]; rk = [rYT[4 + h] for h in range(4)]; rv = [rYT[8 + h] for h in range(4)]
        self.proj_tm(Wgab, rWgab, HT, rHT, n, pe[:, 0:8], rpe)
        P.op(DVE, lambda e: e.tensor_tensor(out=sm[:, GV, :], in0=pe[:, 0:4], in1=PP[:, dbo:dbo + 4], op=ALU.add), [self.rPP], [rsm, rpe])
        P.op(ACT, lambda e: e.activation(out=sm[:, GV, :], in_=sm[:, GV, :], func=AF.Exp), [rsm], [rsm])
        P.op(ACT, lambda e: e.activation(out=sm[:, GV, :], in_=sm[:, GV, :], func=AF.Ln, bias=1.0), [rsm], [rsm])
        P.op(DVE, lambda e: e.scalar_tensor_tensor(out=sm[:, GV, :], in0=sm[:, GV, :], scalar=-1.0, in1=eAg[:], op0=ALU.mult, op1=ALU.mult), [rsm, reAg], [rsm])
        P.op(ACT, lambda e: e.activation(out=sm[:, LNB, :], in_=pe[:, 4:8], func=AF.Exp, scale=-1.0), [], [rsm, rpe])
        P.op(ACT, lambda e: e.activation(out=sm[:, LNB, :], in_=sm[:, LNB, :], func=AF.Ln, bias=1.0), [rsm], [rsm])
        P.op(ACT, lambda e: e.activation(out=sm[:, BETA, :], in_=sm[:, LNB, :], func=AF.Exp, scale=-1.0), [rsm], [rsm])
        P.op(ACT, lambda e: e.activation(out=sqq[:], in_=YT[:, 0:4, cs], func=AF.Square), rq, [rsqq])
        P.op(ACT, lambda e: e.activation(out=sqk[:], in_=YT[:, 4:8, cs], func=AF.Square), rk, [rsqk])
        def mmg(e):
            e.matmul(pe[:, 8:12], lhsT=self.tri_incl[:], rhs=sm[:, GV, :], start=True, stop=True)
            ins = e.matmul(pe[:, 12:16], lhsT=self.ones_f[:], rhs=sm[:, GV, :], start=True, stop=True)
            for h in range(4):
                e.matmul(pe[:, 16 + h:17 + h], lhsT=sqk[:, h, :], rhs=self.ones_bf[:, 0:1], start=True, stop=True)
                ins = e.matmul(pe[:, 20 + h:21 + h], lhsT=sqq[:, h, :], rhs=self.ones_bf[:, 0:1], start=True, stop=True)
            return ins
        P.op(PE, mmg, [rsm, rsqq, rsqk, self.rconst], [rpe])
        P.op(ACT, lambda e: e.activation(out=sm[:, GC, :], in_=pe[:, 8:12], func=AF.Copy), [], [rsm, rpe])
        P.op(ACT, lambda e: e.activation(out=sm[:, EG, :], in_=pe[:, 8:12], func=AF.Exp), [], [rsm, rpe])
        P.op(ACT, lambda e: e.activation(out=sm[:, GLB, :], in_=pe[:, 12:16], func=AF.Exp), [], [rsm, rpe])
        P.op(DVE, lambda e: e.tensor_tensor(out=sm[:, GL, :], in0=pe[:, 12:16], in1=sm[:, GC, :], op=ALU.subtract), [rsm], [rsm, rpe])
        P.op(ACT, lambda e: e.activation(out=sm[:, EGLG, :], in_=sm[:, GL, :], func=AF.Exp), [rsm], [rsm])
        P.op(ACT, lambda e: e.activation(out=sm[:, LNRK, :], in_=pe[:, 16:20], func=AF.Ln, bias=self.eps_t[:, 0:1]), [], [rsm, rpe])
        P.op(DVE, lambda e: e.tensor_scalar(out=sm[:, LNRK, :], in0=sm[:, LNRK, :], scalar1=-0.5, scalar2=None, op0=ALU.mult), [rsm], [rsm])
        P.op(ACT, lambda e: e.activation(out=sm[:, RK, :], in_=sm[:, LNRK, :], func=AF.Exp), [rsm], [rsm])
        P.op(ACT, lambda e: e.activation(out=sm[:, CR, :], in_=sm[:, LNRK, :], func=AF.Exp, scale=-1.0), [rsm], [rsm])
        P.op(DVE, lambda e: e.tensor_tensor(out=sm[:, CD, :], in0=sm[:, RK, :], in1=sm[:, EGLG, :], op=ALU.mult), [rsm], [rsm])
        P.op(DVE, lambda e: e.tensor_tensor(out=sm[:, T1, :], in0=sm[:, RK, :], in1=sm[:, BETA, :], op=ALU.mult), [rsm], [rsm])
        P.op(DVE, lambda e: e.tensor_scalar(out=sm[:, T2, :], in0=pe[:, 20:24], scalar1=EPS, scalar2=128.0 * EPS, op0=ALU.add, op1=ALU.mult), [], [rsm, rpe])
        def mmgram(e):
            ins = None
            for h in range(4):
                e.matmul(pa[:, h * 128:(h + 1) * 128], lhsT=YT[:, 4 + h, cs], rhs=YT[:, 4 + h, cs], start=True, stop=True)
                ins = e.matmul(pb[:, h * 128:(h + 1) * 128], lhsT=YT[:, 4 + h, cs], rhs=YT[:, h, cs], start=True, stop=True)
            return ins
        P.op(PE, mmgram, rq + rk, [rpa, rpb])
        I4 = self.ident_f[:].unsqueeze(1).to_broadcast([128, 4, 128])
        P.op(DVE, lambda e: e.tensor_tensor(out=v4(TG4), in0=self.tri_incl[:].unsqueeze(1).to_broadcast([128, 4, 128]), in1=bc(GV), op=ALU.mult), [rsm, self.rconst], [rTG])
        P.op(DVE, lambda e: e.tensor_tensor(out=v4(DRA4), in0=I4, in1=bc(LNRK), op=ALU.mult), [rsm, self.rconst], [rDRA])
        def mmlg(e):
            ins = None
            for h in range(4):
                hs = slice(h * 128, (h + 1) * 128)
                e.matmul(pc_[:, hs], lhsT=sgt_c[:], rhs=TG4[:, hs], start=True, stop=False)
                e.matmul(pc_[:, hs], lhsT=DRA4[:, hs], rhs=self.ones_f[:], start=False, stop=False)
                ins = e.matmul(pc_[:, hs], lhsT=self.ident_f[:], rhs=self.neg_incl[:], start=False, stop=True)
            return ins
        P.op(PE, mmlg, [rTG, rDRA, rsgt_c, self.rconst], [rpcd])
        P.op(ACT, lambda e: e.activation(out=gA, in_=pc_, func=AF.Exp), [], [rgm, rpcd])
        P.op(DVE, lambda e: e.tensor_tensor(out=ATt[:].rearrange("p h c -> p (h c)"), in0=pb[:], in1=gA, op=ALU.mult), [rgm], [rAT, rpb])
        P.op(POOL, lambda e: e.tensor_tensor(out=gNs[:], in0=v4(gA), in1=bc(T1), op=ALU.mult), [rgm, rsm], [rgNs])
        P.op(POOL, lambda e: e.tensor_tensor(out=gNs[:], in0=gNs[:], in1=offd[:].unsqueeze(1).to_broadcast([128, 4, 128]), op=ALU.mult), [rgNs, roffd], [rgNs])
        P0, P0T, rP0, rP0T = Pc[0]
        P.op(DVE, lambda e: e.scalar_tensor_tensor(out=P0T, in0=pa[:], scalar=-1.0, in1=gNs[:].rearrange("p h c -> p (h c)"), op0=ALU.mult, op1=ALU.mult), [rgNs], [rP0T, rpa])
        def trp(e):
            ins = None
            for h in range(4):
                hs = slice(h * 128, (h + 1) * 128)
                ins = e.matmul(pb[:, hs], lhsT=P0T[:, hs], rhs=self.ident_f[:], start=True, stop=True)
            return ins
        P.op(PE, trp, [rP0T, self.rconst], [rpb])
        P.op(ACT, lambda e: e.activation(out=P0, in_=pb[:], func=AF.Copy), [], [rP0, rpb])
        def trkv(e):
            ins = None
            for h in range(4):
                e.transpose(pt0[:, h, :], YT[:, 4 + h, cs], self.ident_bf[:])
                ins = e.transpose(pt0[:, 4 + h, :], YT[:, 8 + h, cs], self.ident_bf[:])
            return ins
        P.op(PE, trkv, rk + rv, [rpt0])
        P.op(DVE, lambda e: e.tensor_tensor(out=Xf[:, :, 0:128], in0=pt0[:, 4:8, :], in1=bc(CR), op=ALU.mult), [rsm], [rXf, rpt0])
        P.op(DVE, lambda e: e.tensor_tensor(out=Xf[:, :, 128:256], in0=pt0[:, 0:4, :], in1=bc(EG), op=ALU.mult), [rsm], [rXf, rpt0])
        P.op(DVE, lambda e: e.tensor_tensor(out=kD[:], in0=pt0[:, 0:4, :], in1=bc(CD), op=ALU.mult), [rsm], [rkD, rpt0])
        P.op(POOL, lambda e: e.tensor_tensor(out=v4(TT), in0=v4(P0T), in1=self.ident_f[:].unsqueeze(1).to_broadcast([128, 4, 128]), op=ALU.add), [rP0T, self.rconst], [rTT])
        yield
        for j in range(6):
            Pj, PjT, rPj, rPjT = Pc[j % 2]
            Pn, PnT, rPn, rPnT = Pc[(j + 1) % 2]
            def mmsq(e, Pj=Pj, PjT=PjT):
                ins = None
                for h in range(4):
                    hs = slice(h * 128, (h + 1) * 128)
                    e.matmul(pa[:, hs], lhsT=PjT[:, hs], rhs=Pj[:, hs], start=True, stop=True)
                    ins = e.matmul(pb[:, hs], lhsT=Pj[:, hs], rhs=PjT[:, hs], start=True, stop=True)
                return ins
            P.op(PE, mmsq, [rPj, rPjT], [rpa, rpb])
            P.op(ACT, lambda e, Pn=Pn: e.activation(out=Pn, in_=pa[:], func=AF.Copy), [], [rPn, rpa])
            P.op(DVE, lambda e, PnT=PnT: e.tensor_copy(out=PnT, in_=pb[:]), [], [rPnT, rpb])
            def mmt(e, Pn=Pn):
                ins = None
                for h in range(4):
                    hs = slice(h * 128, (h + 1) * 128)
                    ins = e.matmul(pc_[:, hs], lhsT=Pn[:, hs], rhs=TT[:, hs], start=True, stop=True)
                return ins
            P.op(PE, mmt, [rPn, rTT], [rpcd])
            P.op(DVE, lambda e: e.tensor_tensor(out=TT, in0=pc_, in1=TT, op=ALU.add), [], [rTT, rpcd])
            yield
        def mmz(e):
            ins = None
            for h in range(4):
                ins = e.matmul(pcd[:, h * 256:(h + 1) * 256], lhsT=TT[:, h * 128:(h + 1) * 128], rhs=Xf[:, h, :], start=True, stop=True)
            return ins
        P.op(PE, mmz, [rTT, rXf], [rpcd])
        P.op(ACT, lambda e: e.activation(out=Xf[:].rearrange("p h c -> p (h c)"), in_=pcd[:], func=AF.Copy), [], [rXf, rpcd])
        Z, rZ = Xf, rXf
        P.op(DVE, lambda e: e.tensor_tensor(out=u0[:], in0=Z[:, :, 0:128], in1=bc(T1), op=ALU.mult), [rZ, rsm], [ru0])
        P.op(POOL, lambda e: e.tensor_tensor(out=wv[:], in0=Z[:, :, 128:256], in1=bc(T1), op=ALU.mult), [rZ, rsm], [rwv])
        def trw(e):
            ins = None
            for h in range(4):
                ins = e.transpose(pt1[:, 4 + h, :], wv[:, h, :], self.ident_bf[:])
            return ins
        P.op(PE, trw, [rwv], [rpt1])
        P.op(ACT, lambda e: e.activation(out=wT[:], in_=pt1[:, 4:8, :], func=AF.Copy), [], [rwT, rpt1])
        if n > 0:
            def mm1(e):
                ins = None
                for h in range(4):
                    ins = e.matmul(pa[:, h * 128:(h + 1) * 128], lhsT=wT[:, h, :], rhs=Sbf[:, h, :], start=True, stop=True)
                return ins
            P.op(PE, mm1, [rwT, rSbf], [rpa])
            P.op(DVE, lambda e: e.tensor_tensor(out=ubf[:].rearrange("p h c -> p (h c)"), in0=u0[:].rearrange("p h c -> p (h c)"), in1=pa[:], op=ALU.subtract), [ru0], [rubf, rpa])
            def mm2(e):
                ins = None
                for h in range(4):
                    ins = e.matmul(pb[:, h * 128:(h + 1) * 128], lhsT=YT[:, h, cs], rhs=Sbf[:, h, :], start=True, stop=True)
                return ins
            P.op(PE, mm2, rq + [rSbf], [rpb])
        else:
            P.op(DVE, lambda e: e.tensor_copy(out=ubf[:], in_=u0[:]), [ru0], [rubf])
        def mm3(e):
            ins = None
            for h in range(4):
                ins = e.matmul(pc_[:, h * 128:(h + 1) * 128], lhsT=ATt[:, h, :], rhs=ubf[:, h, :], start=True, stop=True)
            return ins
        P.op(PE, mm3, [rAT, rubf], [rpcd])
        if n > 0:
            P.op(DVE, lambda e: e.tensor_tensor(out=v4(otmp), in0=v4(pb[:]), in1=bc(EG), op=ALU.mult), [rsm], [ro, rpb])
            P.op(DVE, lambda e: e.tensor_tensor(out=o_t, in0=pc_, in1=otmp, op=ALU.add), [], [ro, rpcd])
        else:
            P.op(DVE, lambda e: e.tensor_copy(out=o_t, in_=pc_), [], [ro, rpcd])
        if n < NT - 1:
            def mm4(e):
                ins = None
                for h in range(4):
                    ins = e.matmul(pd_[:, h * 128:(h + 1) * 128], lhsT=kD[:, h, :], rhs=ubf[:, h, :], start=True, stop=True)
                return ins
            P.op(PE, mm4, [rkD, rubf], [rpcd])
            if n == 0:
                P.op(DVE, lambda e: e.tensor_copy(out=Sst[:].rearrange("p h c -> p (h c)"), in_=pd_), [], [rS, rpcd])
            else:
                P.op(DVE, lambda e: e.tensor_tensor(out=Sst[:], in0=Sst[:], in1=bc(GLB), op=ALU.mult), [rS, rsm], [rS])
                P.op(DVE, lambda e: e.tensor_tensor(out=Sst[:].rearrange("p h c -> p (h c)"), in0=pd_, in1=Sst[:].rearrange("p h c -> p (h c)"), op=ALU.add), [rS], [rS, rpcd])
            P.op(ACT, lambda e: e.activation(out=Sbf[:], in_=Sst[:], func=AF.Copy), [rS], [rSbf])
        P.op(POOL, lambda e: e.tensor_tensor(out=sq_t, in0=o_t, in1=o_t, op=ALU.mult), [ro], [rsq])
        P.op(DVE, lambda e: e.tensor_reduce(out=sm[:, MSR, :], in_=v4(sq_t), axis=AX.X, op=ALU.add), [rsq], [rsm])
        P.op(DVE, lambda e: e.scalar_tensor_tensor(out=sm[:, MSR, :], in0=sm[:, MSR, :], scalar=1.0 / 128, in1=sm[:, T2, :], op0=ALU.mult, op1=ALU.add), [rsm], [rsm])
        P.op(ACT, lambda e: e.activation(out=sm[:, RSTD, :], in_=sm[:, MSR, :], func=AF.Sqrt), [rsm], [rsm])
        P.op(DVE, lambda e: e.reciprocal(out=sm[:, RSTD, :], in_=sm[:, RSTD, :]), [rsm], [rsm])
        P.op(DVE, lambda e: e.tensor_tensor(out=v4(o_t), in0=v4(o_t), in1=bc(RSTD), op=ALU.mult), [ro, rsm], [ro])
        P.op(POOL, lambda e: e.tensor_tensor(out=v4(o_t), in0=v4(o_t), in1=PP[:, ngo:ngo + 128].unsqueeze(1).to_broadcast([128, 4, 128]), op=ALU.mult), [ro, self.rPP], [ro])
        self.proj_tm(Wgg, rWgg, HT, rHT, n, pg[:], rpg)
        P.op(ACT, lambda e: e.activation(out=sgg, in_=pg[:], func=AF.Silu), [], [rsgg, rpg])
        P.op(POOL, lambda e: e.tensor_tensor(out=ybf, in0=o_t, in1=sgg, op=ALU.mult), [ro, rsgg], [rybf])
        pgb = pg[:].bitcast(BF16)
        def try_(e):
            ins = None
            for c in range(4):
                ins = e.transpose(pgb[:, c * 128:(c + 1) * 128], ybf[:, c * 128:(c + 1) * 128], self.ident_bf[:])
            return ins
        P.op(PE, try_, [rybf], [rpg])
        P.op(ACT, lambda e: e.activation(out=MT[:, 4:8, gs], in_=pgb[:, 0:512].rearrange("p (c t) -> p c t", c=4), func=AF.Copy), [], [rMT[n], rpg])

    def run(gens):
        alive = list(gens)
        for gn in alive:
            next(gn)
        for j in range(6):
            for gn in alive:
                next(gn)
        for gn in alive:
            for _ in gn:
                pass

    for b in range(NB):
        for c in range(12):
            conv_chunk(b, c)
        run([chunk(b * 4 + 0), chunk(b * 4 + 1)])
        run([chunk(b * 4 + 2), chunk(b * 4 + 3)])


Builder.gdn = _gdn
```

```python
import contextlib
import numpy as np
import concourse.bass as bass
import concourse.mybir as mybir
from concourse.bass_utils import run_bass_kernel_spmd

F32 = mybir.dt.float32
BF16 = mybir.dt.bfloat16
ALU = mybir.AluOpType
AF = mybir.ActivationFunctionType
AX = mybir.AxisListType

PE, ACT, DVE, POOL, SP = "tensor", "scalar", "vector", "gpsimd", "sync"
EPOCH = 30000
NDSEM = 8

D = 1024
DFF = 2816
NFF = 22
EPS = 1e-6
NCORES = 8


import heapq


class V:
    def __init__(self, ap):
        self.ap = ap

    def __getitem__(self, k):
        return self.ap if (isinstance(k, slice) and k == slice(None)) else self.ap[k]


class Res:
    __slots__ = ("lw", "rd")

    def __init__(self):
        self.lw = None
        self.rd = []


class _Rec:
    class _I:
        def then_inc(self, *a, **k):
            return self

    def __init__(self):
        self.calls = []

    def __getattr__(self, name):
        def f(*a, **k):
            self.calls.append((name, a, k))
            return _Rec._I()
        return f


def _free(ap):
    n = 1
    for d in ap.shape[1:]:
        n *= d
    return n


def _cost(eng, fn, is_dma):
    r = _Rec()
    try:
        fn(r)
    except Exception:
        return 500.0
    c = 0.0
    for name, a, k in r.calls:
        out = k.get("out", a[0] if a else None)
        try:
            n = _free(out)
        except Exception:
            n = 128
        if is_dma:
            try:
                byts = n * out.shape[0] * (4 if out.dtype == F32 else 2)
            except Exception:
                byts = 65536
            c += 2000.0 + byts / 150.0
        elif eng == PE:
            lhsT = k.get("lhsT", a[1] if len(a) > 1 else None)
            f32 = False
            try:
                f32 = (lhsT is not None and lhsT.dtype == F32)
            except Exception:
                pass
            if name == "transpose":
                c += 110.0
            else:
                c += (max(n, 64) / 2.4 + 25.0) * (4.0 if f32 else 1.0)
        elif eng == ACT:
            c += 230.0 + 0.85 * n
        elif eng == DVE:
            c += 120.0 + 1.05 * n
        else:
            c += 150.0 + 1.7 * n
    return max(c, 60.0)


class Op:
    __slots__ = ("eng", "fn", "deps", "is_dma", "sig", "ticket", "dslot", "dtarget", "cost", "seg", "nobar", "is_bar", "fin", "start", "tag")

    def __init__(self, eng, fn, is_dma):
        self.eng = eng
        self.fn = fn
        self.deps = set()
        self.is_dma = is_dma
        self.sig = False
        self.ticket = None
        self.nobar = False
        self.is_bar = False
        self.fin = 0.0


SYNC_LAT = 500.0
SCHEDULE = True
PRIO_CP = False


class Prog:
    def __init__(self, nc):
        self.nc = nc
        self.ops = []
        self.seg = 0
        self.bar = None

    def op(self, eng, fn, reads=(), writes=(), dma=False, nobar=False):
        i = len(self.ops)
        o = Op(eng, fn, dma)
        o.cost = _cost(eng, fn, dma)
        o.seg = self.seg
        import sys as _sys
        fr = _sys._getframe(1)
        if fr.f_code.co_name in ("dma", "proj_fm", "proj_tm", "_proj_fm", "_proj_tm", "rstd_from_ss"):
            fr = fr.f_back
        o.tag = f"{fr.f_code.co_name}:{fr.f_lineno}"
        o.start = 0.0
        o.nobar = nobar
        for r in reads:
            if r.lw is not None:
                o.deps.add(r.lw)
        for w in writes:
            if w.lw is not None:
                o.deps.add(w.lw)
            o.deps.update(w.rd)
        for r in reads:
            r.rd.append(i)
        for w in writes:
            w.lw = i
            w.rd = []
        if not nobar and self.bar is not None:
            o.deps.add(self.bar)
        o.deps.discard(i)
        self.ops.append(o)
        return i

    def dma(self, q, out, in_, reads=(), writes=(), nobar=False):
        return self.op(q, lambda e: e.dma_start(out=out, in_=in_), reads, writes, dma=True, nobar=nobar)

    def barrier(self, scratch_ap):
        i = len(self.ops)
        o = Op(POOL, lambda e: e.memset(scratch_ap, 0.0), False)
        o.cost = 100.0
        o.seg = self.seg
        o.is_bar = True
        if self.bar is not None:
            o.deps.add(self.bar)
        self.ops.append(o)
        self.bar = i
        self.seg += 1
        return i

    def schedule(self):
        ops = self.ops
        order = {}
        eng_free = {}
        self.seg_stats = []
        segs = {}
        for i, o in enumerate(ops):
            segs.setdefault(o.seg, []).append(i)
        tglob = 0.0
        for sg in sorted(segs):
            idxs = segs[sg]
            inseg = set(idxs)
            bar_i = None
            body = []
            for i in idxs:
                if ops[i].is_bar:
                    bar_i = i
                else:
                    body.append(i)
            if not SCHEDULE:
                for i in body:
                    order.setdefault(ops[i].eng, []).append(i)
            else:
                ndeps = {}
                users = {}
                for i in body:
                    c = 0
                    for j in ops[i].deps:
                        if j in inseg and not ops[j].is_bar:
                            c += 1
                            users.setdefault(j, []).append(i)
                    ndeps[i] = c
                bl = {}
                for i in reversed(body):
                    m = 0.0
                    for u in users.get(i, ()):
                        lat = 0.0 if (ops[u].eng == ops[i].eng == PE) else SYNC_LAT
                        m = max(m, bl[u] + lat)
                    bl[i] = m + ops[i].cost
                ready = {}
                rtime = {}
                for i in body:
                    if ndeps[i] == 0:
                        rtime[i] = tglob
                        heapq.heappush(ready.setdefault(ops[i].eng, []), (tglob, i))
                for e in ready:
                    eng_free.setdefault(e, tglob)
                remaining = len(body)
                while remaining:
                    best = None
                    for e, hp in ready.items():
                        if not hp:
                            continue
                        ef = max(eng_free.get(e, tglob), tglob)
                        cand = None
                        avail = [x for x in hp if x[0] <= ef]
                        if avail:
                            if PRIO_CP:
                                ci = min(avail, key=lambda x: (-bl[x[1]], x[1]))
                            else:
                                ci = min(avail, key=lambda x: x[1])
                            cand = (ef, ci[1], e, ci)
                        else:
                            ci = hp[0]
                            cand = (ci[0], ci[1], e, ci)
                        if best is None or cand[:2] < best[:2]:
                            best = cand
                    start, i, e, item = best
                    hp = ready[e]
                    hp.remove(item)
                    heapq.heapify(hp)
                    o = ops[i]
                    if o.is_dma:
                        eng_free[e] = start + (6000.0 if e == POOL else 70.0)
                        o.fin = start + o.cost
                    else:
                        eng_free[e] = start + o.cost
                        o.fin = start + o.cost
                    order.setdefault(e, []).append(i)
                    o.start = start
                    remaining -= 1
                    for u in users.get(i, ()):
                        ndeps[u] -= 1
                        lat = 0.0 if (ops[u].eng == e and e == PE) else SYNC_LAT
                        rtime[u] = max(rtime.get(u, tglob), o.fin + lat)
                        if ndeps[u] == 0:
                            heapq.heappush(ready.setdefault(ops[u].eng, []), (rtime[u], u))
                if body:
                    t_prev = tglob
                    tglob = max([tglob] + [ops[i].fin for i in body])
                    busy = {}
                    for i in body:
                        if not ops[i].is_dma:
                            busy[ops[i].eng] = busy.get(ops[i].eng, 0.0) + ops[i].cost
                    self.seg_stats.append((sg, tglob - t_prev, busy))
            if bar_i is not None:
                b = ops[bar_i]
                last = {}
                for i in body:
                    o = ops[i]
                    if o.nobar:
                        continue
                    if o.is_dma:
                        b.deps.add(i)
                for e, lst in order.items():
                    for i in reversed(lst):
                        if ops[i].seg != sg:
                            break
                        if not ops[i].is_dma and not ops[i].nobar:
                            b.deps.add(i)
                            break
                order.setdefault(POOL, []).append(bar_i)
                eng_free[POOL] = tglob + 100.0
                tglob += 300.0
        self.est_ns = tglob
        return order

    def emit(self, final_wait_ops=()):
        nc = self.nc
        ops = self.ops
        per_eng = self.schedule()
        for i, o in enumerate(ops):
            for j in o.deps:
                d = ops[j]
                if d.is_dma:
                    continue
                if d.eng == PE and o.eng == PE and not o.is_dma:
                    continue
                d.sig = True
        for j in final_wait_ops:
            if not ops[j].is_dma:
                ops[j].sig = True
        cnt = {}
        dcnt = {}
        for eng, lst in per_eng.items():
            for i in lst:
                o = ops[i]
                if o.is_dma:
                    k = dcnt.get(eng, 0)
                    dcnt[eng] = k + 1
                    o.dslot = k % NDSEM
                    o.dtarget = 16 * (k // NDSEM + 1)
                elif o.sig:
                    cnt[eng] = cnt.get(eng, 0) + 1
                    o.ticket = cnt[eng]
        with contextlib.ExitStack() as st:
            sems = {}
            for eng, n in cnt.items():
                for ep in range((n + EPOCH - 1) // EPOCH + 1):
                    sems[(eng, ep)] = st.enter_context(nc.semaphore(f"s_{eng}_{ep}"))
            dsems = {}
            for q in dcnt:
                for s in range(NDSEM):
                    dsems[(q, s)] = st.enter_context(nc.semaphore(f"d_{q}_{s}"))
            block = st.enter_context(nc.Block())

            def run_engine(engname):
                def body(e):
                    waited = {}
                    for i in per_eng.get(engname, []):
                        o = ops[i]
                        need = {}
                        for j in o.deps:
                            d = ops[j]
                            if d.is_dma:
                                key = ("d", d.eng, d.dslot)
                                need[key] = max(need.get(key, 0), d.dtarget)
                            else:
                                if d.eng == PE and o.eng == PE and not o.is_dma:
                                    continue
                                ep = (d.ticket - 1) // EPOCH
                                key = ("c", d.eng, ep)
                                need[key] = max(need.get(key, 0), d.ticket - ep * EPOCH)
                        if o.is_dma and o.dtarget > 16:
                            key = ("d", o.eng, o.dslot)
                            need[key] = max(need.get(key, 0), o.dtarget - 16)
                        for key, v in need.items():
                            if waited.get(key, 0) >= v:
                                continue
                            waited[key] = v
                            s = dsems[(key[1], key[2])] if key[0] == "d" else sems[(key[1], key[2])]
                            e.wait_ge(s, v)
                        ins = o.fn(e)
                        if o.is_dma:
                            ins.then_inc(dsems[(o.eng, o.dslot)], 16)
                        elif o.sig:
                            ep = (o.ticket - 1) // EPOCH
                            ins.then_inc(sems[(o.eng, ep)], 1)
                    if engname == SP:
                        for j in final_wait_ops:
                            d = ops[j]
                            if d.is_dma:
                                e.wait_ge(dsems[(d.eng, d.dslot)], d.dtarget)
                            else:
                                ep = (d.ticket - 1) // EPOCH
                                e.wait_ge(sems[(d.eng, ep)], d.ticket - ep * EPOCH)
                return body

            for engname in (SP, ACT, POOL, DVE, PE):
                if engname in per_eng or engname == SP:
                    getattr(block, engname)(run_engine(engname))


def _kp(w):
    n = w.shape[1]
    return np.ascontiguousarray(w.reshape(8, 128, n).transpose(1, 0, 2).reshape(128, 8 * n))


def _pc(v):
    return np.ascontiguousarray(v.reshape(-1, 128).T)


class Pack:
    def __init__(self):
        self.parts = []
        self.off = {}
        self.n = 0

    def add(self, name, arr):
        arr = np.asarray(arr, np.float32)
        assert arr.shape[0] == 128, (name, arr.shape)
        arr = arr.reshape(128, -1)
        self.off[name] = (self.n, arr.shape[1])
        self.parts.append(arr)
        self.n += arr.shape[1]

    def cat(self):
        return np.ascontiguousarray(np.concatenate(self.parts, axis=1))


EVEN_GROUPS = {}
for j in range(4):
    EVEN_GROUPS[f"fq{j}"] = (128 * j, 128)
    EVEN_GROUPS[f"fk{j}"] = (512 + 128 * j, 128)
EVEN_GROUPS["fv"] = (1024, 512)
EVEN_GROUPS["ff"] = (1536, 8)
for h in range(4):
    EVEN_GROUPS[f"gq{h}"] = (1544 + 128 * h, 128)
    EVEN_GROUPS[f"gk{h}"] = (2056 + 128 * h, 128)
    EVEN_GROUPS[f"gv{h}"] = (2568 + 128 * h, 128)
EVEN_GROUPS["gab"] = (3080, 8)
EVEN_GROUPS["gg"] = (3088, 512)
ODD_GROUPS = {}
for h in range(4):
    ODD_GROUPS[f"hq{h}"] = (128 * h, 128)
    ODD_GROUPS[f"hf{h}"] = (512 + 128 * h, 128)
ODD_GROUPS["hi"] = (1024, 512)
ODD_GROUPS["hg"] = (1536, 512)
ODD_GROUPS["mz"] = (2048, 512)
for c in range(8):
    ODD_GROUPS[f"mx{c}"] = (2560 + 128 * c, 128)
ODD_GROUPS["mdt"] = (3584, 8)


def weight_layout():
    lay = []
    for l in range(2):
        groups = EVEN_GROUPS if l == 0 else ODD_GROUPS
        for name, (c0, n) in groups.items():
            lay.append((f"in{l}_{name}", 8 * n))
        lay.append((f"out{l}", 8 * 1024))
        for j in range(NFF):
            lay.append((f"up{l}_{j}", 8 * 256))
        for hf in range(2):
            lay.append((f"down{l}_{hf}", NFF * 512))
    return lay


def host_pack(inp):
    W = Pack()
    for l in range(2):
        groups = EVEN_GROUPS if l == 0 else ODD_GROUPS
        w_in = inp["even_w_in"] if l == 0 else inp["odd_w_in"]
        for name, (c0, n) in groups.items():
            W.add(f"in{l}_{name}", _kp(w_in[:, c0:c0 + n]))
        W.add(f"out{l}", _kp(inp["w_out"][l]))
        wu = inp["ffn_w_up"][l]
        for j in range(NFF):
            t = np.concatenate([wu[:, j * 128:(j + 1) * 128], wu[:, DFF + j * 128:DFF + (j + 1) * 128]], axis=1)
            W.add(f"up{l}_{j}", _kp(t))
        wd = inp["ffn_w_down"][l]
        wd = wd.reshape(NFF, 128, 1024).transpose(1, 0, 2)
        for hf in range(2):
            W.add(f"down{l}_{hf}", wd[:, :, hf * 512:(hf + 1) * 512])
    Pp = Pack()
    rep = lambda v: np.broadcast_to(np.asarray(v, np.float32).reshape(1, -1), (128, np.asarray(v).size))
    for l in range(2):
        cw = inp["ffn_conv_w"][l]
        Pp.add(f"fcw{l}", np.stack([_pc(cw[k]) for k in range(3)], axis=2))
        Pp.add(f"fcb{l}", _pc(inp["ffn_conv_b"][l]))
    Pp.add("gcw", np.stack([_pc(inp["gdn_conv_w"][k]) for k in range(4)], axis=2))
    Pp.add("mcw", np.stack([_pc(inp["m2_conv_w"][k]) for k in range(4)], axis=2))
    Pp.add("mcb", _pc(inp["m2_conv_b"]))
    Pp.add("lb0", _pc(inp["hgrn_lb_logits"][0]))
    Pp.add("lb1", _pc(inp["hgrn_lb_logits"][1]))
    Pp.add("foxb", np.broadcast_to(inp["fox_f_bias"].reshape(8, 1), (8, 1)).repeat(16, axis=0).reshape(128, 1))
    Pp.add("foxb8", np.concatenate([inp["fox_f_bias"].reshape(8, 1), np.zeros((120, 1), np.float32)], axis=0))
    Pp.add("gAlog", rep(inp["gdn_A_log"]))
    Pp.add("gdtb", rep(inp["gdn_dt_bias"]))
    Pp.add("mAlog", rep(inp["m2_A_log"]))
    Pp.add("mdtb", rep(inp["m2_dt_bias"]))
    Pp.add("mD", rep(inp["m2_D"]))
    Pp.add("gng", rep(inp["gdn_norm_gain"]))
    Pp.add("hng", rep(inp["hgrn_norm_gain"]))
    Pp.add("mng", rep(inp["m2_norm_gain"]))
    G = np.broadcast_to(inp["norm_gains"].reshape(1, 8, 1024), (128, 8, 1024))
    return W, Pp, np.ascontiguousarray(G.reshape(128, 8 * 1024))


class Builder:
    def __init__(self, S, NSEQ, woff, poff, wtot, ptot, layers=(0, 1), mixers=True, dbg=None):
        self.S, self.NSEQ = S, NSEQ
        self.NT = S // 128
        self.NB = S // 512
        self.woff, self.poff = woff, poff
        self.layers = layers
        self.mixers = mixers
        self.dbg = dbg or {}
        self.uid = 0
        nc = self.nc = bass.Bass("TRN2", target_bir_lowering=False)
        self.P = Prog(nc)
        self.x_d = nc.dram_tensor("x", [NSEQ, S, D], F32, kind="ExternalInput").ap()
        self.w_d = nc.dram_tensor("wbig", [128, wtot], F32, kind="ExternalInput").ap()
        self.pp_d = nc.dram_tensor("pp", [128, ptot], F32, kind="ExternalInput").ap()
        self.gn_d = nc.dram_tensor("gains", [128, 8 * 1024], F32, kind="ExternalInput").ap()
        self.y_d = nc.dram_tensor("y", [NSEQ, S, D], F32, kind="ExternalOutput").ap()
        self.wbf = nc.dram_tensor("wbf", [128, wtot], BF16, kind="Internal").ap()
        self.xspill = nc.dram_tensor("xspill", [14, 128, D], F32, kind="Internal").ap()
        self.wtot, self.ptot = wtot, ptot
        self.wres = {name: Res() for name in woff}
        self.wcast_done = set()
        self.out_dmas = []
        self.dbg_out = {}

    def sb(self, st, shape, dt, name="t"):
        self.uid += 1
        return st.enter_context(self.nc.sbuf_tensor(f"{name}_{self.uid}", list(shape), dt))

    def ps(self, st, shape, dt, name="p"):
        self.uid += 1
        return st.enter_context(self.nc.psum_tensor(f"{name}_{self.uid}", list(shape), dt))

    def wseg(self, name):
        o, n = self.woff[name]
        return self.wbf[:, o:o + n], self.wres[name]

    def pcol(self, name):
        o, n = self.poff[name]
        return self.PP[:, o:o + n]

    def dbg_dump(self, key, ap_sbuf, shape, res):
        if key not in self.dbg:
            return
        dt = ap_sbuf.dtype
        t = self.nc.dram_tensor(f"dbg_{key}", list(shape), dt, kind="ExternalOutput").ap()
        self.out_dmas.append(self.P.dma(SP, t, ap_sbuf, reads=res))

    def cast_weights(self, prefixes):
        P = self.P
        for name, (o, n) in self.woff.items():
            if not any(name.startswith(p) for p in prefixes):
                continue
            if name in self.wcast_done:
                continue
            self.wcast_done.add(name)
            r = self.wres[name]
            c = 0
            while c < n:
                m = min(8192, n - c)
                P.dma(POOL, self.wbf[:, o + c:o + c + m], self.w_d[:, o + c:o + c + m], writes=[r], nobar=True)
                c += m

    def rstd_from_ss(self, ss_ap, rs_ap, res_ss, res_rs, n):
        P = self.P
        P.op(ACT, lambda e: e.activation(out=rs_ap, in_=ss_ap, func=AF.Ln, scale=1.0 / n, bias=self.eps_t[:, 0:1]), [res_ss], [res_rs])
        P.op(ACT, lambda e: e.activation(out=rs_ap, in_=rs_ap, func=AF.Exp, scale=-0.5), [res_rs], [res_rs])

    def prenorm_tile(self, st_unused, t, gain_idx, dst_ap, dst_res):
        P = self.P
        X, rX = self.X, self.rX
        k = t % 2
        junk = self.junk_bf[0]
        ss, rss = self.ss[k], self.r_ss[k]
        hn, rhn = self.hn[k], self.r_hn[k]
        pt, rpt = self.pt[k], self.r_pt[k]
        gB, rgB = self.gainB[gain_idx % 2], self.r_gainB[gain_idx % 2]
        P.op(ACT, lambda e: e.activation(out=hn[:], in_=X[:, t, :], func=AF.Square, accum_out=ss[:, 0:1]), [rX[t]], [rss, rhn])
        self.rstd_from_ss(ss[:, 0:1], ss[:, 1:2], rss, rss, D)
        P.op(DVE, lambda e: e.scalar_tensor_tensor(out=hn[:], in0=X[:, t, :], scalar=ss[:, 1:2], in1=gB[:], op0=ALU.mult, op1=ALU.mult), [rX[t], rss, rgB], [rhn])

        def tr(e):
            ins = None
            for c in range(8):
                ins = e.transpose(pt[:, c, :], hn[:, c * 128:(c + 1) * 128], self.ident_bf[:])
            return ins
        P.op(PE, tr, [rhn], [rpt])
        P.op(ACT, lambda e: e.activation(out=dst_ap, in_=pt[:], func=AF.Copy), [rpt], [dst_res])

    def load_gain(self, l, i):
        idx = l * 4 + i
        gB, rgB = self.gainB[idx % 2], self.r_gainB[idx % 2]
        self.P.dma(SP, gB[:], self.gn_d[:, idx * 1024:(idx + 1) * 1024], writes=[rgB])
        return idx

    def postnorm_residual(self, t, pacc, rpacc, gain_idx, junk=None, evac=None):
        P = self.P
        X, rX = self.X, self.rX
        k = t % 2
        ss, rss = self.ss[k], self.r_ss[k]
        tmp, rtmp = self.tmp32[0], self.r_tmp32[0]
        jb, rjb = junk if junk is not None else (tmp, rtmp)
        gB, rgB = self.gainB[gain_idx % 2], self.r_gainB[gain_idx % 2]
        if evac is not None:
            cp, rcp = evac
            P.op(DVE, lambda e: e.tensor_copy(out=cp[:], in_=pacc[:]), [], [rcp, rpacc])
            P.op(ACT, lambda e: e.activation(out=jb[:], in_=cp[:], func=AF.Square, accum_out=ss[:, 0:1]), [rcp], [rss, rjb])
            self.rstd_from_ss(ss[:, 0:1], ss[:, 1:2], rss, rss, D)
            P.op(DVE, lambda e: e.scalar_tensor_tensor(out=tmp[:], in0=cp[:], scalar=ss[:, 1:2], in1=gB[:], op0=ALU.mult, op1=ALU.mult), [rss, rgB, rcp], [rtmp])
            P.op(POOL, lambda e: e.tensor_tensor(out=X[:, t, :], in0=X[:, t, :], in1=tmp[:], op=ALU.add), [rtmp, rX[t]], [rX[t]])
            return
        P.op(ACT, lambda e: e.activation(out=jb[:], in_=pacc[:], func=AF.Square, accum_out=ss[:, 0:1]), [], [rss, rjb, rpacc])
        self.rstd_from_ss(ss[:, 0:1], ss[:, 1:2], rss, rss, D)
        P.op(DVE, lambda e: e.scalar_tensor_tensor(out=tmp[:], in0=pacc[:], scalar=ss[:, 1:2], in1=gB[:], op0=ALU.mult, op1=ALU.mult), [rss, rgB], [rtmp, rpacc])
        P.op(POOL, lambda e: e.tensor_tensor(out=X[:, t, :], in0=X[:, t, :], in1=tmp[:], op=ALU.add), [rtmp, rX[t]], [rX[t]])

    def ffn_phase(self, l):
        P = self.P
        S, NT, NB = self.S, self.NT, self.NB
        with contextlib.ExitStack() as st:
            HTb = self.sb(st, [128, 8, 512], BF16, "HTb"); rHTb = [Res() for _ in range(4)]
            AT = self.sb(st, [128, NFF, 512], BF16, "AT"); rAT = [Res() for _ in range(NFF)]
            Wd = [self.sb(st, [128, NFF, 512], BF16, "Wd") for _ in range(2)]; rWd = [Res(), Res()]
            NWU = 4
            Wu = [self.sb(st, [128, 8, 256], BF16, "Wu") for _ in range(NWU)]; rWu = [Res() for _ in range(NWU)]
            U = [[self.sb(st, [128, 514], F32, "U") for _ in range(2)] for _ in range(2)]
            rU = [[Res(), Res()], [Res(), Res()]]
            Y = [[self.sb(st, [128, 512], F32, "Y") for _ in range(2)] for _ in range(2)]
            rY = [[Res(), Res()], [Res(), Res()]]
            HALO = self.sb(st, [128, 2 * NFF, 2], F32, "HALO"); rH = [Res() for _ in range(2 * NFF)]
            pu = [[self.ps(st, [128, 512], F32, "pu") for _ in range(2)] for _ in range(2)]
            rpu = [[Res(), Res()], [Res(), Res()]]
            pacc = [self.ps(st, [128, 1024], F32, "pacc") for _ in range(1)]; rpacc = [Res()]
            pcp = [self.sb(st, [128, 1024], F32, "pcp") for _ in range(2)]; rpcp = [Res(), Res()]
            cwo, _ = self.poff[f"fcw{l}"]
            cbo, _ = self.poff[f"fcb{l}"]
            PP = self.PP
            g_pre = self.load_gain(l, 2)
            nxt = [x for x in self.layers if x > l]
            if nxt:
                self.cast_weights([f"in{nxt[0]}_"])
            P.op(POOL, lambda e: e.memset(HALO[:], 0.0), [], rH)
            for hf in range(2):
                seg, rs = self.wseg(f"down{l}_{hf}")
                P.dma(SP, Wd[hf][:], seg.rearrange("p (j c) -> p j c", j=NFF), reads=[rs], writes=[rWd[hf]])
            for b in range(NB):
                for tt in range(4):
                    t = b * 4 + tt
                    self.prenorm_tile(None, t, g_pre, HTb[:, :, tt * 128:(tt + 1) * 128], rHTb[tt])
                if b == 0:
                    g_post = self.load_gain(l, 3)
                for j in range(NFF):
                    k = j % 2
                    kw = j % NWU
                    seg, rs = self.wseg(f"up{l}_{j}")
                    P.dma(SP, Wu[kw][:], seg.rearrange("p (k c) -> p k c", k=8), reads=[rs], writes=[rWu[kw]])
                    for gu in range(2):
                        def mm(e, gu=gu, k=k, kw=kw):
                            ins = None
                            for kk in range(8):
                                ins = e.matmul(pu[gu][k][:], lhsT=Wu[kw][:, kk, gu * 128:(gu + 1) * 128], rhs=HTb[:, kk, :],
                                               start=(kk == 0), stop=(kk == 7))
                            return ins
                        P.op(PE, mm, [rWu[kw]] + rHTb, [rpu[gu][k]])
                        u, ru = U[gu][k], rU[gu][k]
                        y, ry = Y[gu][k], rY[gu][k]
                        hidx = gu * NFF + j
                        fidx = gu * NFF + j
                        w0 = PP[:, cwo + fidx * 3 + 0:cwo + fidx * 3 + 1]
                        w1 = PP[:, cwo + fidx * 3 + 1:cwo + fidx * 3 + 2]
                        w2 = PP[:, cwo + fidx * 3 + 2:cwo + fidx * 3 + 3]
                        bb = PP[:, cbo + fidx:cbo + fidx + 1]
                        P.op(POOL, lambda e, u=u, hidx=hidx: e.tensor_copy(out=u[:, 0:2], in_=HALO[:, hidx, :]), [rH[hidx]], [ru])
                        if gu == 0:
                            P.op(ACT, lambda e, u=u, gu=gu, k=k: e.activation(out=u[:, 2:514], in_=pu[gu][k][:], func=AF.Copy), [], [ru, rpu[gu][k]])
                        else:
                            P.op(DVE, lambda e, u=u, gu=gu, k=k: e.tensor_copy(out=u[:, 2:514], in_=pu[gu][k][:]), [], [ru, rpu[gu][k]])
                        P.op(ACT, lambda e, y=y, gu=gu, k=k, w2=w2, bb=bb: e.activation(out=y[:], in_=pu[gu][k][:], func=AF.Identity, scale=w2, bias=bb), [], [ry, rpu[gu][k]])
                        P.op(POOL, lambda e, u=u, hidx=hidx: e.tensor_copy(out=HALO[:, hidx, :], in_=u[:, 512:514]), [ru], [rH[hidx]])
                        P.op(DVE, lambda e, y=y, u=u, w0=w0: e.scalar_tensor_tensor(out=y[:], in0=u[:, 0:512], scalar=w0, in1=y[:], op0=ALU.mult, op1=ALU.add), [ru, ry], [ry])
                        P.op(DVE, lambda e, y=y, u=u, w1=w1: e.scalar_tensor_tensor(out=y[:], in0=u[:, 1:513], scalar=w1, in1=y[:], op0=ALU.mult, op1=ALU.add), [ru, ry], [ry])
                    yg, yu = Y[0][k], Y[1][k]
                    P.op(ACT, lambda e, yg=yg: e.activation(out=yg[:], in_=yg[:], func=AF.Silu), [rY[0][k]], [rY[0][k]])
                    P.op(POOL, lambda e, yg=yg, yu=yu, j=j: e.tensor_tensor(out=AT[:, j, :], in0=yg[:], in1=yu[:], op=ALU.mult), [rY[0][k], rY[1][k]], [rAT[j]])
                for tt in range(4):
                    t = b * 4 + tt
                    JG = 6
                    for hf in range(2):
                        for j0 in range(0, NFF, JG):
                            j1 = min(NFF, j0 + JG)
                            def mmd(e, tt=tt, hf=hf, j0=j0, j1=j1):
                                ins = None
                                for j in range(j0, j1):
                                    ins = e.matmul(pacc[0][:, hf * 512:(hf + 1) * 512], lhsT=AT[:, j, tt * 128:(tt + 1) * 128], rhs=Wd[hf][:, j, :],
                                                   start=(j == 0), stop=(j == NFF - 1))
                                return ins
                            P.op(PE, mmd, rAT[j0:j1] + [rWd[hf]], [rpacc[0]])
                    self.postnorm_residual(t, pacc[0], rpacc[0], g_post, evac=(pcp[t % 2], rpcp[t % 2]))

    def outproj_phase(self, l, MT, rMT, st):
        P = self.P
        Wo = self.sb(st, [128, 8, 1024], BF16, "Wo"); rWo = Res()
        paccs = [self.ps(st, [128, 1024], F32, "pacc_o") for _ in range(2)]; rpaccs = [Res(), Res()]
        seg, rs = self.wseg(f"out{l}")
        P.dma(SP, Wo[:], seg.rearrange("p (k c) -> p k c", k=8), reads=[rs], writes=[rWo])
        g_post = self.load_gain(l, 1)
        for t in range(self.NT):
            pacc, rpacc = paccs[t % 2], rpaccs[t % 2]
            def mm(e, t=t, pacc=pacc):
                ins = None
                for hf in range(2):
                    for kk in range(8):
                        ins = e.matmul(pacc[:, hf * 512:(hf + 1) * 512], lhsT=MT[:, kk, t * 128:(t + 1) * 128], rhs=Wo[:, kk, hf * 512:(hf + 1) * 512],
                                       start=(kk == 0), stop=(kk == 7))
                return ins
            P.op(PE, mm, [rWo, rMT[t]], [rpacc])
            self.postnorm_residual(t, pacc, rpacc, g_post, junk=(self.hn[t % 2], self.r_hn[t % 2]))

    def mixer_phase(self, l):
        P = self.P
        S, NT = self.S, self.NT
        with contextlib.ExitStack() as st:
            HT = self.sb(st, [128, 8, S], BF16, "HT"); rHT = [Res() for _ in range(NT)]
            MT = self.sb(st, [128, 8, S], BF16, "MT"); rMT = [Res() for _ in range(NT)]
            g_pre = self.load_gain(l, 0)
            for t in range(NT):
                self.prenorm_tile(None, t, g_pre, HT[:, :, t * 128:(t + 1) * 128], rHT[t])
            self.dbg_dump(f"HT{l}", HT[:], [128, 8, S], rHT)
            self.cast_weights([f"out{l}", f"up{l}_", f"down{l}_"])
            if not self.mixers:
                for t in range(NT):
                    P.op(DVE, lambda e, t=t: e.tensor_copy(out=MT[:, :, t * 128:(t + 1) * 128], in_=HT[:, :, t * 128:(t + 1) * 128]), [rHT[t]], [rMT[t]])
            elif l == 0:
                only = getattr(self, "only", "both")
                self.gdn_nx = 14 if NT >= 14 else (7 if NT >= 7 else 0)
                self.gdn_xsub = [[] for _ in range(self.gdn_nx)]
                self.gdn_spill_ops = []
                for t in range(self.gdn_nx):
                    self.gdn_spill_ops.append(P.dma(SP, self.xspill[t], self.X[:, t, :], reads=[self.rX[t]]))
                if only != "both":
                    for t in range(NT):
                        P.op(POOL, lambda e, t=t: e.memset(MT[:, :, t * 128:(t + 1) * 128], 0.0), [], [rMT[t]])
                if only in ("both", "first"):
                    with contextlib.ExitStack() as st2:
                        self.fox(HT, rHT, MT, rMT, st2)
                    P.barrier(self.bar_scratch[:])
                if only in ("both", "second"):
                    with contextlib.ExitStack() as st2:
                        self.gdn(HT, rHT, MT, rMT, st2)
                    for t in range(self.gdn_nx):
                        P.dma(SP, self.X[:, t, :], self.xspill[t], reads=[], writes=[self.rX[t]] + self.gdn_xsub[t])
                    P.barrier(self.bar_scratch[:])
            else:
                only = getattr(self, "only", "both")
                if only != "both":
                    for t in range(NT):
                        P.op(POOL, lambda e, t=t: e.memset(MT[:, :, t * 128:(t + 1) * 128], 0.0), [], [rMT[t]])
                if only in ("both", "first"):
                    nxt = (NT + 3) // 4
                    self.hg_xsub = [[] for _ in range(nxt)]
                    self.hg_spill_ops = [P.dma(SP, self.xspill[t], self.X[:, t, :], reads=[self.rX[t]]) for t in range(nxt)]
                    with contextlib.ExitStack() as st2:
                        self.hgrn(HT, rHT, MT, rMT, st2)
                    for t in range(nxt):
                        P.dma(SP, self.X[:, t, :], self.xspill[t], reads=[], writes=[self.rX[t]] + self.hg_xsub[t])
                    P.barrier(self.bar_scratch[:])
                if only in ("both", "second"):
                    with contextlib.ExitStack() as st2:
                        self.ssd(HT, rHT, MT, rMT, st2)
                    P.barrier(self.bar_scratch[:])
            self.dbg_dump(f"MT{l}", MT[:], [128, 8, S], rMT)
            with contextlib.ExitStack() as st2:
                self.outproj_phase(l, MT, rMT, st2)


    def build(self):
        P = self.P
        nc = self.nc
        S, NT = self.S, self.NT
        with contextlib.ExitStack() as st:
            self.X = self.sb(st, [128, NT, D], F32, "X"); self.rX = [Res() for _ in range(NT)]
            self.PP = self.sb(st, [128, self.ptot], F32, "PP"); self.rPP = Res()
            self.gainB = [self.sb(st, [128, D], F32, "gainB") for _ in range(2)]; self.r_gainB = [Res(), Res()]
            self.junk_bf = [self.sb(st, [128, D], BF16, "junk") for _ in range(1)]
            self.hn = [self.sb(st, [128, D], BF16, "hn") for _ in range(2)]; self.r_hn = [Res(), Res()]
            self.tmp32 = [self.sb(st, [128, D], F32, "tmp32") for _ in range(1)]; self.r_tmp32 = [Res()]
            self.ss = [self.sb(st, [128, 2], F32, "ss") for _ in range(2)]; self.r_ss = [Res(), Res()]
            self.ident_bf = self.sb(st, [128, 128], BF16, "ident"); r_id = Res()
            self.ident_f = self.sb(st, [128, 128], F32, "identf")
            self.ones_f = self.sb(st, [128, 128], F32, "onesf")
            self.ones_bf = self.sb(st, [128, 128], BF16, "onesbf")
            self.tri_incl = self.sb(st, [128, 128], F32, "tri_incl")
            self.tri_incl_bf = self.sb(st, [128, 128], BF16, "tri_incl_bf")
            self.neg_incl = self.sb(st, [128, 128], F32, "neg_incl")
            self.neg_strict = self.sb(st, [128, 128], F32, "neg_strict")
            self.eps_t = self.sb(st, [128, 1], F32, "eps")
            self.bar_scratch = self.sb(st, [128, 1], F32, "bar")
            self.pt = [self.ps(st, [128, 8, 128], BF16, "pt") for _ in range(2)]; self.r_pt = [Res(), Res()]
            rc = self.rconst = Res()
            P.op(POOL, lambda e: e.memset(self.ident_bf[:], 0.0), [], [rc], nobar=True)
            P.op(POOL, lambda e: e.affine_select(out=self.ident_bf[:], in_=self.ident_bf[:], pattern=[[-1, 128]], compare_op=ALU.not_equal, fill=1.0, base=0, channel_multiplier=1), [rc], [rc], nobar=True)
            P.op(POOL, lambda e: e.memset(self.ident_f[:], 0.0), [rc], [rc], nobar=True)
            P.op(POOL, lambda e: e.affine_select(out=self.ident_f[:], in_=self.ident_f[:], pattern=[[-1, 128]], compare_op=ALU.not_equal, fill=1.0, base=0, channel_multiplier=1), [rc], [rc], nobar=True)
            P.op(POOL, lambda e: e.memset(self.ones_f[:], 1.0), [rc], [rc], nobar=True)
            P.op(POOL, lambda e: e.memset(self.ones_bf[:], 1.0), [rc], [rc], nobar=True)
            P.op(POOL, lambda e: e.memset(self.tri_incl[:], 1.0), [rc], [rc], nobar=True)
            P.op(POOL, lambda e: e.affine_select(out=self.tri_incl[:], in_=self.tri_incl[:], pattern=[[1, 128]], compare_op=ALU.is_ge, fill=0.0, base=0, channel_multiplier=-1), [rc], [rc], nobar=True)
            P.op(POOL, lambda e: e.tensor_copy(out=self.tri_incl_bf[:], in_=self.tri_incl[:]), [rc], [rc], nobar=True)
            P.op(POOL, lambda e: e.memset(self.neg_incl[:], 0.0), [rc], [rc], nobar=True)
            P.op(POOL, lambda e: e.affine_select(out=self.neg_incl[:], in_=self.neg_incl[:], pattern=[[1, 128]], compare_op=ALU.is_ge, fill=-30000.0, base=0, channel_multiplier=-1), [rc], [rc], nobar=True)
            P.op(POOL, lambda e: e.memset(self.neg_strict[:], 0.0), [rc], [rc], nobar=True)
            P.op(POOL, lambda e: e.affine_select(out=self.neg_strict[:], in_=self.neg_strict[:], pattern=[[1, 128]], compare_op=ALU.is_gt, fill=-30000.0, base=0, channel_multiplier=-1), [rc], [rc], nobar=True)
            P.op(POOL, lambda e: e.memset(self.eps_t[:], EPS), [rc], [rc], nobar=True)
            P.dma(SP, self.PP[:], self.pp_d, writes=[self.rPP])
            P.barrier(self.bar_scratch[:])
            self.cast_weights([f"in{self.layers[0]}_"])
            for s in range(self.NSEQ):
                for t in range(NT):
                    P.dma(SP, self.X[:, t, :], self.x_d[s, t * 128:(t + 1) * 128, :], writes=[self.rX[t]])
                for l in self.layers:
                    self.mixer_phase(l)
                    P.barrier(self.bar_scratch[:])
                    self.ffn_phase(l)
                    P.barrier(self.bar_scratch[:])
                for t in range(NT):
                    self.out_dmas.append(P.dma(SP, self.y_d[s, t * 128:(t + 1) * 128, :], self.X[:, t, :], reads=[self.rX[t]]))
            P.emit(final_wait_ops=self.out_dmas)
        return nc


def make_builder(S, NSEQ, layers=(0, 1), mixers=True, dbg=None, poff=None, ptot=None):
    woff = {}
    n = 0
    for name, sz in weight_layout():
        woff[name] = (n, sz)
        n += sz
    return Builder(S, NSEQ, woff, poff, n, ptot, layers=layers, mixers=mixers, dbg=dbg)


def kernel(**inputs):
    inp = {k: np.asarray(v) for k, v in inputs.items()}
    W, Pp, G = host_pack(inp)
    wbig = W.cat()
    pp = Pp.cat()
    B, S, _ = inp["x"].shape
    nseq = B // NCORES
    b = Builder(S, nseq, W.off, Pp.off, W.n, Pp.n)
    nc = b.build()
    x = np.ascontiguousarray(inp["x"], dtype=np.float32)
    in_maps = [{"x": x[c * nseq:(c + 1) * nseq], "wbig": wbig, "pp": pp, "gains": G} for c in range(NCORES)]
    res = run_bass_kernel_spmd(nc, in_maps, core_ids=list(range(NCORES)))
    out = np.concatenate([r["y"] for r in res.results], axis=0)
    return out.astype(np.float32)


def _fox(self, HT, rHT, MT, rMT, st):
    P = self.P
    S, NT, NB = self.S, self.NT, self.NB
    sb, ps = self.sb, self.ps
    Vaug = sb(st, [128, NT, 2, 128], BF16, "Vaug"); rV = [Res() for _ in range(NT)]
    qz = [sb(st, [128, S], BF16, "qz") for _ in range(2)]; rqz = [[Res() for _ in range(NB)] for _ in range(2)]
    kT = sb(st, [128, S], BF16, "kT"); rk = [Res() for _ in range(NB)]
    Wq = sb(st, [128, 8, 128], BF16, "Wq"); rWq = Res()
    Wk = sb(st, [128, 8, 128], BF16, "Wk"); rWk = Res()
    Wv = sb(st, [128, 8, 128], BF16, "Wv"); rWv = Res()
    Wf = sb(st, [128, 8, 8], BF16, "Wf"); rWf = Res()
    ebuf = sb(st, [8, 512], F32, "ebuf"); reb = Res()
    CTb = [sb(st, [8, 512], F32, "CTb") for _ in range(2)]; rCT = [Res(), Res()]
    CrefB = sb(st, [8, NB, 128], F32, "CrefB"); rCr = Res()
    negb = sb(st, [8, 1], F32, "negb"); rnegb = Res()
    Ctok = sb(st, [128, NT, 8], F32, "Ctok"); rCtok = Res()
    nb = sb(st, [128, NB, NT, 8], F32, "nb"); rnb = Res()
    PT = [sb(st, [128, 512], BF16, "PT") for _ in range(3)]; rPT = [Res() for _ in range(3)]
    rec = sb(st, [64, 512], F32, "rec"); rrec = Res()
    pS = [ps(st, [128, 512], F32, "pS") for _ in range(2)]; rpS = [Res(), Res()]
    po = [ps(st, [128, 512], F32, "po") for _ in range(2)]; rpo = [Res(), Res()]
    pm = ps(st, [128, 512], F32, "pm"); rpm = Res()
    pC = ps(st, [128, NT * 8 + NB * 8], F32, "pC"); rpC = Res()
    PP = self.PP
    fo, _ = self.poff["foxb8"]

    seg, rs = self.wseg("in0_ff")
    P.dma(SP, Wf[:], seg.rearrange("p (k c) -> p k c", k=8), reads=[rs], writes=[rWf])
    P.op(DVE, lambda e: e.tensor_scalar(out=negb[:], in0=PP[0:8, fo:fo + 1], scalar1=-1.0, scalar2=None, op0=ALU.mult), [self.rPP], [rnegb])
    P.op(POOL, lambda e: e.memset(Vaug[:, :, :, 64:128], 1.0), [], rV)
    for hh_ in range(2):
        P.op(POOL, lambda e, hh_=hh_: e.memset(qz[hh_][:], 0.0), [], rqz[hh_])
    for b in range(NB):
        def mmf(e, b=b):
            ins = None
            for kk in range(8):
                ins = e.matmul(pm[0:8, :], lhsT=Wf[:, kk, :], rhs=HT[:, kk, b * 512:(b + 1) * 512], start=(kk == 0), stop=(kk == 7))
            return ins
        P.op(PE, mmf, [rWf] + rHT[b * 4:(b + 1) * 4], [rpm])
        P.op(ACT, lambda e: e.activation(out=ebuf[:], in_=pm[0:8, :], func=AF.Exp, bias=negb[:, 0:1], scale=-1.0), [rpm, rnegb], [reb])
        P.op(ACT, lambda e: e.activation(out=ebuf[:], in_=ebuf[:], func=AF.Ln, bias=1.0), [reb], [reb])
        cur, prev = CTb[b % 2], CTb[(b + 1) % 2]
        rcur, rprev = rCT[b % 2], rCT[(b + 1) % 2]
        P.op(POOL, lambda e, cur=cur: e.memset(cur[:], 1.0), [], [rcur])
        if b == 0:
            P.op(DVE, lambda e, cur=cur: e.tensor_tensor_scan(out=cur[:], data0=cur[:], data1=ebuf[:], initial=0.0, op0=ALU.mult, op1=ALU.add), [reb, rcur], [rcur])
        else:
            P.op(DVE, lambda e, cur=cur, prev=prev: e.tensor_tensor_scan(out=cur[:], data0=cur[:], data1=ebuf[:], initial=prev[:, 511:512], op0=ALU.mult, op1=ALU.add), [reb, rcur, rprev], [rcur])
        P.op(DVE, lambda e, cur=cur, b=b: e.tensor_copy(out=CrefB[:, b, :], in_=cur[:, 256:257].to_broadcast([8, 128])), [rcur], [rCr])
        def mmc(e, cur=cur, b=b):
            ins = None
            for tt in range(4):
                t = b * 4 + tt
                ins = e.matmul(pC[:, t * 8:(t + 1) * 8], lhsT=cur[0:8, tt * 128:(tt + 1) * 128], rhs=self.ident_f[0:8, 0:8], start=True, stop=True)
            ins = e.matmul(pC[:, NT * 8 + b * 8:NT * 8 + (b + 1) * 8], lhsT=CrefB[0:8, b, :], rhs=self.ident_f[0:8, 0:8], start=True, stop=True)
            return ins
        P.op(PE, mmc, [rcur, rCr, self.rconst], [rpC])
    P.op(ACT, lambda e: e.activation(out=Ctok[:], in_=pC[:, 0:NT * 8].rearrange("p (t h) -> p t h", h=8), func=AF.Copy), [rpC], [rCtok])
    for i in range(NB):
        P.op(DVE, lambda e, i=i: e.tensor_tensor(out=nb[:, i, :, :], in0=Ctok[:], in1=pC[:, NT * 8 + i * 8:NT * 8 + (i + 1) * 8].unsqueeze(1).to_broadcast([128, NT, 8]), op=ALU.subtract), [rCtok, rpC], [rnb])
    self.dbg_dump("foxC", Ctok[:], [128, NT, 8], [rCtok])

    for pr in range(4):
        seg, rs = self.wseg(f"in0_fq{pr}")
        P.dma(SP, Wq[:], seg.rearrange("p (k c) -> p k c", k=8), reads=[rs], writes=[rWq])
        seg, rs = self.wseg(f"in0_fk{pr}")
        P.dma(SP, Wk[:], seg.rearrange("p (k c) -> p k c", k=8), reads=[rs], writes=[rWk])
        seg, rs = self.wseg("in0_fv")
        P.dma(SP, Wv[:], seg.rearrange("p (k c) -> p k c", k=8)[:, :, pr * 128:(pr + 1) * 128], reads=[rs], writes=[rWv])
        for b in range(NB):
            for which in ("q", "k"):
                W_, rW_ = (Wq, rWq) if which == "q" else (Wk, rWk)
                pb, rpb = pS[b % 2], rpS[b % 2]
                def mmp(e, W_=W_, pb=pb, b=b):
                    ins = None
                    for kk in range(8):
                        ins = e.matmul(pb[:], lhsT=W_[:, kk, :], rhs=HT[:, kk, b * 512:(b + 1) * 512], start=(kk == 0), stop=(kk == 7))
                    return ins
                P.op(PE, mmp, [rW_] + rHT[b * 4:(b + 1) * 4], [rpb])
                if which == "q":
                    P.op(ACT, lambda e, pb=pb, b=b: e.activation(out=qz[0][0:64, b * 512:(b + 1) * 512], in_=pb[0:64, :], func=AF.Copy), [], [rqz[0][b], rpb])
                    P.op(ACT, lambda e, pb=pb, b=b: e.activation(out=qz[1][64:128, b * 512:(b + 1) * 512], in_=pb[64:128, :], func=AF.Copy), [], [rqz[1][b], rpb])
                else:
                    P.op(DVE, lambda e, pb=pb, b=b: e.tensor_copy(out=kT[:, b * 512:(b + 1) * 512], in_=pb[:]), [], [rk[b], rpb])
        for t in range(NT):
            def mmv(e, t=t):
                ins = None
                for kk in range(8):
                    ins = e.matmul(pm[:, 0:128], lhsT=HT[:, kk, t * 128:(t + 1) * 128], rhs=Wv[:, kk, :], start=(kk == 0), stop=(kk == 7))
                return ins
            P.op(PE, mmv, [rWv, rHT[t]], [rpm])
            P.op(DVE, lambda e, t=t: e.tensor_copy(out=Vaug[:, t, :, 0:64], in_=pm[:, 0:128].rearrange("p (a b) -> p a b", a=2)), [rpm], [rV[t]])
        for hh in range(2):
            h = 2 * pr + hh
            base = hh * 64
            cnt = 0
            for i in range(NB):
                njb = 4 * i + 4
                pacc, rpacc = po[i % 2], rpo[i % 2]
                items = []
                for j in range(njb):
                    r = j - 4 * i
                    c0 = 128 * r if r > 0 else 0
                    items.append((j, r, c0))

                def emit_qk(j, r, c0, slot, i=i, base=base, h=h, hh=hh):
                    pb, rpb = pS[slot % 2], rpS[slot % 2]
                    pt_, rpt_ = PT[slot % 3], rPT[slot % 3]
                    P.op(PE, lambda e: e.matmul(pb[:, c0:512], lhsT=kT[:, j * 128:(j + 1) * 128],
                                                rhs=qz[hh][:, i * 512 + c0:(i + 1) * 512], start=True, stop=True),
                         [rk[j // 4], rqz[hh][i]], [rpb])
                    P.op(ACT, lambda e: e.activation(out=pt_[:, c0:512], in_=pb[:, c0:512], func=AF.Exp, bias=nb[:, i, j, h:h + 1], scale=0.125),
                         [rpb, rnb], [rpt_])
                    if r >= 0:
                        P.op(POOL, lambda e: e.tensor_tensor(out=pt_[:, c0:c0 + 128], in0=pt_[:, c0:c0 + 128], in1=self.tri_incl_bf[:], op=ALU.mult),
                             [rpt_, self.rconst], [rpt_])

                def emit_pv(j, r, c0, slot, first, last, pacc=pacc, rpacc=rpacc, hh=hh):
                    pt_, rpt_ = PT[slot % 3], rPT[slot % 3]
                    P.op(PE, lambda e: e.matmul(pacc[:, c0:512], lhsT=Vaug[:, j, hh, :], rhs=pt_[:, c0:512], start=first, stop=last),
                         [rV[j], rpt_], [rpacc])

                for idx, (j, r, c0) in enumerate(items):
                    emit_qk(j, r, c0, cnt + idx)
                    if idx >= 1:
                        pj, pr_, pc0 = items[idx - 1]
                        emit_pv(pj, pr_, pc0, cnt + idx - 1, idx - 1 == 0, False)
                pj, pr_, pc0 = items[-1]
                emit_pv(pj, pr_, pc0, cnt + len(items) - 1, len(items) == 1, True)
                cnt += len(items)
                P.op(DVE, lambda e, pacc=pacc: e.reciprocal(out=rec[:], in_=pacc[64:128, :]), [rpacc], [rrec])
                P.op(DVE, lambda e, pacc=pacc, i=i, base=base, pr=pr: e.tensor_tensor(out=MT[base:base + 64, pr, i * 512:(i + 1) * 512], in0=pacc[0:64, :], in1=rec[:], op=ALU.mult),
                     [rpacc, rrec], rMT[i * 4:(i + 1) * 4])


Builder.fox = _fox


def _proj_fm(self, W_, rW_, HT, rHT, b, pb, rpb):
    def mm(e):
        ins = None
        for kk in range(8):
            ins = e.matmul(pb[:], lhsT=W_[:, kk, :], rhs=HT[:, kk, b * 512:(b + 1) * 512], start=(kk == 0), stop=(kk == 7))
        return ins
    self.P.op(PE, mm, [rW_] + rHT[b * 4:(b + 1) * 4], [rpb])


def _proj_tm(self, W_ap, rW_, HT, rHT, t, pb_ap, rpb):
    def mm(e):
        ins = None
        for kk in range(8):
            ins = e.matmul(pb_ap, lhsT=HT[:, kk, t * 128:(t + 1) * 128], rhs=W_ap[:, kk, :], start=(kk == 0), stop=(kk == 7))
        return ins
    self.P.op(PE, mm, [rW_, rHT[t]], [rpb])


def _load_w(self, name, W_, rW_, cols=None, ncol=None):
    seg, rs = self.wseg(name)
    n = self.woff[name][1] // 8
    src = seg.rearrange("p (k c) -> p k c", k=8)
    if cols is not None:
        src = src[:, :, cols:cols + ncol]
    self.P.dma(SP, W_[:], src, reads=[rs], writes=[rW_])


Builder.proj_fm = _proj_fm
Builder.proj_tm = _proj_tm
Builder.load_w = _load_w


def _hgrn(self, HT, rHT, MT, rMT, st):
    P = self.P
    S, NT, NB = self.S, self.NT, self.NB
    sb, ps = self.sb, self.ps
    PP = self.PP
    Wq = sb(st, [128, 8, 128], BF16, "Wq"); rWq = Res()
    Wf = sb(st, [128, 8, 128], BF16, "Wf"); rWf = Res()
    Wg = sb(st, [128, 8, 128], BF16, "Wg"); rWg = Res()
    Wi = sb(st, [128, 8, 512], BF16, "Wi"); rWi = Res()
    vtok = sb(st, [128, NT, 512], BF16, "vtok"); rvt = [Res() for _ in range(NT)]
    qtT = sb(st, [128, S], BF16, "qtT"); rqt = [Res() for _ in range(NB)]
    ktT = sb(st, [128, S], BF16, "ktT"); rkt = [Res() for _ in range(NB)]
    qhT = sb(st, [128, S], BF16, "qhT"); rqh = [Res() for _ in range(NB)]
    class _V:
        def __init__(self, ap):
            self.ap = ap
        def __getitem__(self, k):
            return self.ap if (isinstance(k, slice) and k == slice(None)) else self.ap[k]
    A1 = _V(self.tmp32[0][:, 0:512]); rA1 = self.r_tmp32[0]
    A2 = _V(self.tmp32[0][:, 512:1024]); rA2 = self.r_tmp32[0]
    A3 = _V(self.gainB[0][:, 0:512]); rA3 = self.r_gainB[0]
    A4 = _V(self.gainB[0][:, 512:1024]); rA4 = self.r_gainB[0]
    E1 = _V(self.gainB[1][:, 0:512]); rE1 = self.r_gainB[1]
    E2 = _V(self.gainB[1][:, 512:1024]); rE2 = self.r_gainB[1]
    QS = sb(st, [128, 512], F32, "QS"); rQS = Res()
    RESET = sb(st, [128, 512], F32, "RESET"); rRS = Res()
    SC = sb(st, [128, NT, 3], F32, "SC"); rSC = Res()
    lbt = sb(st, [128, 4, 4], F32, "lbt"); rlb = Res()
    ktok = [sb(st, [128, 128], BF16, "ktok") for _ in range(2)]; rktok = [Res(), Res()]
    ATm = [sb(st, [128, 128], BF16, "ATm") for _ in range(2)]; rATm = [Res(), Res()]
    Sst = sb(st, [128, 128], F32, "Sst"); rS = Res()
    Sbf = sb(st, [128, 128], BF16, "Sbf"); rSbf = Res()
    yb = [sb(st, [128, 128], F32, "yb") for _ in range(2)]; ryb = [Res(), Res()]
    sg = [sb(st, [128, 128], F32, "sg") for _ in range(2)]; rsg = [Res(), Res()]
    ym = [sb(st, [128, 128], BF16, "ym") for _ in range(2)]; rym = [Res(), Res()]
    ssn = [sb(st, [128, 2], F32, "ssn") for _ in range(2)]; rssn = [Res(), Res()]
    pA = [ps(st, [128, 512], F32, "pA") for _ in range(2)]; rpA = [Res(), Res()]
    pv = ps(st, [128, 512], F32, "pv"); rpv = Res()
    pG = ps(st, [128, 512], F32, "pG"); rpG = Res()
    pKl = [pA[0][:, 0:128], pA[1][:, 0:128], pv[:, 0:128], pG[:, 0:128]]
    rpK = [rpA[0], rpA[1], rpv, rpG]
    l0o, _ = self.poff["lb0"]; l1o, _ = self.poff["lb1"]
    hgo, _ = self.poff["hng"]

    P.op(DVE, lambda e: e.tensor_tensor(out=lbt[:, :, 0], in0=PP[:, l1o:l1o + 4], in1=PP[:, l0o:l0o + 4], op=ALU.subtract), [self.rPP], [rlb])
    P.op(ACT, lambda e: e.activation(out=lbt[:, :, 0], in_=lbt[:, :, 0], func=AF.Exp, scale=-1.0), [rlb], [rlb])
    P.op(DVE, lambda e: e.tensor_scalar(out=lbt[:, :, 0], in0=lbt[:, :, 0], scalar1=1.0, scalar2=None, op0=ALU.add), [rlb], [rlb])
    P.op(DVE, lambda e: e.reciprocal(out=lbt[:, :, 0], in_=lbt[:, :, 0]), [rlb], [rlb])
    P.op(DVE, lambda e: e.tensor_scalar(out=lbt[:, :, 1], in0=lbt[:, :, 0], scalar1=-1.0, scalar2=1.0, op0=ALU.mult, op1=ALU.add), [rlb], [rlb])
    P.op(DVE, lambda e: e.tensor_scalar(out=lbt[:, :, 3], in0=lbt[:, :, 1], scalar1=0.5, scalar2=None, op0=ALU.mult), [rlb], [rlb])
    P.op(DVE, lambda e: e.tensor_tensor(out=lbt[:, :, 2], in0=lbt[:, :, 0], in1=lbt[:, :, 3], op=ALU.add), [rlb], [rlb])
    P.op(POOL, lambda e: e.memset(RESET[:], 1.0), [], [rRS])
    P.op(POOL, lambda e: e.memset(RESET[:].rearrange("p (c t) -> p c t", c=4)[:, :, 0:1], 0.0), [rRS], [rRS])
    self.load_w("in1_hi", Wi, rWi)
    for t in range(NT):
        self.proj_tm(Wi, rWi, HT, rHT, t, pv[:], rpv)
        P.op(ACT, lambda e, t=t: e.activation(out=vtok[:, t, :], in_=pv[:], func=AF.Copy), [rpv], [rvt[t]])

    X = self.X
    nxt = (NT + 3) // 4
    sgs_res = []
    for xt in range(nxt):
        r = Res(); self.hg_xsub[xt].append(r); r.rd.append(self.hg_spill_ops[xt]); sgs_res.append(r)

    def sgs(t):
        return X[:, t // 4, :].bitcast(BF16)[:, (t % 4) * 512:(t % 4 + 1) * 512]
    self.load_w("in1_hg", Wi, rWi)
    gts = [(_V(self.gainB[0][:, 0:512]), self.r_gainB[0]), (_V(self.gainB[1][:, 0:512]), self.r_gainB[1])]
    pgs = [(pv, rpv), (pG, rpG)]
    for t in range(NT):
        gtmp, rgt = gts[t % 2]
        pp_, rpp_ = pgs[t % 2]
        self.proj_tm(Wi, rWi, HT, rHT, t, pp_[:], rpp_)
        P.op(ACT, lambda e, gtmp=gtmp, pp_=pp_: e.activation(out=gtmp[:], in_=pp_[:], func=AF.Exp, scale=-1.0), [], [rgt, rpp_])
        P.op(ACT, lambda e, gtmp=gtmp: e.activation(out=gtmp[:], in_=gtmp[:], func=AF.Ln, bias=1.0), [rgt], [rgt])
        P.op(ACT, lambda e, gtmp=gtmp: e.activation(out=gtmp[:], in_=gtmp[:], func=AF.Exp, scale=-1.0), [rgt], [rgt])
        P.op(DVE, lambda e, t=t, gtmp=gtmp, pp_=pp_: e.tensor_tensor(out=sgs(t), in0=pp_[:], in1=gtmp[:], op=ALU.mult), [rgt], [sgs_res[t // 4], rpp_])

    def head_block(h, b):
        lb = lbt[:, h, 0:1]; oml = lbt[:, h, 1:2]
        self.proj_fm(Wf, rWf, HT, rHT, b, pA[0], rpA[0])
        self.proj_fm(Wq, rWq, HT, rHT, b, pA[1], rpA[1])
        P.op(ACT, lambda e: e.activation(out=A1[:], in_=pA[0][:], func=AF.Tanh, scale=0.5), [], [rA1, rpA[0]])
        P.op(ACT, lambda e: e.activation(out=QS[:], in_=pA[1][:], func=AF.Silu), [], [rQS, rpA[1]])
        P.op(DVE, lambda e: e.tensor_scalar(out=A1[:], in0=A1[:], scalar1=lbt[:, h, 3:4], scalar2=lbt[:, h, 2:3], op0=ALU.mult, op1=ALU.add), [rA1, rlb], [rA1])
        P.op(POOL, lambda e: e.tensor_scalar(out=A2[:], in0=A1[:], scalar1=-1.0, scalar2=1.0, op0=ALU.mult, op1=ALU.add), [rA1], [rA2])
        P.op(ACT, lambda e: e.activation(out=A1[:], in_=A1[:], func=AF.Ln), [rA1, rA2], [rA1])
        P.op(DVE, lambda e: e.tensor_tensor_scan(out=A3[:], data0=RESET[:], data1=A1[:], initial=0.0, op0=ALU.mult, op1=ALU.add), [rA1, rRS], [rA3])
        A3v = A3[:].rearrange("p (c t) -> p c t", c=4)
        A4v = A4[:].rearrange("p (c t) -> p c t", c=4)
        P.op(DVE, lambda e: e.tensor_tensor(out=A4v, in0=A3v, in1=A3v[:, :, 63:64].to_broadcast([128, 4, 128]), op=ALU.subtract), [rA3], [rA4])
        P.op(ACT, lambda e: e.activation(out=E1[:], in_=A4[:], func=AF.Exp), [rA4], [rE1])
        P.op(ACT, lambda e: e.activation(out=E2[:], in_=A4[:], func=AF.Exp, scale=-1.0), [rA4], [rE2])
        P.op(ACT, lambda e: e.activation(out=SC[:, b * 4:(b + 1) * 4, 0], in_=A3v[:, :, 127], func=AF.Exp), [rA3], [rSC])
        P.op(ACT, lambda e: e.activation(out=SC[:, b * 4:(b + 1) * 4, 1], in_=A4v[:, :, 127], func=AF.Exp), [rA4], [rSC])
        P.op(ACT, lambda e: e.activation(out=SC[:, b * 4:(b + 1) * 4, 2], in_=A3v[:, :, 63], func=AF.Exp), [rA3], [rSC])
        P.op(DVE, lambda e: e.tensor_tensor(out=qtT[:, b * 512:(b + 1) * 512], in0=QS[:], in1=E1[:], op=ALU.mult), [rQS, rE1], [rqt[b]])
        P.op(DVE, lambda e: e.tensor_tensor(out=ktT[:, b * 512:(b + 1) * 512], in0=A2[:], in1=E2[:], op=ALU.mult), [rA2, rE2], [rkt[b]])
        P.op(POOL, lambda e: e.tensor_tensor(out=qhT[:, b * 512:(b + 1) * 512].rearrange("p (c t) -> p c t", c=4),
                                             in0=qtT[:, b * 512:(b + 1) * 512].rearrange("p (c t) -> p c t", c=4),
                                             in1=SC[:, b * 4:(b + 1) * 4, 2:3].to_broadcast([128, 4, 128]), op=ALU.mult), [rqt[b], rSC], [rqh[b]])

    def head_chunk(h, n):
        k = n % 2
        b = n // 4
        cs = slice(n * 128, (n + 1) * 128)
        pt, rpt = self.pt[k], self.r_pt[k]
        P.op(PE, lambda e: e.transpose(pt[:, 0, :], ktT[:, cs], self.ident_bf[:]), [rkt[b]], [rpt])
        P.op(DVE, lambda e: e.tensor_copy(out=ktok[k][:], in_=pt[:, 0, :]), [rpt], [rktok[k]])
        P.op(PE, lambda e: e.matmul(pKl[0], lhsT=ktT[:, cs], rhs=qtT[:, cs], start=True, stop=True), [rkt[b], rqt[b]], [rpK[0]])
        P.op(DVE, lambda e: e.tensor_tensor(out=ATm[k][:], in0=pKl[0], in1=self.tri_incl[:], op=ALU.mult), [self.rconst], [rATm[k], rpK[0]])
        P.op(PE, lambda e: e.matmul(pKl[1], lhsT=ktok[k][:], rhs=vtok[:, n, h * 128:(h + 1) * 128], start=True, stop=True), [rktok[k], rvt[n]], [rpK[1]])
        def mmo(e):
            ins = None
            if n > 0:
                ins = e.matmul(pKl[2], lhsT=qhT[:, cs], rhs=Sbf[:], start=True, stop=False)
            ins = e.matmul(pKl[2], lhsT=ATm[k][:], rhs=vtok[:, n, h * 128:(h + 1) * 128], start=(n == 0), stop=True)
            return ins
        P.op(PE, mmo, [rqh[b], rSbf, rATm[k], rvt[n]], [rpK[2]])
        if n == 0:
            P.op(DVE, lambda e: e.tensor_scalar(out=Sst[:], in0=pKl[1], scalar1=SC[:, n, 1:2], scalar2=None, op0=ALU.mult), [rSC], [rS, rpK[1]])
        else:
            P.op(DVE, lambda e: e.tensor_scalar(out=Sst[:], in0=Sst[:], scalar1=SC[:, n, 0:1], scalar2=None, op0=ALU.mult), [rS, rSC], [rS])
            P.op(DVE, lambda e: e.scalar_tensor_tensor(out=Sst[:], in0=pKl[1], scalar=SC[:, n, 1:2], in1=Sst[:], op0=ALU.mult, op1=ALU.add), [rSC, rS], [rS, rpK[1]])
        if n < NT - 1:
            P.op(ACT, lambda e: e.activation(out=Sbf[:], in_=Sst[:], func=AF.Copy), [rS], [rSbf])
        P.op(ACT, lambda e: e.activation(out=yb[k][:], in_=pKl[2], func=AF.Square, accum_out=ssn[k][:, 0:1]), [], [ryb[k], rssn[k], rpK[2]])
        self.rstd_from_ss(ssn[k][:, 0:1], ssn[k][:, 1:2], rssn[k], rssn[k], 128)
        P.op(DVE, lambda e: e.scalar_tensor_tensor(out=yb[k][:], in0=pKl[2], scalar=ssn[k][:, 1:2], in1=PP[:, hgo:hgo + 128], op0=ALU.mult, op1=ALU.mult), [rssn[k], ryb[k]], [ryb[k], rpK[2]])
        P.op(POOL, lambda e: e.tensor_tensor(out=ym[k][:], in0=yb[k][:], in1=sgs(n)[:, h * 128:(h + 1) * 128], op=ALU.mult), [ryb[k], sgs_res[n // 4]], [rym[k]])
        P.op(PE, lambda e: e.transpose(pt[:, 1, :], ym[k][:], self.ident_bf[:]), [rym[k]], [rpt])
        P.op(ACT, lambda e: e.activation(out=MT[:, h, cs], in_=pt[:, 1, :], func=AF.Copy), [rpt], [rMT[n]])

    for h in range(4):
        self.load_w(f"in1_hq{h}", Wq, rWq)
        self.load_w(f"in1_hf{h}", Wf, rWf)
        for b in range(NB):
            head_block(h, b)
        for n in range(NT):
            head_chunk(h, n)


Builder.hgrn = _hgrn


def _ssd(self, HT, rHT, MT, rMT, st):
    P = self.P
    S, NT, NB = self.S, self.NT, self.NB
    sb, ps = self.sb, self.ps
    PP = self.PP
    XT = sb(st, [128, 8, 512], BF16, "XT"); rXT = [Res() for _ in range(8)]
    U = [sb(st, [128, 515], F32, "U") for _ in range(2)]; rU = [Res(), Res()]
    Y = [sb(st, [128, 512], F32, "Y") for _ in range(2)]; rY = [Res(), Res()]
    HALO = sb(st, [128, 8, 3], F32, "HALO"); rH = [Res() for _ in range(8)]
    Wx = [sb(st, [128, 8, 128], BF16, "Wx") for _ in range(2)]; rWx = [Res(), Res()]
    Wz = sb(st, [128, 8, 512], BF16, "Wz"); rWz = Res()
    Wdt = sb(st, [128, 8, 8], BF16, "Wdt"); rWdt = Res()
    sgt = sb(st, [128, 8], F32, "sgt"); rsgt = Res()
    eA = sb(st, [128, 8], F32, "eA"); reA = Res()
    dt = sb(st, [128, 8], F32, "dt"); rdt = Res()
    av = sb(st, [128, 8], F32, "av"); rav = Res()
    sc = sb(st, [128, 3, 8], F32, "sc"); rsc = Res()
    tmp8 = sb(st, [128, 8], F32, "tmp8"); rtmp8 = Res()
    class _V3:
        def __init__(self, ap):
            self.ap = ap
        def __getitem__(self, k):
            return self.ap if (isinstance(k, slice) and k == slice(None)) else self.ap[k]
    TA8 = _V3(self.gainB[0][:].rearrange("p (h c) -> p h c", h=8)); rTA = self.r_gainB[0]
    gT = sb(st, [128, 8, 128], BF16, "gT"); rgT = Res()
    ATt = sb(st, [128, 8, 128], BF16, "ATt"); rAT = Res()
    Btok = sb(st, [128, 2, 128], BF16, "Btok"); rBt = Res()
    Xd = sb(st, [128, 8, 64], BF16, "Xd"); rXd = Res()
    Xdec = sb(st, [128, 8, 64], BF16, "Xdec"); rXdec = Res()
    xsD = sb(st, [128, 8, 64], F32, "xsD"); rxsD = Res()
    ytmp = sb(st, [128, 8, 64], F32, "ytmp"); rytmp = Res()
    yy = sb(st, [128, 512], F32, "yy"); ryy = Res()
    sgz = _V3(self.tmp32[0][:, 0:512]); rsgz = self.r_tmp32[0]
    sq = _V3(self.tmp32[0][:, 512:1024]); rsq = self.r_tmp32[0]
    ybf = sb(st, [128, 512], BF16, "ybf"); rybf = Res()
    ssg = sb(st, [128, 2, 2], F32, "ssg"); rssg = Res()
    Sst = sb(st, [128, 8, 64], F32, "Sst"); rS = Res()
    Sbf = sb(st, [128, 512], BF16, "Sbf"); rSbf = Res()
    sgt_c = sb(st, [128, 128], F32, "sgt_c"); rsgt_c = Res()
    pLG = ps(st, [128, 4, 128], F32, "pLG"); rpLG = Res()
    pCB = ps(st, [128, 512], F32, "pCB"); rpCB = Res()
    p2 = ps(st, [128, 512], F32, "p2"); rp2 = Res()
    p3 = ps(st, [128, 512], F32, "p3"); rp3 = Res()
    p4 = ps(st, [128, 512], F32, "p4"); rp4 = Res()
    pz = ps(st, [128, 512], F32, "pz"); rpz = Res()
    cwo, _ = self.poff["mcw"]; cbo, _ = self.poff["mcb"]
    alo, _ = self.poff["mAlog"]; dbo, _ = self.poff["mdtb"]; mDo, _ = self.poff["mD"]; ngo, _ = self.poff["mng"]

    P.op(POOL, lambda e: e.memset(sgt_c[:], 1.0), [], [rsgt_c])
    P.op(POOL, lambda e: e.affine_select(out=sgt_c[:], in_=sgt_c[:], pattern=[[-1, 128]], compare_op=ALU.is_gt, fill=0.0, base=0, channel_multiplier=1), [rsgt_c], [rsgt_c])
    P.op(POOL, lambda e: e.memset(HALO[:], 0.0), [], rH)
    P.op(ACT, lambda e: e.activation(out=eA[:], in_=PP[:, alo:alo + 8], func=AF.Exp), [self.rPP], [reA])
    self.load_w("in1_mz", Wz, rWz)
    self.load_w("in1_mdt", Wdt, rWdt)

    def conv_chunk(b, c):
        k = c % 2
        self.load_w(f"in1_mx{c}", Wx[k], rWx[k])
        pb, rpb = (p2, rp2) if k == 0 else (p3, rp3)
        self.proj_fm(Wx[k], rWx[k], HT, rHT, b, pb, rpb)
        u, ru, y, ry = U[k], rU[k], Y[k], rY[k]
        P.op(POOL, lambda e: e.tensor_copy(out=u[:, 0:3], in_=HALO[:, c, :]), [rH[c]], [ru])
        P.op(ACT, lambda e: e.activation(out=u[:, 3:515], in_=pb[:], func=AF.Copy), [], [ru, rpb])
        P.op(POOL, lambda e: e.tensor_copy(out=HALO[:, c, :], in_=u[:, 512:515]), [ru], [rH[c]])
        w = [PP[:, cwo + c * 4 + i:cwo + c * 4 + i + 1] for i in range(4)]
        bb = PP[:, cbo + c:cbo + c + 1]
        P.op(DVE, lambda e: e.tensor_scalar(out=y[:], in0=u[:, 0:512], scalar1=w[0], scalar2=bb, op0=ALU.mult, op1=ALU.add), [ru], [ry])
        for i in range(1, 4):
            P.op(DVE, lambda e, i=i: e.scalar_tensor_tensor(out=y[:], in0=u[:, i:i + 512], scalar=w[i], in1=y[:], op0=ALU.mult, op1=ALU.add), [ru, ry], [ry])
        P.op(ACT, lambda e: e.activation(out=XT[:, c, :], in_=y[:], func=AF.Silu), [ry], [rXT[c]])

    def chunk(n):
        tt = n % 4
        cs = slice(tt * 128, (tt + 1) * 128)
        gs = slice(n * 128, (n + 1) * 128)
        pt0, rpt0 = self.pt[0], self.r_pt[0]
        pt1, rpt1 = self.pt[1], self.r_pt[1]
        self.proj_tm(Wdt, rWdt, HT, rHT, n, pCB[:, 256:264], rpCB)
        P.op(DVE, lambda e: e.tensor_tensor(out=dt[:], in0=pCB[:, 256:264], in1=PP[:, dbo:dbo + 8], op=ALU.add), [self.rPP], [rdt, rpCB])
        P.op(ACT, lambda e: e.activation(out=dt[:], in_=dt[:], func=AF.Exp), [rdt], [rdt])
        P.op(ACT, lambda e: e.activation(out=dt[:], in_=dt[:], func=AF.Ln, bias=1.0), [rdt], [rdt])
        P.op(DVE, lambda e: e.scalar_tensor_tensor(out=av[:], in0=dt[:], scalar=-1.0, in1=eA[:], op0=ALU.mult, op1=ALU.mult), [rdt, reA], [rav])
        def mma(e):
            e.matmul(pCB[:, 264:272], lhsT=self.tri_incl[:], rhs=av[:], start=True, stop=True)
            return e.matmul(pCB[:, 272:280], lhsT=self.ones_f[:], rhs=av[:], start=True, stop=True)
        P.op(PE, mma, [rav, self.rconst], [rpCB])
        P.op(ACT, lambda e: e.activation(out=sc[:, 0, :], in_=pCB[:, 264:272], func=AF.Exp), [], [rsc, rpCB])
        P.op(ACT, lambda e: e.activation(out=sc[:, 2, :], in_=pCB[:, 272:280], func=AF.Exp), [], [rsc, rpCB])
        P.op(ACT, lambda e: e.activation(out=tmp8[:], in_=pCB[:, 264:272], func=AF.Copy), [], [rtmp8, rpCB])
        P.op(DVE, lambda e: e.tensor_tensor(out=tmp8[:], in0=pCB[:, 272:280], in1=tmp8[:], op=ALU.subtract), [], [rtmp8, rpCB])
        P.op(ACT, lambda e: e.activation(out=sc[:, 1, :], in_=tmp8[:], func=AF.Exp), [rtmp8], [rsc])
        P.op(DVE, lambda e: e.tensor_tensor(out=TA8[:], in0=self.tri_incl[:].unsqueeze(1).to_broadcast([128, 8, 128]),
                                            in1=av[:].unsqueeze(2).to_broadcast([128, 8, 128]), op=ALU.mult), [rav, self.rconst], [rTA])
        for half in range(2):
            def mml(e, half=half):
                ins = None
                for hh in range(4):
                    h = half * 4 + hh
                    e.matmul(pLG[:, hh, :], lhsT=sgt_c[:], rhs=TA8[:, h, :], start=True, stop=False)
                    ins = e.matmul(pLG[:, hh, :], lhsT=self.ident_f[:], rhs=self.neg_incl[:], start=False, stop=True)
                return ins
            P.op(PE, mml, [rTA, rsgt_c, self.rconst], [rpLG])
            P.op(ACT, lambda e, half=half: e.activation(out=gT[:, half * 4:(half + 1) * 4, :], in_=pLG[:], func=AF.Exp), [], [rgT, rpLG])
        def mmcb(e):
            ins = None
            for g in range(2):
                ins = e.matmul(pCB[:, g * 128:(g + 1) * 128], lhsT=XT[:, 4 + g, cs], rhs=XT[:, 6 + g, cs], start=True, stop=True)
            return ins
        P.op(PE, mmcb, [rXT[4], rXT[5], rXT[6], rXT[7]], [rpCB])
        P.op(DVE, lambda e: e.tensor_tensor(out=ATt[:].rearrange("p (g a) l -> p g a l", g=2),
                                            in0=pCB[:, 0:256].rearrange("p (g l) -> p g l", g=2).unsqueeze(2).to_broadcast([128, 2, 4, 128]),
                                            in1=gT[:].rearrange("p (g a) l -> p g a l", g=2), op=ALU.mult), [rgT], [rAT, rpCB])
        def trx(e):
            ins = None
            for c in range(4):
                ins = e.transpose(pt0[:, c, :], XT[:, c, cs], self.ident_bf[:])
            return ins
        P.op(PE, trx, [rXT[0], rXT[1], rXT[2], rXT[3]], [rpt0])
        def trb(e):
            ins = None
            for g in range(2):
                ins = e.transpose(pt1[:, g, :], XT[:, 4 + g, cs], self.ident_bf[:])
            return ins
        P.op(PE, trb, [rXT[4], rXT[5]], [rpt1])
        P.op(ACT, lambda e: e.activation(out=Btok[:], in_=pt1[:, 0:2, :], func=AF.Copy), [], [rBt, rpt1])
        xs_ps = pt0[:, 0:4, :].rearrange("p c (a q) -> p (c a) q", q=64)
        P.op(DVE, lambda e: e.tensor_tensor(out=Xd[:], in0=xs_ps, in1=dt[:].unsqueeze(2).to_broadcast([128, 8, 64]), op=ALU.mult), [rdt], [rXd, rpt0])
        P.op(DVE, lambda e: e.tensor_tensor(out=xsD[:], in0=xs_ps, in1=PP[:, mDo:mDo + 8].unsqueeze(2).to_broadcast([128, 8, 64]), op=ALU.mult), [self.rPP], [rxsD, rpt0])
        P.op(POOL, lambda e: e.tensor_tensor(out=Xdec[:], in0=Xd[:], in1=sc[:, 1, :].unsqueeze(2).to_broadcast([128, 8, 64]), op=ALU.mult), [rXd, rsc], [rXdec])
        if n > 0:
            def mm2(e):
                ins = None
                for g in range(2):
                    ins = e.matmul(p2[:, g * 256:(g + 1) * 256], lhsT=XT[:, 6 + g, cs], rhs=Sbf[:, g * 256:(g + 1) * 256], start=True, stop=True)
                return ins
            P.op(PE, mm2, [rXT[6], rXT[7], rSbf], [rp2])
        def mm3(e):
            ins = None
            for h in range(8):
                ins = e.matmul(p3[:, h * 64:(h + 1) * 64], lhsT=ATt[:, h, :], rhs=Xd[:, h, :], start=True, stop=True)
            return ins
        P.op(PE, mm3, [rAT, rXd], [rp3])
        if n > 0:
            P.op(DVE, lambda e: e.tensor_tensor(out=ytmp[:], in0=p2[:].rearrange("p (h q) -> p h q", q=64), in1=sc[:, 0, :].unsqueeze(2).to_broadcast([128, 8, 64]), op=ALU.mult), [rsc], [rytmp, rp2])
            P.op(DVE, lambda e: e.tensor_tensor(out=yy[:], in0=p3[:], in1=ytmp[:].rearrange("p h q -> p (h q)"), op=ALU.add), [rytmp], [ryy, rp3])
        else:
            P.op(DVE, lambda e: e.tensor_copy(out=yy[:], in_=p3[:]), [], [ryy, rp3])
        P.op(POOL, lambda e: e.tensor_tensor(out=yy[:], in0=yy[:], in1=xsD[:].rearrange("p h q -> p (h q)"), op=ALU.add), [ryy, rxsD], [ryy])
        if n < NT - 1:
            def mm4(e):
                ins = None
                for g in range(2):
                    ins = e.matmul(p4[:, g * 256:(g + 1) * 256], lhsT=Btok[:, g, :], rhs=Xdec[:, g * 4:(g + 1) * 4, :].rearrange("p a q -> p (a q)"), start=True, stop=True)
                return ins
            P.op(PE, mm4, [rBt, rXdec], [rp4])
            if n == 0:
                P.op(DVE, lambda e: e.tensor_copy(out=Sst[:].rearrange("p h q -> p (h q)"), in_=p4[:]), [], [rS, rp4])
            else:
                P.op(DVE, lambda e: e.tensor_tensor(out=Sst[:], in0=Sst[:], in1=sc[:, 2, :].unsqueeze(2).to_broadcast([128, 8, 64]), op=ALU.mult), [rS, rsc], [rS])
                P.op(DVE, lambda e: e.tensor_tensor(out=Sst[:].rearrange("p h q -> p (h q)"), in0=p4[:], in1=Sst[:].rearrange("p h q -> p (h q)"), op=ALU.add), [rS], [rS, rp4])
            P.op(ACT, lambda e: e.activation(out=Sbf[:], in_=Sst[:].rearrange("p h q -> p (h q)"), func=AF.Copy), [rS], [rSbf])
        self.proj_tm(Wz, rWz, HT, rHT, n, pz[:], rpz)
        P.op(ACT, lambda e: e.activation(out=sgz[:], in_=pz[:], func=AF.Silu), [], [rsgz, rpz])
        P.op(POOL, lambda e: e.tensor_tensor(out=yy[:], in0=yy[:], in1=sgz[:], op=ALU.mult), [ryy, rsgz], [ryy])
        P.op(POOL, lambda e: e.tensor_tensor(out=sq[:], in0=yy[:], in1=yy[:], op=ALU.mult), [ryy], [rsq])
        P.op(DVE, lambda e: e.tensor_reduce(out=ssg[:, 0, :], in_=sq[:].rearrange("p (g f) -> p g f", g=2), axis=AX.X, op=ALU.add), [rsq], [rssg])
        self.rstd_from_ss(ssg[:, 0, :], ssg[:, 1, :], rssg, rssg, 256)
        P.op(DVE, lambda e: e.tensor_tensor(out=yy[:].rearrange("p (g f) -> p g f", g=2), in0=yy[:].rearrange("p (g f) -> p g f", g=2),
                                            in1=ssg[:, 1, :].unsqueeze(2).to_broadcast([128, 2, 256]), op=ALU.mult), [ryy, rssg], [ryy])
        P.op(POOL, lambda e: e.tensor_tensor(out=ybf[:], in0=yy[:], in1=PP[:, ngo:ngo + 512], op=ALU.mult), [ryy, self.rPP], [rybf])
        def try_(e):
            ins = None
            for c in range(4):
                ins = e.transpose(pt1[:, 4 + c, :], ybf[:, c * 128:(c + 1) * 128], self.ident_bf[:])
            return ins
        P.op(PE, try_, [rybf], [rpt1])
        P.op(ACT, lambda e: e.activation(out=MT[:, 4:8, gs], in_=pt1[:, 4:8, :], func=AF.Copy), [], [rMT[n], rpt1])

    for b in range(NB):
        for c in range(8):
            conv_chunk(b, c)
        for tt in range(4):
            chunk(b * 4 + tt)


Builder.ssd = _ssd


def _gdn(self, HT, rHT, MT, rMT, st):
    P = self.P
    S, NT, NB = self.S, self.NT, self.NB
    sb, ps = self.sb, self.ps
    PP = self.PP
    v4 = lambda ap: ap.rearrange("p (h c) -> p h c", h=4)
    YT = sb(st, [128, 12, 512], BF16, "YT"); rYT = [Res() for _ in range(12)]
    U = [sb(st, [128, 515], F32, "U") for _ in range(2)]; rU = [Res(), Res()]
    Y = [sb(st, [128, 512], F32, "Y") for _ in range(2)]; rY = [Res(), Res()]
    HALO = sb(st, [128, 12, 3], F32, "HALO"); rH = [Res() for _ in range(12)]
    Wc = [sb(st, [128, 8, 128], BF16, "Wc") for _ in range(2)]; rWc = [Res(), Res()]
    Wgg = sb(st, [128, 8, 512], BF16, "Wgg"); rWgg = Res()
    Wgab = sb(st, [128, 8, 8], BF16, "Wgab"); rWgab = Res()
    sm2 = [sb(st, [128, 16, 4], F32, "sm") for _ in range(3)]; rsm2 = [Res(), Res(), Res()]
    eAg = sb(st, [128, 4], F32, "eAg"); reAg = Res()
    sqq = sb(st, [128, 4, 128], BF16, "sqq"); rsqq = Res()
    sqk = sb(st, [128, 4, 128], BF16, "sqk"); rsqk = Res()
    TTt = sb(st, [128, 512], F32, "TT"); TT = TTt[:]; rTT = Res()
    offd = sb(st, [128, 128], BF16, "offd"); roffd = Res()
    gNs = sb(st, [128, 4, 128], BF16, "gNs"); rgNs = Res()
    ATt = sb(st, [128, 4, 128], BF16, "ATt"); rAT = Res()
    Xf = sb(st, [128, 4, 256], F32, "Xf"); rXf = Res()
    kD = sb(st, [128, 4, 128], BF16, "kD"); rkD = Res()
    u0 = sb(st, [128, 4, 128], F32, "u0"); ru0 = Res()
    wv = sb(st, [128, 4, 128], BF16, "wv"); rwv = Res()
    wT = sb(st, [128, 4, 128], BF16, "wT"); rwT = Res()
    ubf = sb(st, [128, 4, 128], BF16, "ubf"); rubf = Res()
    Sst = sb(st, [128, 4, 128], F32, "Sst"); rS = Res()
    Sbf = sb(st, [128, 4, 128], BF16, "Sbf"); rSbf = Res()
    sgt_c = sb(st, [128, 128], F32, "sgt_c"); rsgt_c = Res()
    o_t = self.tmp32[0][:, 0:512]; otmp = self.tmp32[0][:, 512:1024]; ro = self.r_tmp32[0]
    sq_t = Y[0][:]; rsq = rY[0]
    sgg = Y[1][:]; rsgg = rY[1]
    TG4 = U[0][:, 0:512]; DRA4 = U[1][:, 0:512]; rTG = rU[0]; rDRA = rU[1]
    gA = self.hn[0][:, 0:512]; gN = self.hn[0][:, 512:1024]; rgm = self.r_hn[0]
    ybf = self.hn[1][:, 0:512]; rybf = self.r_hn[1]
    Pc = [(self.gainB[0][:, 0:512], self.gainB[0][:, 512:1024], self.r_gainB[0], Res()),
          (self.gainB[1][:, 0:512], self.gainB[1][:, 512:1024], self.r_gainB[1], Res())]
    X = self.X
    xs = self.gdn_xsub
    def sub(t):
        r = Res(); xs[t].append(r); r.rd.append(self.gdn_spill_ops[t]); return r
    bf = lambda t: X[:, t, :].bitcast(BF16)
    r4 = lambda ap: ap.rearrange("p (h c) -> p h c", h=4)
    B0 = dict(sqq=sqq, rsqq=rsqq, sqk=sqk, rsqk=rsqk, TG4=TG4, DRA4=DRA4, rTG=rTG, rDRA=rDRA, gA=gA, rgm=rgm,
              gNs=gNs, rgNs=rgNs, ATt=ATt, rAT=rAT, Pc=Pc, TT=TT, rTT=rTT, Xf=Xf, rXf=rXf, kD=kD, rkD=rkD,
              u0=u0, ru0=ru0, wv=wv, rwv=rwv, wT=wT, rwT=rwT)
    def xset(o):
        return dict(sqq=V(r4(bf(o + 5)[:, 0:512])), rsqq=sub(o + 5), sqk=V(r4(bf(o + 5)[:, 512:1024])), rsqk=sub(o + 5),
              gA=bf(o + 5)[:, 1024:1536], rgm=sub(o + 5), gNs=V(r4(bf(o + 5)[:, 1536:2048])), rgNs=sub(o + 5),
              TG4=X[:, o + 4, 0:512], DRA4=X[:, o + 4, 512:1024], rTG=sub(o + 4), rDRA=sub(o + 4),
              ATt=V(r4(bf(o + 6)[:, 0:512])), rAT=sub(o + 6), kD=V(r4(bf(o + 6)[:, 512:1024])), rkD=sub(o + 6),
              wv=V(r4(bf(o + 6)[:, 1024:1536])), rwv=sub(o + 6), wT=V(r4(bf(o + 6)[:, 1536:2048])), rwT=sub(o + 6),
              Pc=[(X[:, o + 0, 0:512], X[:, o + 0, 512:1024], sub(o + 0), sub(o + 0)), (X[:, o + 1, 0:512], X[:, o + 1, 512:1024], sub(o + 1), sub(o + 1))],
              TT=X[:, o + 2, 0:512], rTT=sub(o + 2), u0=V(r4(X[:, o + 2, 512:1024])), ru0=sub(o + 2),
              Xf=V(X[:, o + 3, :].rearrange("p (h c) -> p h c", h=4)), rXf=sub(o + 3))
    NSET = 1 + self.gdn_nx // 7
    BUFS = [B0] + [xset(7 * i) for i in range(NSET - 1)]
    pa = ps(st, [128, 512], F32, "pa"); rpa = Res()
    pb = ps(st, [128, 512], F32, "pb"); rpb = Res()
    pcd = ps(st, [128, 1024], F32, "pcd"); rpcd = Res()
    pc_, pd_ = pcd[:, 0:512], pcd[:, 512:1024]
    pe = ps(st, [128, 512], F32, "pe"); rpe = Res()
    pg = ps(st, [128, 512], F32, "pg"); rpg = Res()
    cwo, _ = self.poff["gcw"]
    alo, _ = self.poff["gAlog"]; dbo, _ = self.poff["gdtb"]; ngo, _ = self.poff["gng"]
    GV, LNB, BETA, GC, GL, EG, EGLG, GLB, LNRK, RK, CR, CD, T1, T2, MSR, RSTD = range(16)
    P.op(POOL, lambda e: e.memset(offd[:], 1.0), [], [roffd])
    P.op(POOL, lambda e: e.affine_select(out=offd[:], in_=offd[:], pattern=[[-1, 128]], compare_op=ALU.not_equal, fill=0.0, base=0, channel_multiplier=1), [roffd], [roffd])

    P.op(POOL, lambda e: e.memset(sgt_c[:], 1.0), [], [rsgt_c])
    P.op(POOL, lambda e: e.affine_select(out=sgt_c[:], in_=sgt_c[:], pattern=[[-1, 128]], compare_op=ALU.is_gt, fill=0.0, base=0, channel_multiplier=1), [rsgt_c], [rsgt_c])
    P.op(POOL, lambda e: e.memset(HALO[:], 0.0), [], rH)
    P.op(ACT, lambda e: e.activation(out=eAg[:], in_=PP[:, alo:alo + 4], func=AF.Exp), [self.rPP], [reAg])
    self.load_w("in0_gg", Wgg, rWgg)
    self.load_w("in0_gab", Wgab, rWgab)

    def conv_chunk(b, c):
        k = c % 2
        name = ("gq", "gk", "gv")[c // 4] + str(c % 4)
        self.load_w(f"in0_{name}", Wc[k], rWc[k])
        pbk, rpbk = (pa, rpa) if k == 0 else (pb, rpb)
        self.proj_fm(Wc[k], rWc[k], HT, rHT, b, pbk, rpbk)
        u, ru, y, ry = U[k], rU[k], Y[k], rY[k]
        P.op(POOL, lambda e: e.tensor_copy(out=u[:, 0:3], in_=HALO[:, c, :]), [rH[c]], [ru])
        P.op(ACT, lambda e: e.activation(out=u[:, 3:515], in_=pbk[:], func=AF.Copy), [], [ru, rpbk])
        P.op(POOL, lambda e: e.tensor_copy(out=HALO[:, c, :], in_=u[:, 512:515]), [ru], [rH[c]])
        w = [PP[:, cwo + c * 4 + i:cwo + c * 4 + i + 1] for i in range(4)]
        P.op(DVE, lambda e: e.tensor_scalar(out=y[:], in0=u[:, 0:512], scalar1=w[0], scalar2=None, op0=ALU.mult), [ru], [ry])
        for i in range(1, 4):
            P.op(DVE, lambda e, i=i: e.scalar_tensor_tensor(out=y[:], in0=u[:, i:i + 512], scalar=w[i], in1=y[:], op0=ALU.mult, op1=ALU.add), [ru, ry], [ry])
        P.op(ACT, lambda e: e.activation(out=YT[:, c, :], in_=y[:], func=AF.Silu), [ry], [rYT[c]])

    def chunk(n):
        sm, rsm = sm2[n % len(BUFS)], rsm2[n % len(BUFS)]
        Bf = BUFS[n % len(BUFS)]
        sqq, rsqq, sqk, rsqk = Bf["sqq"], Bf["rsqq"], Bf["sqk"], Bf["rsqk"]
        TG4, DRA4, rTG, rDRA = Bf["TG4"], Bf["DRA4"], Bf["rTG"], Bf["rDRA"]
        gA, rgm, gNs, rgNs = Bf["gA"], Bf["rgm"], Bf["gNs"], Bf["rgNs"]
        ATt, rAT, Pc, TT, rTT = Bf["ATt"], Bf["rAT"], Bf["Pc"], Bf["TT"], Bf["rTT"]
        Xf, rXf, kD, rkD = Bf["Xf"], Bf["rXf"], Bf["kD"], Bf["rkD"]
        u0, ru0, wv, rwv, wT, rwT = Bf["u0"], Bf["ru0"], Bf["wv"], Bf["rwv"], Bf["wT"], Bf["rwT"]

        def bc(row):
            return sm[:, row, :].unsqueeze(2).to_broadcast([128, 4, 128])
        tt = n % 4
        cs = slice(tt * 128, (tt + 1) * 128)
        gs = slice(n * 128, (n + 1) * 128)
        pt0, rpt0 = self.pt[0], self.r_pt[0]
        pt1, rpt1 = self.pt[1], self.r_pt[1]
        rq = [rYT[h] for h in range(4)]; rk = [rYT[4 + h] for h in range(4)]; rv = [rYT[8 + h] for h in range(4)]
        self.proj_tm(Wgab, rWgab, HT, rHT, n, pe[:, 0:8], rpe)
        P.op(DVE, lambda e: e.tensor_tensor(out=sm[:, GV, :], in0=pe[:, 0:4], in1=PP[:, dbo:dbo + 4], op=ALU.add), [self.rPP], [rsm, rpe])
        P.op(ACT, lambda e: e.activation(out=sm[:, GV, :], in_=sm[:, GV, :], func=AF.Exp), [rsm], [rsm])
        P.op(ACT, lambda e: e.activation(out=sm[:, GV, :], in_=sm[:, GV, :], func=AF.Ln, bias=1.0), [rsm], [rsm])
        P.op(DVE, lambda e: e.scalar_tensor_tensor(out=sm[:, GV, :], in0=sm[:, GV, :], scalar=-1.0, in1=eAg[:], op0=ALU.mult, op1=ALU.mult), [rsm, reAg], [rsm])
        P.op(ACT, lambda e: e.activation(out=sm[:, LNB, :], in_=pe[:, 4:8], func=AF.Exp, scale=-1.0), [], [rsm, rpe])
        P.op(ACT, lambda e: e.activation(out=sm[:, LNB, :], in_=sm[:, LNB, :], func=AF.Ln, bias=1.0), [rsm], [rsm])
        P.op(ACT, lambda e: e.activation(out=sm[:, BETA, :], in_=sm[:, LNB, :], func=AF.Exp, scale=-1.0), [rsm], [rsm])
        P.op(ACT, lambda e: e.activation(out=sqq[:], in_=YT[:, 0:4, cs], func=AF.Square), rq, [rsqq])
        P.op(ACT, lambda e: e.activation(out=sqk[:], in_=YT[:, 4:8, cs], func=AF.Square), rk, [rsqk])
        def mmg(e):
            e.matmul(pe[:, 8:12], lhsT=self.tri_incl[:], rhs=sm[:, GV, :], start=True, stop=True)
            ins = e.matmul(pe[:, 12:16], lhsT=self.ones_f[:], rhs=sm[:, GV, :], start=True, stop=True)
            for h in range(4):
                e.matmul(pe[:, 16 + h:17 + h], lhsT=sqk[:, h, :], rhs=self.ones_bf[:, 0:1], start=True, stop=True)
                ins = e.matmul(pe[:, 20 + h:21 + h], lhsT=sqq[:, h, :], rhs=self.ones_bf[:, 0:1], start=True, stop=True)
            return ins
        P.op(PE, mmg, [rsm, rsqq, rsqk, self.rconst], [rpe])
        P.op(ACT, lambda e: e.activation(out=sm[:, GC, :], in_=pe[:, 8:12], func=AF.Copy), [], [rsm, rpe])
        P.op(ACT, lambda e: e.activation(out=sm[:, EG, :], in_=pe[:, 8:12], func=AF.Exp), [], [rsm, rpe])
        P.op(ACT, lambda e: e.activation(out=sm[:, GLB, :], in_=pe[:, 12:16], func=AF.Exp), [], [rsm, rpe])
        P.op(DVE, lambda e: e.tensor_tensor(out=sm[:, GL, :], in0=pe[:, 12:16], in1=sm[:, GC, :], op=ALU.subtract), [rsm], [rsm, rpe])
        P.op(ACT, lambda e: e.activation(out=sm[:, EGLG, :], in_=sm[:, GL, :], func=AF.Exp), [rsm], [rsm])
        P.op(ACT, lambda e: e.activation(out=sm[:, LNRK, :], in_=pe[:, 16:20], func=AF.Ln, bias=self.eps_t[:, 0:1]), [], [rsm, rpe])
        P.op(DVE, lambda e: e.tensor_scalar(out=sm[:, LNRK, :], in0=sm[:, LNRK, :], scalar1=-0.5, scalar2=None, op0=ALU.mult), [rsm], [rsm])
        P.op(ACT, lambda e: e.activation(out=sm[:, RK, :], in_=sm[:, LNRK, :], func=AF.Exp), [rsm], [rsm])
        P.op(ACT, lambda e: e.activation(out=sm[:, CR, :], in_=sm[:, LNRK, :], func=AF.Exp, scale=-1.0), [rsm], [rsm])
        P.op(DVE, lambda e: e.tensor_tensor(out=sm[:, CD, :], in0=sm[:, RK, :], in1=sm[:, EGLG, :], op=ALU.mult), [rsm], [rsm])
        P.op(DVE, lambda e: e.tensor_tensor(out=sm[:, T1, :], in0=sm[:, RK, :], in1=sm[:, BETA, :], op=ALU.mult), [rsm], [rsm])
        P.op(DVE, lambda e: e.tensor_scalar(out=sm[:, T2, :], in0=pe[:, 20:24], scalar1=EPS, scalar2=128.0 * EPS, op0=ALU.add, op1=ALU.mult), [], [rsm, rpe])
        def mmgram(e):
            ins = None
            for h in range(4):
                e.matmul(pa[:, h * 128:(h + 1) * 128], lhsT=YT[:, 4 + h, cs], rhs=YT[:, 4 + h, cs], start=True, stop=True)
                ins = e.matmul(pb[:, h * 128:(h + 1) * 128], lhsT=YT[:, 4 + h, cs], rhs=YT[:, h, cs], start=True, stop=True)
            return ins
        P.op(PE, mmgram, rq + rk, [rpa, rpb])
        I4 = self.ident_f[:].unsqueeze(1).to_broadcast([128, 4, 128])
        P.op(DVE, lambda e: e.tensor_tensor(out=v4(TG4), in0=self.tri_incl[:].unsqueeze(1).to_broadcast([128, 4, 128]), in1=bc(GV), op=ALU.mult), [rsm, self.rconst], [rTG])
        P.op(DVE, lambda e: e.tensor_tensor(out=v4(DRA4), in0=I4, in1=bc(LNRK), op=ALU.mult), [rsm, self.rconst], [rDRA])
        def mmlg(e):
            ins = None
            for h in range(4):
                hs = slice(h * 128, (h + 1) * 128)
                e.matmul(pc_[:, hs], lhsT=sgt_c[:], rhs=TG4[:, hs], start=True, stop=False)
                e.matmul(pc_[:, hs], lhsT=DRA4[:, hs], rhs=self.ones_f[:], start=False, stop=False)
                ins = e.matmul(pc_[:, hs], lhsT=self.ident_f[:], rhs=self.neg_incl[:], start=False, stop=True)
            return ins
        P.op(PE, mmlg, [rTG, rDRA, rsgt_c, self.rconst], [rpcd])
        P.op(ACT, lambda e: e.activation(out=gA, in_=pc_, func=AF.Exp), [], [rgm, rpcd])
        P.op(DVE, lambda e: e.tensor_tensor(out=ATt[:].rearrange("p h c -> p (h c)"), in0=pb[:], in1=gA, op=ALU.mult), [rgm], [rAT, rpb])
        P.op(POOL, lambda e: e.tensor_tensor(out=gNs[:], in0=v4(gA), in1=bc(T1), op=ALU.mult), [rgm, rsm], [rgNs])
        P.op(POOL, lambda e: e.tensor_tensor(out=gNs[:], in0=gNs[:], in1=offd[:].unsqueeze(1).to_broadcast([128, 4, 128]), op=ALU.mult), [rgNs, roffd], [rgNs])
        P0, P0T, rP0, rP0T = Pc[0]
        P.op(DVE, lambda e: e.scalar_tensor_tensor(out=P0T, in0=pa[:], scalar=-1.0, in1=gNs[:].rearrange("p h c -> p (h c)"), op0=ALU.mult, op1=ALU.mult), [rgNs], [rP0T, rpa])
        def trp(e):
            ins = None
            for h in range(4):
                hs = slice(h * 128, (h + 1) * 128)
                ins = e.matmul(pb[:, hs], lhsT=P0T[:, hs], rhs=self.ident_f[:], start=True, stop=True)
            return ins
        P.op(PE, trp, [rP0T, self.rconst], [rpb])
        P.op(ACT, lambda e: e.activation(out=P0, in_=pb[:], func=AF.Copy), [], [rP0, rpb])
        def trkv(e):
            ins = None
            for h in range(4):
                e.transpose(pt0[:, h, :], YT[:, 4 + h, cs], self.ident_bf[:])
                ins = e.transpose(pt0[:, 4 + h, :], YT[:, 8 + h, cs], self.ident_bf[:])
            return ins
        P.op(PE, trkv, rk + rv, [rpt0])
        P.op(DVE, lambda e: e.tensor_tensor(out=Xf[:, :, 0:128], in0=pt0[:, 4:8, :], in1=bc(CR), op=ALU.mult), [rsm], [rXf, rpt0])
        P.op(DVE, lambda e: e.tensor_tensor(out=Xf[:, :, 128:256], in0=pt0[:, 0:4, :], in1=bc(EG), op=ALU.mult), [rsm], [rXf, rpt0])
        P.op(DVE, lambda e: e.tensor_tensor(out=kD[:], in0=pt0[:, 0:4, :], in1=bc(CD), op=ALU.mult), [rsm], [rkD, rpt0])
        P.op(POOL, lambda e: e.tensor_tensor(out=v4(TT), in0=v4(P0T), in1=self.ident_f[:].unsqueeze(1).to_broadcast([128, 4, 128]), op=ALU.add), [rP0T, self.rconst], [rTT])
        yield
        for j in range(6):
            Pj, PjT, rPj, rPjT = Pc[j % 2]
            Pn, PnT, rPn, rPnT = Pc[(j + 1) % 2]
            def mmsq(e, Pj=Pj, PjT=PjT):
                ins = None
                for h in range(4):
                    hs = slice(h * 128, (h + 1) * 128)
                    e.matmul(pa[:, hs], lhsT=PjT[:, hs], rhs=Pj[:, hs], start=True, stop=True)
                    ins = e.matmul(pb[:, hs], lhsT=Pj[:, hs], rhs=PjT[:, hs], start=True, stop=True)
                return ins
            P.op(PE, mmsq, [rPj, rPjT], [rpa, rpb])
            P.op(ACT, lambda e, Pn=Pn: e.activation(out=Pn, in_=pa[:], func=AF.Copy), [], [rPn, rpa])
            P.op(DVE, lambda e, PnT=PnT: e.tensor_copy(out=PnT, in_=pb[:]), [], [rPnT, rpb])
            def mmt(e, Pn=Pn):
                ins = None
                for h in range(4):
                    hs = slice(h * 128, (h + 1) * 128)
                    ins = e.matmul(pc_[:, hs], lhsT=Pn[:, hs], rhs=TT[:, hs], start=True, stop=True)
                return ins
            P.op(PE, mmt, [rPn, rTT], [rpcd])
            P.op(DVE, lambda e: e.tensor_tensor(out=TT, in0=pc_, in1=TT, op=ALU.add), [], [rTT, rpcd])
            yield
        def mmz(e):
            ins = None
            for h in range(4):
                ins = e.matmul(pcd[:, h * 256:(h + 1) * 256], lhsT=TT[:, h * 128:(h + 1) * 128], rhs=Xf[:, h, :], start=True, stop=True)
            return ins
        P.op(PE, mmz, [rTT, rXf], [rpcd])
        P.op(ACT, lambda e: e.activation(out=Xf[:].rearrange("p h c -> p (h c)"), in_=pcd[:], func=AF.Copy), [], [rXf, rpcd])
        Z, rZ = Xf, rXf
        P.op(DVE, lambda e: e.tensor_tensor(out=u0[:], in0=Z[:, :, 0:128], in1=bc(T1), op=ALU.mult), [rZ, rsm], [ru0])
        P.op(POOL, lambda e: e.tensor_tensor(out=wv[:], in0=Z[:, :, 128:256], in1=bc(T1), op=ALU.mult), [rZ, rsm], [rwv])
        def trw(e):
            ins = None
            for h in range(4):
                ins = e.transpose(pt1[:, 4 + h, :], wv[:, h, :], self.ident_bf[:])
            return ins
        P.op(PE, trw, [rwv], [rpt1])
        P.op(ACT, lambda e: e.activation(out=wT[:], in_=pt1[:, 4:8, :], func=AF.Copy), [], [rwT, rpt1])
        if n > 0:
            def mm1(e):
                ins = None
                for h in range(4):
                    ins = e.matmul(pa[:, h * 128:(h + 1) * 128], lhsT=wT[:, h, :], rhs=Sbf[:, h, :], start=True, stop=True)
                return ins
            P.op(PE, mm1, [rwT, rSbf], [rpa])
            P.op(DVE, lambda e: e.tensor_tensor(out=ubf[:].rearrange("p h c -> p (h c)"), in0=u0[:].rearrange("p h c -> p (h c)"), in1=pa[:], op=ALU.subtract), [ru0], [rubf, rpa])
            def mm2(e):
                ins = None
                for h in range(4):
                    ins = e.matmul(pb[:, h * 128:(h + 1) * 128], lhsT=YT[:, h, cs], rhs=Sbf[:, h, :], start=True, stop=True)
                return ins
            P.op(PE, mm2, rq + [rSbf], [rpb])
        else:
            P.op(DVE, lambda e: e.tensor_copy(out=ubf[:], in_=u0[:]), [ru0], [rubf])
        def mm3(e):
            ins = None
            for h in range(4):
                ins = e.matmul(pc_[:, h * 128:(h + 1) * 128], lhsT=ATt[:, h, :], rhs=ubf[:, h, :], start=True, stop=True)
            return ins
        P.op(PE, mm3, [rAT, rubf], [rpcd])
        if n > 0:
            P.op(DVE, lambda e: e.tensor_tensor(out=v4(otmp), in0=v4(pb[:]), in1=bc(EG), op=ALU.mult), [rsm], [ro, rpb])
            P.op(DVE, lambda e: e.tensor_tensor(out=o_t, in0=pc_, in1=otmp, op=ALU.add), [], [ro, rpcd])
        else:
            P.op(DVE, lambda e: e.tensor_copy(out=o_t, in_=pc_), [], [ro, rpcd])
        if n < NT - 1:
            def mm4(e):
                ins = None
                for h in range(4):
                    ins = e.matmul(pd_[:, h * 128:(h + 1) * 128], lhsT=kD[:, h, :], rhs=ubf[:, h, :], start=True, stop=True)
                return ins
            P.op(PE, mm4, [rkD, rubf], [rpcd])
            if n == 0:
                P.op(DVE, lambda e: e.tensor_copy(out=Sst[:].rearrange("p h c -> p (h c)"), in_=pd_), [], [rS, rpcd])
            else:
                P.op(DVE, lambda e: e.tensor_tensor(out=Sst[:], in0=Sst[:], in1=bc(GLB), op=ALU.mult), [rS, rsm], [rS])
                P.op(DVE, lambda e: e.tensor_tensor(out=Sst[:].rearrange("p h c -> p (h c)"), in0=pd_, in1=Sst[:].rearrange("p h c -> p (h c)"), op=ALU.add), [rS], [rS, rpcd])
            P.op(ACT, lambda e: e.activation(out=Sbf[:], in_=Sst[:], func=AF.Copy), [rS], [rSbf])
        P.op(POOL, lambda e: e.tensor_tensor(out=sq_t, in0=o_t, in1=o_t, op=ALU.mult), [ro], [rsq])
        P.op(DVE, lambda e: e.tensor_reduce(out=sm[:, MSR, :], in_=v4(sq_t), axis=AX.X, op=ALU.add), [rsq], [rsm])
        P.op(DVE, lambda e: e.scalar_tensor_tensor(out=sm[:, MSR, :], in0=sm[:, MSR, :], scalar=1.0 / 128, in1=sm[:, T2, :], op0=ALU.mult, op1=ALU.add), [rsm], [rsm])
        P.op(ACT, lambda e: e.activation(out=sm[:, RSTD, :], in_=sm[:, MSR, :], func=AF.Sqrt), [rsm], [rsm])
        P.op(DVE, lambda e: e.reciprocal(out=sm[:, RSTD, :], in_=sm[:, RSTD, :]), [rsm], [rsm])
        P.op(DVE, lambda e: e.tensor_tensor(out=v4(o_t), in0=v4(o_t), in1=bc(RSTD), op=ALU.mult), [ro, rsm], [ro])
        P.op(POOL, lambda e: e.tensor_tensor(out=v4(o_t), in0=v4(o_t), in1=PP[:, ngo:ngo + 128].unsqueeze(1).to_broadcast([128, 4, 128]), op=ALU.mult), [ro, self.rPP], [ro])
        self.proj_tm(Wgg, rWgg, HT, rHT, n, pg[:], rpg)
        P.op(ACT, lambda e: e.activation(out=sgg, in_=pg[:], func=AF.Silu), [], [rsgg, rpg])
        P.op(POOL, lambda e: e.tensor_tensor(out=ybf, in0=o_t, in1=sgg, op=ALU.mult), [ro, rsgg], [rybf])
        pgb = pg[:].bitcast(BF16)
        def try_(e):
            ins = None
            for c in range(4):
                ins = e.transpose(pgb[:, c * 128:(c + 1) * 128], ybf[:, c * 128:(c + 1) * 128], self.ident_bf[:])
            return ins
        P.op(PE, try_, [rybf], [rpg])
        P.op(ACT, lambda e: e.activation(out=MT[:, 4:8, gs], in_=pgb[:, 0:512].rearrange("p (c t) -> p c t", c=4), func=AF.Copy), [], [rMT[n], rpg])

    def run(gens):
        alive = list(gens)
        for gn in alive:
            next(gn)
        for j in range(6):
            for gn in alive:
                next(gn)
        for gn in alive:
            for _ in gn:
                pass

    for b in range(NB):
        for c in range(12):
            conv_chunk(b, c)
        run([chunk(b * 4 + 0), chunk(b * 4 + 1)])
        run([chunk(b * 4 + 2), chunk(b * 4 + 3)])


Builder.gdn = _gdn
```

```python
import contextlib
import numpy as np
import concourse.bass as bass
import concourse.mybir as mybir
from concourse.bass_utils import run_bass_kernel_spmd

F32 = mybir.dt.float32
BF16 = mybir.dt.bfloat16
ALU = mybir.AluOpType
AF = mybir.ActivationFunctionType
AX = mybir.AxisListType

PE, ACT, DVE, POOL, SP = "tensor", "scalar", "vector", "gpsimd", "sync"
EPOCH = 30000
NDSEM = 8

D = 1024
DFF = 2816
NFF = 22
EPS = 1e-6
NCORES = 8


import heapq


class V:
    def __init__(self, ap):
        self.ap = ap

    def __getitem__(self, k):
        return self.ap if (isinstance(k, slice) and k == slice(None)) else self.ap[k]


class Res:
    __slots__ = ("lw", "rd")

    def __init__(self):
        self.lw = None
        self.rd = []


class _Rec:
    class _I:
        def then_inc(self, *a, **k):
            return self

    def __init__(self):
        self.calls = []

    def __getattr__(self, name):
        def f(*a, **k):
            self.calls.append((name, a, k))
            return _Rec._I()
        return f


def _free(ap):
    n = 1
    for d in ap.shape[1:]:
        n *= d
    return n


def _cost(eng, fn, is_dma):
    r = _Rec()
    try:
        fn(r)
    except Exception:
        return 500.0
    c = 0.0
    for name, a, k in r.calls:
        out = k.get("out", a[0] if a else None)
        try:
            n = _free(out)
        except Exception:
            n = 128
        if is_dma:
            try:
                byts = n * out.shape[0] * (4 if out.dtype == F32 else 2)
            except Exception:
                byts = 65536
            c += 2000.0 + byts / 150.0
        elif eng == PE:
            lhsT = k.get("lhsT", a[1] if len(a) > 1 else None)
            f32 = False
            try:
                f32 = (lhsT is not None and lhsT.dtype == F32)
            except Exception:
                pass
            if name == "transpose":
                c += 110.0
            else:
                c += (max(n, 64) / 2.4 + 25.0) * (4.0 if f32 else 1.0)
        elif eng == ACT:
            c += 230.0 + 0.85 * n
        elif eng == DVE:
            c += 120.0 + 1.05 * n
        else:
            c += 150.0 + 1.7 * n
    return max(c, 60.0)


class Op:
    __slots__ = ("eng", "fn", "deps", "is_dma", "sig", "ticket", "dslot", "dtarget", "cost", "seg", "nobar", "is_bar", "fin", "start", "tag")

    def __init__(self, eng, fn, is_dma):
        self.eng = eng
        self.fn = fn
        self.deps = set()
        self.is_dma = is_dma
        self.sig = False
        self.ticket = None
        self.nobar = False
        self.is_bar = False
        self.fin = 0.0


SYNC_LAT = 500.0
SCHEDULE = True
PRIO_CP = False


class Prog:
    def __init__(self, nc):
        self.nc = nc
        self.ops = []
        self.seg = 0
        self.bar = None

    def op(self, eng, fn, reads=(), writes=(), dma=False, nobar=False):
        i = len(self.ops)
        o = Op(eng, fn, dma)
        o.cost = _cost(eng, fn, dma)
        o.seg = self.seg
        import sys as _sys
        fr = _sys._getframe(1)
        if fr.f_code.co_name in ("dma", "proj_fm", "proj_tm", "_proj_fm", "_proj_tm", "rstd_from_ss"):
            fr = fr.f_back
        o.tag = f"{fr.f_code.co_name}:{fr.f_lineno}"
        o.start = 0.0
        o.nobar = nobar
        for r in reads:
            if r.lw is not None:
                o.deps.add(r.lw)
        for w in writes:
            if w.lw is not None:
                o.deps.add(w.lw)
            o.deps.update(w.rd)
        for r in reads:
            r.rd.append(i)
        for w in writes:
            w.lw = i
            w.rd = []
        if not nobar and self.bar is not None:
            o.deps.add(self.bar)
        o.deps.discard(i)
        self.ops.append(o)
        return i

    def dma(self, q, out, in_, reads=(), writes=(), nobar=False):
        return self.op(q, lambda e: e.dma_start(out=out, in_=in_), reads, writes, dma=True, nobar=nobar)

    def barrier(self, scratch_ap):
        i = len(self.ops)
        o = Op(POOL, lambda e: e.memset(scratch_ap, 0.0), False)
        o.cost = 100.0
        o.seg = self.seg
        o.is_bar = True
        if self.bar is not None:
            o.deps.add(self.bar)
        self.ops.append(o)
        self.bar = i
        self.seg += 1
        return i

    def schedule(self):
        ops = self.ops
        order = {}
        eng_free = {}
        self.seg_stats = []
        segs = {}
        for i, o in enumerate(ops):
            segs.setdefault(o.seg, []).append(i)
        tglob = 0.0
        for sg in sorted(segs):
            idxs = segs[sg]
            inseg = set(idxs)
            bar_i = None
            body = []
            for i in idxs:
                if ops[i].is_bar:
                    bar_i = i
                else:
                    body.append(i)
            if not SCHEDULE:
                for i in body:
                    order.setdefault(ops[i].eng, []).append(i)
            else:
                ndeps = {}
                users = {}
                for i in body:
                    c = 0
                    for j in ops[i].deps:
                        if j in inseg and not ops[j].is_bar:
                            c += 1
                            users.setdefault(j, []).append(i)
                    ndeps[i] = c
                bl = {}
                for i in reversed(body):
                    m = 0.0
                    for u in users.get(i, ()):
                        lat = 0.0 if (ops[u].eng == ops[i].eng == PE) else SYNC_LAT
                        m = max(m, bl[u] + lat)
                    bl[i] = m + ops[i].cost
                ready = {}
                rtime = {}
                for i in body:
                    if ndeps[i] == 0:
                        rtime[i] = tglob
                        heapq.heappush(ready.setdefault(ops[i].eng, []), (tglob, i))
                for e in ready:
                    eng_free.setdefault(e, tglob)
                remaining = len(body)
                while remaining:
                    best = None
                    for e, hp in ready.items():
                        if not hp:
                            continue
                        ef = max(eng_free.get(e, tglob), tglob)
                        cand = None
                        avail = [x for x in hp if x[0] <= ef]
                        if avail:
                            if PRIO_CP:
                                ci = min(avail, key=lambda x: (-bl[x[1]], x[1]))
                            else:
                                ci = min(avail, key=lambda x: x[1])
                            cand = (ef, ci[1], e, ci)
                        else:
                            ci = hp[0]
                            cand = (ci[0], ci[1], e, ci)
                        if best is None or cand[:2] < best[:2]:
                            best = cand
                    start, i, e, item = best
                    hp = ready[e]
                    hp.remove(item)
                    heapq.heapify(hp)
                    o = ops[i]
                    if o.is_dma:
                        eng_free[e] = start + (6000.0 if e == POOL else 70.0)
                        o.fin = start + o.cost
                    else:
                        eng_free[e] = start + o.cost
                        o.fin = start + o.cost
                    order.setdefault(e, []).append(i)
                    o.start = start
                    remaining -= 1
                    for u in users.get(i, ()):
                        ndeps[u] -= 1
                        lat = 0.0 if (ops[u].eng == e and e == PE) else SYNC_LAT
                        rtime[u] = max(rtime.get(u, tglob), o.fin + lat)
                        if ndeps[u] == 0:
                            heapq.heappush(ready.setdefault(ops[u].eng, []), (rtime[u], u))
                if body:
                    t_prev = tglob
                    tglob = max([tglob] + [ops[i].fin for i in body])
                    busy = {}
                    for i in body:
                        if not ops[i].is_dma:
                            busy[ops[i].eng] = busy.get(ops[i].eng, 0.0) + ops[i].cost
                    self.seg_stats.append((sg, tglob - t_prev, busy))
            if bar_i is not None:
                b = ops[bar_i]
                last = {}
                for i in body:
                    o = ops[i]
                    if o.nobar:
                        continue
                    if o.is_dma:
                        b.deps.add(i)
                for e, lst in order.items():
                    for i in reversed(lst):
                        if ops[i].seg != sg:
                            break
                        if not ops[i].is_dma and not ops[i].nobar:
                            b.deps.add(i)
                            break
                order.setdefault(POOL, []).append(bar_i)
                eng_free[POOL] = tglob + 100.0
                tglob += 300.0
        self.est_ns = tglob
        return order

    def emit(self, final_wait_ops=()):
        nc = self.nc
        ops = self.ops
        per_eng = self.schedule()
        for i, o in enumerate(ops):
            for j in o.deps:
                d = ops[j]
                if d.is_dma:
                    continue
                if d.eng == PE and o.eng == PE and not o.is_dma:
                    continue
                d.sig = True
        for j in final_wait_ops:
            if not ops[j].is_dma:
                ops[j].sig = True
        cnt = {}
        dcnt = {}
        for eng, lst in per_eng.items():
            for i in lst:
                o = ops[i]
                if o.is_dma:
                    k = dcnt.get(eng, 0)
                    dcnt[eng] = k + 1
                    o.dslot = k % NDSEM
                    o.dtarget = 16 * (k // NDSEM + 1)
                elif o.sig:
                    cnt[eng] = cnt.get(eng, 0) + 1
                    o.ticket = cnt[eng]
        with contextlib.ExitStack() as st:
            sems = {}
            for eng, n in cnt.items():
                for ep in range((n + EPOCH - 1) // EPOCH + 1):
                    sems[(eng, ep)] = st.enter_context(nc.semaphore(f"s_{eng}_{ep}"))
            dsems = {}
            for q in dcnt:
                for s in range(NDSEM):
                    dsems[(q, s)] = st.enter_context(nc.semaphore(f"d_{q}_{s}"))
            block = st.enter_context(nc.Block())

            def run_engine(engname):
                def body(e):
                    waited = {}
                    for i in per_eng.get(engname, []):
                        o = ops[i]
                        need = {}
                        for j in o.deps:
                            d = ops[j]
                            if d.is_dma:
                                key = ("d", d.eng, d.dslot)
                                need[key] = max(need.get(key, 0), d.dtarget)
                            else:
                                if d.eng == PE and o.eng == PE and not o.is_dma:
                                    continue
                                ep = (d.ticket - 1) // EPOCH
                                key = ("c", d.eng, ep)
                                need[key] = max(need.get(key, 0), d.ticket - ep * EPOCH)
                        if o.is_dma and o.dtarget > 16:
                            key = ("d", o.eng, o.dslot)
                            need[key] = max(need.get(key, 0), o.dtarget - 16)
                        for key, v in need.items():
                            if waited.get(key, 0) >= v:
                                continue
                            waited[key] = v
                            s = dsems[(key[1], key[2])] if key[0] == "d" else sems[(key[1], key[2])]
                            e.wait_ge(s, v)
                        ins = o.fn(e)
                        if o.is_dma:
                            ins.then_inc(dsems[(o.eng, o.dslot)], 16)
                        elif o.sig:
                            ep = (o.ticket - 1) // EPOCH
                            ins.then_inc(sems[(o.eng, ep)], 1)
                    if engname == SP:
                        for j in final_wait_ops:
                            d = ops[j]
                            if d.is_dma:
                                e.wait_ge(dsems[(d.eng, d.dslot)], d.dtarget)
                            else:
                                ep = (d.ticket - 1) // EPOCH
                                e.wait_ge(sems[(d.eng, ep)], d.ticket - ep * EPOCH)
                return body

            for engname in (SP, ACT, POOL, DVE, PE):
                if engname in per_eng or engname == SP:
                    getattr(block, engname)(run_engine(engname))


def _kp(w):
    n = w.shape[1]
    return np.ascontiguousarray(w.reshape(8, 128, n).transpose(1, 0, 2).reshape(128, 8 * n))


def _pc(v):
    return np.ascontiguousarray(v.reshape(-1, 128).T)


class Pack:
    def __init__(self):
        self.parts = []
        self.off = {}
        self.n = 0

    def add(self, name, arr):
        arr = np.asarray(arr, np.float32)
        assert arr.shape[0] == 128, (name, arr.shape)
        arr = arr.reshape(128, -1)
        self.off[name] = (self.n, arr.shape[1])
        self.parts.append(arr)
        self.n += arr.shape[1]

    def cat(self):
        return np.ascontiguousarray(np.concatenate(self.parts, axis=1))


EVEN_GROUPS = {}
for j in range(4):
    EVEN_GROUPS[f"fq{j}"] = (128 * j, 128)
    EVEN_GROUPS[f"fk{j}"] = (512 + 128 * j, 128)
EVEN_GROUPS["fv"] = (1024, 512)
EVEN_GROUPS["ff"] = (1536, 8)
for h in range(4):
    EVEN_GROUPS[f"gq{h}"] = (1544 + 128 * h, 128)
    EVEN_GROUPS[f"gk{h}"] = (2056 + 128 * h, 128)
    EVEN_GROUPS[f"gv{h}"] = (2568 + 128 * h, 128)
EVEN_GROUPS["gab"] = (3080, 8)
EVEN_GROUPS["gg"] = (3088, 512)
ODD_GROUPS = {}
for h in range(4):
    ODD_GROUPS[f"hq{h}"] = (128 * h, 128)
    ODD_GROUPS[f"hf{h}"] = (512 + 128 * h, 128)
ODD_GROUPS["hi"] = (1024, 512)
ODD_GROUPS["hg"] = (1536, 512)
ODD_GROUPS["mz"] = (2048, 512)
for c in range(8):
    ODD_GROUPS[f"mx{c}"] = (2560 + 128 * c, 128)
ODD_GROUPS["mdt"] = (3584, 8)


def weight_layout():
    lay = []
    for l in range(2):
        groups = EVEN_GROUPS if l == 0 else ODD_GROUPS
        for name, (c0, n) in groups.items():
            lay.append((f"in{l}_{name}", 8 * n))
        lay.append((f"out{l}", 8 * 1024))
        for j in range(NFF):
            lay.append((f"up{l}_{j}", 8 * 256))
        for hf in range(2):
            lay.append((f"down{l}_{hf}", NFF * 512))
    return lay


def host_pack(inp):
    W = Pack()
    for l in range(2):
        groups = EVEN_GROUPS if l == 0 else ODD_GROUPS
        w_in = inp["even_w_in"] if l == 0 else inp["odd_w_in"]
        for name, (c0, n) in groups.items():
            W.add(f"in{l}_{name}", _kp(w_in[:, c0:c0 + n]))
        W.add(f"out{l}", _kp(inp["w_out"][l]))
        wu = inp["ffn_w_up"][l]
        for j in range(NFF):
            t = np.concatenate([wu[:, j * 128:(j + 1) * 128], wu[:, DFF + j * 128:DFF + (j + 1) * 128]], axis=1)
            W.add(f"up{l}_{j}", _kp(t))
        wd = inp["ffn_w_down"][l]
        wd = wd.reshape(NFF, 128, 1024).transpose(1, 0, 2)
        for hf in range(2):
            W.add(f"down{l}_{hf}", wd[:, :, hf * 512:(hf + 1) * 512])
    Pp = Pack()
    rep = lambda v: np.broadcast_to(np.asarray(v, np.float32).reshape(1, -1), (128, np.asarray(v).size))
    for l in range(2):
        cw = inp["ffn_conv_w"][l]
        Pp.add(f"fcw{l}", np.stack([_pc(cw[k]) for k in range(3)], axis=2))
        Pp.add(f"fcb{l}", _pc(inp["ffn_conv_b"][l]))
    Pp.add("gcw", np.stack([_pc(inp["gdn_conv_w"][k]) for k in range(4)], axis=2))
    Pp.add("mcw", np.stack([_pc(inp["m2_conv_w"][k]) for k in range(4)], axis=2))
    Pp.add("mcb", _pc(inp["m2_conv_b"]))
    Pp.add("lb0", _pc(inp["hgrn_lb_logits"][0]))
    Pp.add("lb1", _pc(inp["hgrn_lb_logits"][1]))
    Pp.add("foxb", np.broadcast_to(inp["fox_f_bias"].reshape(8, 1), (8, 1)).repeat(16, axis=0).reshape(128, 1))
    Pp.add("foxb8", np.concatenate([inp["fox_f_bias"].reshape(8, 1), np.zeros((120, 1), np.float32)], axis=0))
    Pp.add("gAlog", rep(inp["gdn_A_log"]))
    Pp.add("gdtb", rep(inp["gdn_dt_bias"]))
    Pp.add("mAlog", rep(inp["m2_A_log"]))
    Pp.add("mdtb", rep(inp["m2_dt_bias"]))
    Pp.add("mD", rep(inp["m2_D"]))
    Pp.add("gng", rep(inp["gdn_norm_gain"]))
    Pp.add("hng", rep(inp["hgrn_norm_gain"]))
    Pp.add("mng", rep(inp["m2_norm_gain"]))
    G = np.broadcast_to(inp["norm_gains"].reshape(1, 8, 1024), (128, 8, 1024))
    return W, Pp, np.ascontiguousarray(G.reshape(128, 8 * 1024))


class Builder:
    def __init__(self, S, NSEQ, woff, poff, wtot, ptot, layers=(0, 1), mixers=True, dbg=None):
        self.S, self.NSEQ = S, NSEQ
        self.NT = S // 128
        self.NB = S // 512
        self.woff, self.poff = woff, poff
        self.layers = layers
        self.mixers = mixers
        self.dbg = dbg or {}
        self.uid = 0
        nc = self.nc = bass.Bass("TRN2", target_bir_lowering=False)
        self.P = Prog(nc)
        self.x_d = nc.dram_tensor("x", [NSEQ, S, D], F32, kind="ExternalInput").ap()
        self.w_d = nc.dram_tensor("wbig", [128, wtot], F32, kind="ExternalInput").ap()
        self.pp_d = nc.dram_tensor("pp", [128, ptot], F32, kind="ExternalInput").ap()
        self.gn_d = nc.dram_tensor("gains", [128, 8 * 1024], F32, kind="ExternalInput").ap()
        self.y_d = nc.dram_tensor("y", [NSEQ, S, D], F32, kind="ExternalOutput").ap()
        self.wbf = nc.dram_tensor("wbf", [128, wtot], BF16, kind="Internal").ap()
        self.xspill = nc.dram_tensor("xspill", [15, 128, D], F32, kind="Internal").ap()
        self.wtot, self.ptot = wtot, ptot
        self.wres = {name: Res() for name in woff}
        self.wcast_done = set()
        self.out_dmas = []
        self.dbg_out = {}

    def sb(self, st, shape, dt, name="t"):
        self.uid += 1
        return st.enter_context(self.nc.sbuf_tensor(f"{name}_{self.uid}", list(shape), dt))

    def ps(self, st, shape, dt, name="p"):
        self.uid += 1
        return st.enter_context(self.nc.psum_tensor(f"{name}_{self.uid}", list(shape), dt))

    def wseg(self, name):
        o, n = self.woff[name]
        return self.wbf[:, o:o + n], self.wres[name]

    def pcol(self, name):
        o, n = self.poff[name]
        return self.PP[:, o:o + n]

    def dbg_dump(self, key, ap_sbuf, shape, res):
        if key not in self.dbg:
            return
        dt = ap_sbuf.dtype
        t = self.nc.dram_tensor(f"dbg_{key}", list(shape), dt, kind="ExternalOutput").ap()
        self.out_dmas.append(self.P.dma(SP, t, ap_sbuf, reads=res))

    def cast_weights(self, prefixes):
        P = self.P
        for name, (o, n) in self.woff.items():
            if not any(name.startswith(p) for p in prefixes):
                continue
            if name in self.wcast_done:
                continue
            self.wcast_done.add(name)
            r = self.wres[name]
            c = 0
            while c < n:
                m = min(8192, n - c)
                P.dma(POOL, self.wbf[:, o + c:o + c + m], self.w_d[:, o + c:o + c + m], writes=[r], nobar=True)
                c += m

    def rstd_from_ss(self, ss_ap, rs_ap, res_ss, res_rs, n):
        P = self.P
        P.op(ACT, lambda e: e.activation(out=rs_ap, in_=ss_ap, func=AF.Ln, scale=1.0 / n, bias=self.eps_t[:, 0:1]), [res_ss], [res_rs])
        P.op(ACT, lambda e: e.activation(out=rs_ap, in_=rs_ap, func=AF.Exp, scale=-0.5), [res_rs], [res_rs])

    def prenorm_tile(self, st_unused, t, gain_idx, dst_ap, dst_res):
        P = self.P
        X, rX = self.X, self.rX
        k = t % 2
        junk = self.junk_bf[0]
        ss, rss = self.ss[k], self.r_ss[k]
        hn, rhn = self.hn[k], self.r_hn[k]
        pt, rpt = self.pt[k], self.r_pt[k]
        gB, rgB = self.gainB[gain_idx % 2], self.r_gainB[gain_idx % 2]
        P.op(ACT, lambda e: e.activation(out=hn[:], in_=X[:, t, :], func=AF.Square, accum_out=ss[:, 0:1]), [rX[t]], [rss, rhn])
        self.rstd_from_ss(ss[:, 0:1], ss[:, 1:2], rss, rss, D)
        P.op(DVE, lambda e: e.scalar_tensor_tensor(out=hn[:], in0=X[:, t, :], scalar=ss[:, 1:2], in1=gB[:], op0=ALU.mult, op1=ALU.mult), [rX[t], rss, rgB], [rhn])

        def tr(e):
            ins = None
            for c in range(8):
                ins = e.transpose(pt[:, c, :], hn[:, c * 128:(c + 1) * 128], self.ident_bf[:])
            return ins
        P.op(PE, tr, [rhn], [rpt])
        P.op(ACT, lambda e: e.activation(out=dst_ap, in_=pt[:], func=AF.Copy), [rpt], [dst_res])

    def load_gain(self, l, i):
        idx = l * 4 + i
        gB, rgB = self.gainB[idx % 2], self.r_gainB[idx % 2]
        self.P.dma(SP, gB[:], self.gn_d[:, idx * 1024:(idx + 1) * 1024], writes=[rgB])
        return idx

    def postnorm_residual(self, t, pacc, rpacc, gain_idx, junk=None, evac=None):
        P = self.P
        X, rX = self.X, self.rX
        k = t % 2
        ss, rss = self.ss[k], self.r_ss[k]
        tmp, rtmp = self.tmp32[0], self.r_tmp32[0]
        jb, rjb = junk if junk is not None else (tmp, rtmp)
        gB, rgB = self.gainB[gain_idx % 2], self.r_gainB[gain_idx % 2]
        if evac is not None:
            cp, rcp = evac
            P.op(DVE, lambda e: e.tensor_copy(out=cp[:], in_=pacc[:]), [], [rcp, rpacc])
            P.op(ACT, lambda e: e.activation(out=jb[:], in_=cp[:], func=AF.Square, accum_out=ss[:, 0:1]), [rcp], [rss, rjb])
            self.rstd_from_ss(ss[:, 0:1], ss[:, 1:2], rss, rss, D)
            P.op(DVE, lambda e: e.scalar_tensor_tensor(out=tmp[:], in0=cp[:], scalar=ss[:, 1:2], in1=gB[:], op0=ALU.mult, op1=ALU.mult), [rss, rgB, rcp], [rtmp])
            P.op(POOL, lambda e: e.tensor_tensor(out=X[:, t, :], in0=X[:, t, :], in1=tmp[:], op=ALU.add), [rtmp, rX[t]], [rX[t]])
            return
        P.op(ACT, lambda e: e.activation(out=jb[:], in_=pacc[:], func=AF.Square, accum_out=ss[:, 0:1]), [], [rss, rjb, rpacc])
        self.rstd_from_ss(ss[:, 0:1], ss[:, 1:2], rss, rss, D)
        P.op(DVE, lambda e: e.scalar_tensor_tensor(out=tmp[:], in0=pacc[:], scalar=ss[:, 1:2], in1=gB[:], op0=ALU.mult, op1=ALU.mult), [rss, rgB], [rtmp, rpacc])
        P.op(POOL, lambda e: e.tensor_tensor(out=X[:, t, :], in0=X[:, t, :], in1=tmp[:], op=ALU.add), [rtmp, rX[t]], [rX[t]])

    def ffn_phase(self, l):
        P = self.P
        S, NT, NB = self.S, self.NT, self.NB
        with contextlib.ExitStack() as st:
            HTb = self.sb(st, [128, 8, 512], BF16, "HTb"); rHTb = [Res() for _ in range(4)]
            AT = self.sb(st, [128, NFF, 512], BF16, "AT"); rAT = [Res() for _ in range(NFF)]
            Wd = [self.sb(st, [128, NFF, 512], BF16, "Wd") for _ in range(2)]; rWd = [Res(), Res()]
            NWU = 4
            Wu = [self.sb(st, [128, 8, 256], BF16, "Wu") for _ in range(NWU)]; rWu = [Res() for _ in range(NWU)]
            U = [[self.sb(st, [128, 514], F32, "U") for _ in range(2)] for _ in range(2)]
            rU = [[Res(), Res()], [Res(), Res()]]
            Y = [[self.sb(st, [128, 512], F32, "Y") for _ in range(2)] for _ in range(2)]
            rY = [[Res(), Res()], [Res(), Res()]]
            HALO = self.sb(st, [128, 2 * NFF, 2], F32, "HALO"); rH = [Res() for _ in range(2 * NFF)]
            pu = [[self.ps(st, [128, 512], F32, "pu") for _ in range(2)] for _ in range(2)]
            rpu = [[Res(), Res()], [Res(), Res()]]
            pacc = [self.ps(st, [128, 1024], F32, "pacc") for _ in range(1)]; rpacc = [Res()]
            pcp = [self.sb(st, [128, 1024], F32, "pcp") for _ in range(2)]; rpcp = [Res(), Res()]
            cwo, _ = self.poff[f"fcw{l}"]
            cbo, _ = self.poff[f"fcb{l}"]
            PP = self.PP
            g_pre = self.load_gain(l, 2)
            nxt = [x for x in self.layers if x > l]
            if nxt:
                self.cast_weights([f"in{nxt[0]}_"])
            P.op(POOL, lambda e: e.memset(HALO[:], 0.0), [], rH)
            for hf in range(2):
                seg, rs = self.wseg(f"down{l}_{hf}")
                P.dma(SP, Wd[hf][:], seg.rearrange("p (j c) -> p j c", j=NFF), reads=[rs], writes=[rWd[hf]])
            for b in range(NB):
                for tt in range(4):
                    t = b * 4 + tt
                    self.prenorm_tile(None, t, g_pre, HTb[:, :, tt * 128:(tt + 1) * 128], rHTb[tt])
                if b == 0:
                    g_post = self.load_gain(l, 3)
                for j in range(NFF):
                    k = j % 2
                    kw = j % NWU
                    seg, rs = self.wseg(f"up{l}_{j}")
                    P.dma(SP, Wu[kw][:], seg.rearrange("p (k c) -> p k c", k=8), reads=[rs], writes=[rWu[kw]])
                    for gu in range(2):
                        def mm(e, gu=gu, k=k, kw=kw):
                            ins = None
                            for kk in range(8):
                                ins = e.matmul(pu[gu][k][:], lhsT=Wu[kw][:, kk, gu * 128:(gu + 1) * 128], rhs=HTb[:, kk, :],
                                               start=(kk == 0), stop=(kk == 7))
                            return ins
                        P.op(PE, mm, [rWu[kw]] + rHTb, [rpu[gu][k]])
                        u, ru = U[gu][k], rU[gu][k]
                        y, ry = Y[gu][k], rY[gu][k]
                        hidx = gu * NFF + j
                        fidx = gu * NFF + j
                        w0 = PP[:, cwo + fidx * 3 + 0:cwo + fidx * 3 + 1]
                        w1 = PP[:, cwo + fidx * 3 + 1:cwo + fidx * 3 + 2]
                        w2 = PP[:, cwo + fidx * 3 + 2:cwo + fidx * 3 + 3]
                        bb = PP[:, cbo + fidx:cbo + fidx + 1]
                        P.op(POOL, lambda e, u=u, hidx=hidx: e.tensor_copy(out=u[:, 0:2], in_=HALO[:, hidx, :]), [rH[hidx]], [ru])
                        if gu == 0:
                            P.op(ACT, lambda e, u=u, gu=gu, k=k: e.activation(out=u[:, 2:514], in_=pu[gu][k][:], func=AF.Copy), [], [ru, rpu[gu][k]])
                        else:
                            P.op(DVE, lambda e, u=u, gu=gu, k=k: e.tensor_copy(out=u[:, 2:514], in_=pu[gu][k][:]), [], [ru, rpu[gu][k]])
                        P.op(ACT, lambda e, y=y, gu=gu, k=k, w2=w2, bb=bb: e.activation(out=y[:], in_=pu[gu][k][:], func=AF.Identity, scale=w2, bias=bb), [], [ry, rpu[gu][k]])
                        P.op(POOL, lambda e, u=u, hidx=hidx: e.tensor_copy(out=HALO[:, hidx, :], in_=u[:, 512:514]), [ru], [rH[hidx]])
                        P.op(DVE, lambda e, y=y, u=u, w0=w0: e.scalar_tensor_tensor(out=y[:], in0=u[:, 0:512], scalar=w0, in1=y[:], op0=ALU.mult, op1=ALU.add), [ru, ry], [ry])
                        P.op(DVE, lambda e, y=y, u=u, w1=w1: e.scalar_tensor_tensor(out=y[:], in0=u[:, 1:513], scalar=w1, in1=y[:], op0=ALU.mult, op1=ALU.add), [ru, ry], [ry])
                    yg, yu = Y[0][k], Y[1][k]
                    P.op(ACT, lambda e, yg=yg: e.activation(out=yg[:], in_=yg[:], func=AF.Silu), [rY[0][k]], [rY[0][k]])
                    P.op(POOL, lambda e, yg=yg, yu=yu, j=j: e.tensor_tensor(out=AT[:, j, :], in0=yg[:], in1=yu[:], op=ALU.mult), [rY[0][k], rY[1][k]], [rAT[j]])
                for tt in range(4):
                    t = b * 4 + tt
                    JG = 6
                    for hf in range(2):
                        for j0 in range(0, NFF, JG):
                            j1 = min(NFF, j0 + JG)
                            def mmd(e, tt=tt, hf=hf, j0=j0, j1=j1):
                                ins = None
                                for j in range(j0, j1):
                                    ins = e.matmul(pacc[0][:, hf * 512:(hf + 1) * 512], lhsT=AT[:, j, tt * 128:(tt + 1) * 128], rhs=Wd[hf][:, j, :],
                                                   start=(j == 0), stop=(j == NFF - 1))
                                return ins
                            P.op(PE, mmd, rAT[j0:j1] + [rWd[hf]], [rpacc[0]])
                    self.postnorm_residual(t, pacc[0], rpacc[0], g_post, evac=(pcp[t % 2], rpcp[t % 2]))

    def outproj_phase(self, l, MT, rMT, st):
        P = self.P
        Wo = self.sb(st, [128, 8, 1024], BF16, "Wo"); rWo = Res()
        paccs = [self.ps(st, [128, 1024], F32, "pacc_o") for _ in range(2)]; rpaccs = [Res(), Res()]
        seg, rs = self.wseg(f"out{l}")
        P.dma(SP, Wo[:], seg.rearrange("p (k c) -> p k c", k=8), reads=[rs], writes=[rWo])
        g_post = self.load_gain(l, 1)
        for t in range(self.NT):
            pacc, rpacc = paccs[t % 2], rpaccs[t % 2]
            def mm(e, t=t, pacc=pacc):
                ins = None
                for hf in range(2):
                    for kk in range(8):
                        ins = e.matmul(pacc[:, hf * 512:(hf + 1) * 512], lhsT=MT[:, kk, t * 128:(t + 1) * 128], rhs=Wo[:, kk, hf * 512:(hf + 1) * 512],
                                       start=(kk == 0), stop=(kk == 7))
                return ins
            P.op(PE, mm, [rWo, rMT[t]], [rpacc])
            self.postnorm_residual(t, pacc, rpacc, g_post, junk=(self.hn[t % 2], self.r_hn[t % 2]))

    def mixer_phase(self, l):
        P = self.P
        S, NT = self.S, self.NT
        with contextlib.ExitStack() as st:
            HT = self.sb(st, [128, 8, S], BF16, "HT"); rHT = [Res() for _ in range(NT)]
            MT = self.sb(st, [128, 8, S], BF16, "MT"); rMT = [Res() for _ in range(NT)]
            g_pre = self.load_gain(l, 0)
            for t in range(NT):
                self.prenorm_tile(None, t, g_pre, HT[:, :, t * 128:(t + 1) * 128], rHT[t])
            self.dbg_dump(f"HT{l}", HT[:], [128, 8, S], rHT)
            self.cast_weights([f"out{l}", f"up{l}_", f"down{l}_"])
            if not self.mixers:
                for t in range(NT):
                    P.op(DVE, lambda e, t=t: e.tensor_copy(out=MT[:, :, t * 128:(t + 1) * 128], in_=HT[:, :, t * 128:(t + 1) * 128]), [rHT[t]], [rMT[t]])
            elif l == 0:
                only = getattr(self, "only", "both")
                self.gdn_nx = 15 if NT >= 15 else (7 if NT >= 7 else 0)
                self.gdn_xsub = [[] for _ in range(self.gdn_nx)]
                self.gdn_spill_ops = []
                for t in range(self.gdn_nx):
                    self.gdn_spill_ops.append(P.dma(SP, self.xspill[t], self.X[:, t, :], reads=[self.rX[t]]))
                if only != "both":
                    for t in range(NT):
                        P.op(POOL, lambda e, t=t: e.memset(MT[:, :, t * 128:(t + 1) * 128], 0.0), [], [rMT[t]])
                if only in ("both", "first"):
                    with contextlib.ExitStack() as st2:
                        self.fox(HT, rHT, MT, rMT, st2)
                    P.barrier(self.bar_scratch[:])
                if only in ("both", "second"):
                    with contextlib.ExitStack() as st2:
                        self.gdn(HT, rHT, MT, rMT, st2)
                    for t in range(self.gdn_nx):
                        P.dma(SP, self.X[:, t, :], self.xspill[t], reads=[], writes=[self.rX[t]] + self.gdn_xsub[t])
                    P.barrier(self.bar_scratch[:])
            else:
                only = getattr(self, "only", "both")
                if only != "both":
                    for t in range(NT):
                        P.op(POOL, lambda e, t=t: e.memset(MT[:, :, t * 128:(t + 1) * 128], 0.0), [], [rMT[t]])
                if only in ("both", "first"):
                    with contextlib.ExitStack() as st2:
                        self.hgrn(HT, rHT, MT, rMT, st2)
                    P.barrier(self.bar_scratch[:])
                if only in ("both", "second"):
                    with contextlib.ExitStack() as st2:
                        self.ssd(HT, rHT, MT, rMT, st2)
                    P.barrier(self.bar_scratch[:])
            self.dbg_dump(f"MT{l}", MT[:], [128, 8, S], rMT)
            with contextlib.ExitStack() as st2:
                self.outproj_phase(l, MT, rMT, st2)


    def build(self):
        P = self.P
        nc = self.nc
        S, NT = self.S, self.NT
        with contextlib.ExitStack() as st:
            self.X = self.sb(st, [128, NT, D], F32, "X"); self.rX = [Res() for _ in range(NT)]
            self.PP = self.sb(st, [128, self.ptot], F32, "PP"); self.rPP = Res()
            self.gainB = [self.sb(st, [128, D], F32, "gainB") for _ in range(2)]; self.r_gainB = [Res(), Res()]
            self.junk_bf = [self.sb(st, [128, D], BF16, "junk") for _ in range(1)]
            self.hn = [self.sb(st, [128, D], BF16, "hn") for _ in range(2)]; self.r_hn = [Res(), Res()]
            self.tmp32 = [self.sb(st, [128, D], F32, "tmp32") for _ in range(1)]; self.r_tmp32 = [Res()]
            self.ss = [self.sb(st, [128, 2], F32, "ss") for _ in range(2)]; self.r_ss = [Res(), Res()]
            self.ident_bf = self.sb(st, [128, 128], BF16, "ident"); r_id = Res()
            self.ident_f = self.sb(st, [128, 128], F32, "identf")
            self.ones_f = self.sb(st, [128, 128], F32, "onesf")
            self.ones_bf = self.sb(st, [128, 128], BF16, "onesbf")
            self.tri_incl = self.sb(st, [128, 128], F32, "tri_incl")
            self.tri_incl_bf = self.sb(st, [128, 128], BF16, "tri_incl_bf")
            self.neg_incl = self.sb(st, [128, 128], F32, "neg_incl")
            self.neg_strict = self.sb(st, [128, 128], F32, "neg_strict")
            self.eps_t = self.sb(st, [128, 1], F32, "eps")
            self.bar_scratch = self.sb(st, [128, 1], F32, "bar")
            self.pt = [self.ps(st, [128, 8, 128], BF16, "pt") for _ in range(2)]; self.r_pt = [Res(), Res()]
            rc = self.rconst = Res()
            P.op(POOL, lambda e: e.memset(self.ident_bf[:], 0.0), [], [rc], nobar=True)
            P.op(POOL, lambda e: e.affine_select(out=self.ident_bf[:], in_=self.ident_bf[:], pattern=[[-1, 128]], compare_op=ALU.not_equal, fill=1.0, base=0, channel_multiplier=1), [rc], [rc], nobar=True)
            P.op(POOL, lambda e: e.memset(self.ident_f[:], 0.0), [rc], [rc], nobar=True)
            P.op(POOL, lambda e: e.affine_select(out=self.ident_f[:], in_=self.ident_f[:], pattern=[[-1, 128]], compare_op=ALU.not_equal, fill=1.0, base=0, channel_multiplier=1), [rc], [rc], nobar=True)
            P.op(POOL, lambda e: e.memset(self.ones_f[:], 1.0), [rc], [rc], nobar=True)
            P.op(POOL, lambda e: e.memset(self.ones_bf[:], 1.0), [rc], [rc], nobar=True)
            P.op(POOL, lambda e: e.memset(self.tri_incl[:], 1.0), [rc], [rc], nobar=True)
            P.op(POOL, lambda e: e.affine_select(out=self.tri_incl[:], in_=self.tri_incl[:], pattern=[[1, 128]], compare_op=ALU.is_ge, fill=0.0, base=0, channel_multiplier=-1), [rc], [rc], nobar=True)
            P.op(POOL, lambda e: e.tensor_copy(out=self.tri_incl_bf[:], in_=self.tri_incl[:]), [rc], [rc], nobar=True)
            P.op(POOL, lambda e: e.memset(self.neg_incl[:], 0.0), [rc], [rc], nobar=True)
            P.op(POOL, lambda e: e.affine_select(out=self.neg_incl[:], in_=self.neg_incl[:], pattern=[[1, 128]], compare_op=ALU.is_ge, fill=-30000.0, base=0, channel_multiplier=-1), [rc], [rc], nobar=True)
            P.op(POOL, lambda e: e.memset(self.neg_strict[:], 0.0), [rc], [rc], nobar=True)
            P.op(POOL, lambda e: e.affine_select(out=self.neg_strict[:], in_=self.neg_strict[:], pattern=[[1, 128]], compare_op=ALU.is_gt, fill=-30000.0, base=0, channel_multiplier=-1), [rc], [rc], nobar=True)
            P.op(POOL, lambda e: e.memset(self.eps_t[:], EPS), [rc], [rc], nobar=True)
            P.dma(SP, self.PP[:], self.pp_d, writes=[self.rPP])
            P.barrier(self.bar_scratch[:])
            self.cast_weights([f"in{self.layers[0]}_"])
            for s in range(self.NSEQ):
                for t in range(NT):
                    P.dma(SP, self.X[:, t, :], self.x_d[s, t * 128:(t + 1) * 128, :], writes=[self.rX[t]])
                for l in self.layers:
                    self.mixer_phase(l)
                    P.barrier(self.bar_scratch[:])
                    self.ffn_phase(l)
                    P.barrier(self.bar_scratch[:])
                for t in range(NT):
                    self.out_dmas.append(P.dma(SP, self.y_d[s, t * 128:(t + 1) * 128, :], self.X[:, t, :], reads=[self.rX[t]]))
            P.emit(final_wait_ops=self.out_dmas)
        return nc


def make_builder(S, NSEQ, layers=(0, 1), mixers=True, dbg=None, poff=None, ptot=None):
    woff = {}
    n = 0
    for name, sz in weight_layout():
        woff[name] = (n, sz)
        n += sz
    return Builder(S, NSEQ, woff, poff, n, ptot, layers=layers, mixers=mixers, dbg=dbg)


def kernel(**inputs):
    inp = {k: np.asarray(v) for k, v in inputs.items()}
    W, Pp, G = host_pack(inp)
    wbig = W.cat()
    pp = Pp.cat()
    B, S, _ = inp["x"].shape
    nseq = B // NCORES
    b = Builder(S, nseq, W.off, Pp.off, W.n, Pp.n)
    nc = b.build()
    x = np.ascontiguousarray(inp["x"], dtype=np.float32)
    in_maps = [{"x": x[c * nseq:(c + 1) * nseq], "wbig": wbig, "pp": pp, "gains": G} for c in range(NCORES)]
    res = run_bass_kernel_spmd(nc, in_maps, core_ids=list(range(NCORES)))
    out = np.concatenate([r["y"] for r in res.results], axis=0)
    return out.astype(np.float32)


def _fox(self, HT, rHT, MT, rMT, st):
    P = self.P
    S, NT, NB = self.S, self.NT, self.NB
    sb, ps = self.sb, self.ps
    Vaug = sb(st, [128, NT, 2, 128], BF16, "Vaug"); rV = [Res() for _ in range(NT)]
    qz = [sb(st, [128, S], BF16, "qz") for _ in range(2)]; rqz = [[Res() for _ in range(NB)] for _ in range(2)]
    kT = sb(st, [128, S], BF16, "kT"); rk = [Res() for _ in range(NB)]
    Wq = sb(st, [128, 8, 128], BF16, "Wq"); rWq = Res()
    Wk = sb(st, [128, 8, 128], BF16, "Wk"); rWk = Res()
    Wv = sb(st, [128, 8, 128], BF16, "Wv"); rWv = Res()
    Wf = sb(st, [128, 8, 8], BF16, "Wf"); rWf = Res()
    ebuf = sb(st, [8, 512], F32, "ebuf"); reb = Res()
    CTb = [sb(st, [8, 512], F32, "CTb") for _ in range(2)]; rCT = [Res(), Res()]
    CrefB = sb(st, [8, NB, 128], F32, "CrefB"); rCr = Res()
    negb = sb(st, [8, 1], F32, "negb"); rnegb = Res()
    Ctok = sb(st, [128, NT, 8], F32, "Ctok"); rCtok = Res()
    nb = sb(st, [128, NB, NT, 8], F32, "nb"); rnb = Res()
    PT = [sb(st, [128, 512], BF16, "PT") for _ in range(3)]; rPT = [Res() for _ in range(3)]
    rec = sb(st, [64, 512], F32, "rec"); rrec = Res()
    pS = [ps(st, [128, 512], F32, "pS") for _ in range(2)]; rpS = [Res(), Res()]
    po = [ps(st, [128, 512], F32, "po") for _ in range(2)]; rpo = [Res(), Res()]
    pm = ps(st, [128, 512], F32, "pm"); rpm = Res()
    pC = ps(st, [128, NT * 8 + NB * 8], F32, "pC"); rpC = Res()
    PP = self.PP
    fo, _ = self.poff["foxb8"]

    seg, rs = self.wseg("in0_ff")
    P.dma(SP, Wf[:], seg.rearrange("p (k c) -> p k c", k=8), reads=[rs], writes=[rWf])
    P.op(DVE, lambda e: e.tensor_scalar(out=negb[:], in0=PP[0:8, fo:fo + 1], scalar1=-1.0, scalar2=None, op0=ALU.mult), [self.rPP], [rnegb])
    P.op(POOL, lambda e: e.memset(Vaug[:, :, :, 64:128], 1.0), [], rV)
    for hh_ in range(2):
        P.op(POOL, lambda e, hh_=hh_: e.memset(qz[hh_][:], 0.0), [], rqz[hh_])
    for b in range(NB):
        def mmf(e, b=b):
            ins = None
            for kk in range(8):
                ins = e.matmul(pm[0:8, :], lhsT=Wf[:, kk, :], rhs=HT[:, kk, b * 512:(b + 1) * 512], start=(kk == 0), stop=(kk == 7))
            return ins
        P.op(PE, mmf, [rWf] + rHT[b * 4:(b + 1) * 4], [rpm])
        P.op(ACT, lambda e: e.activation(out=ebuf[:], in_=pm[0:8, :], func=AF.Exp, bias=negb[:, 0:1], scale=-1.0), [rpm, rnegb], [reb])
        P.op(ACT, lambda e: e.activation(out=ebuf[:], in_=ebuf[:], func=AF.Ln, bias=1.0), [reb], [reb])
        cur, prev = CTb[b % 2], CTb[(b + 1) % 2]
        rcur, rprev = rCT[b % 2], rCT[(b + 1) % 2]
        P.op(POOL, lambda e, cur=cur: e.memset(cur[:], 1.0), [], [rcur])
        if b == 0:
            P.op(DVE, lambda e, cur=cur: e.tensor_tensor_scan(out=cur[:], data0=cur[:], data1=ebuf[:], initial=0.0, op0=ALU.mult, op1=ALU.add), [reb, rcur], [rcur])
        else:
            P.op(DVE, lambda e, cur=cur, prev=prev: e.tensor_tensor_scan(out=cur[:], data0=cur[:], data1=ebuf[:], initial=prev[:, 511:512], op0=ALU.mult, op1=ALU.add), [reb, rcur, rprev], [rcur])
        P.op(DVE, lambda e, cur=cur, b=b: e.tensor_copy(out=CrefB[:, b, :], in_=cur[:, 256:257].to_broadcast([8, 128])), [rcur], [rCr])
        def mmc(e, cur=cur, b=b):
            ins = None
            for tt in range(4):
                t = b * 4 + tt
                ins = e.matmul(pC[:, t * 8:(t + 1) * 8], lhsT=cur[0:8, tt * 128:(tt + 1) * 128], rhs=self.ident_f[0:8, 0:8], start=True, stop=True)
            ins = e.matmul(pC[:, NT * 8 + b * 8:NT * 8 + (b + 1) * 8], lhsT=CrefB[0:8, b, :], rhs=self.ident_f[0:8, 0:8], start=True, stop=True)
            return ins
        P.op(PE, mmc, [rcur, rCr, self.rconst], [rpC])
    P.op(ACT, lambda e: e.activation(out=Ctok[:], in_=pC[:, 0:NT * 8].rearrange("p (t h) -> p t h", h=8), func=AF.Copy), [rpC], [rCtok])
    for i in range(NB):
        P.op(DVE, lambda e, i=i: e.tensor_tensor(out=nb[:, i, :, :], in0=Ctok[:], in1=pC[:, NT * 8 + i * 8:NT * 8 + (i + 1) * 8].unsqueeze(1).to_broadcast([128, NT, 8]), op=ALU.subtract), [rCtok, rpC], [rnb])
    self.dbg_dump("foxC", Ctok[:], [128, NT, 8], [rCtok])

    for pr in range(4):
        seg, rs = self.wseg(f"in0_fq{pr}")
        P.dma(SP, Wq[:], seg.rearrange("p (k c) -> p k c", k=8), reads=[rs], writes=[rWq])
        seg, rs = self.wseg(f"in0_fk{pr}")
        P.dma(SP, Wk[:], seg.rearrange("p (k c) -> p k c", k=8), reads=[rs], writes=[rWk])
        seg, rs = self.wseg("in0_fv")
        P.dma(SP, Wv[:], seg.rearrange("p (k c) -> p k c", k=8)[:, :, pr * 128:(pr + 1) * 128], reads=[rs], writes=[rWv])
        for b in range(NB):
            for which in ("q", "k"):
                W_, rW_ = (Wq, rWq) if which == "q" else (Wk, rWk)
                pb, rpb = pS[b % 2], rpS[b % 2]
                def mmp(e, W_=W_, pb=pb, b=b):
                    ins = None
                    for kk in range(8):
                        ins = e.matmul(pb[:], lhsT=W_[:, kk, :], rhs=HT[:, kk, b * 512:(b + 1) * 512], start=(kk == 0), stop=(kk == 7))
                    return ins
                P.op(PE, mmp, [rW_] + rHT[b * 4:(b + 1) * 4], [rpb])
                if which == "q":
                    P.op(ACT, lambda e, pb=pb, b=b: e.activation(out=qz[0][0:64, b * 512:(b + 1) * 512], in_=pb[0:64, :], func=AF.Copy), [], [rqz[0][b], rpb])
                    P.op(ACT, lambda e, pb=pb, b=b: e.activation(out=qz[1][64:128, b * 512:(b + 1) * 512], in_=pb[64:128, :], func=AF.Copy), [], [rqz[1][b], rpb])
                else:
                    P.op(DVE, lambda e, pb=pb, b=b: e.tensor_copy(out=kT[:, b * 512:(b + 1) * 512], in_=pb[:]), [], [rk[b], rpb])
        for t in range(NT):
            def mmv(e, t=t):
                ins = None
                for kk in range(8):
                    ins = e.matmul(pm[:, 0:128], lhsT=HT[:, kk, t * 128:(t + 1) * 128], rhs=Wv[:, kk, :], start=(kk == 0), stop=(kk == 7))
                return ins
            P.op(PE, mmv, [rWv, rHT[t]], [rpm])
            P.op(DVE, lambda e, t=t: e.tensor_copy(out=Vaug[:, t, :, 0:64], in_=pm[:, 0:128].rearrange("p (a b) -> p a b", a=2)), [rpm], [rV[t]])
        for hh in range(2):
            h = 2 * pr + hh
            base = hh * 64
            cnt = 0
            for i in range(NB):
                njb = 4 * i + 4
                pacc, rpacc = po[i % 2], rpo[i % 2]
                items = []
                for j in range(njb):
                    r = j - 4 * i
                    c0 = 128 * r if r > 0 else 0
                    items.append((j, r, c0))

                def emit_qk(j, r, c0, slot, i=i, base=base, h=h, hh=hh):
                    pb, rpb = pS[slot % 2], rpS[slot % 2]
                    pt_, rpt_ = PT[slot % 3], rPT[slot % 3]
                    P.op(PE, lambda e: e.matmul(pb[:, c0:512], lhsT=kT[:, j * 128:(j + 1) * 128],
                                                rhs=qz[hh][:, i * 512 + c0:(i + 1) * 512], start=True, stop=True),
                         [rk[j // 4], rqz[hh][i]], [rpb])
                    P.op(ACT, lambda e: e.activation(out=pt_[:, c0:512], in_=pb[:, c0:512], func=AF.Exp, bias=nb[:, i, j, h:h + 1], scale=0.125),
                         [rpb, rnb], [rpt_])
                    if r >= 0:
                        P.op(POOL, lambda e: e.tensor_tensor(out=pt_[:, c0:c0 + 128], in0=pt_[:, c0:c0 + 128], in1=self.tri_incl_bf[:], op=ALU.mult),
                             [rpt_, self.rconst], [rpt_])

                def emit_pv(j, r, c0, slot, first, last, pacc=pacc, rpacc=rpacc, hh=hh):
                    pt_, rpt_ = PT[slot % 3], rPT[slot % 3]
                    P.op(PE, lambda e: e.matmul(pacc[:, c0:512], lhsT=Vaug[:, j, hh, :], rhs=pt_[:, c0:512], start=first, stop=last),
                         [rV[j], rpt_], [rpacc])

                for idx, (j, r, c0) in enumerate(items):
                    emit_qk(j, r, c0, cnt + idx)
                    if idx >= 1:
                        pj, pr_, pc0 = items[idx - 1]
                        emit_pv(pj, pr_, pc0, cnt + idx - 1, idx - 1 == 0, False)
                pj, pr_, pc0 = items[-1]
                emit_pv(pj, pr_, pc0, cnt + len(items) - 1, len(items) == 1, True)
                cnt += len(items)
                P.op(DVE, lambda e, pacc=pacc: e.reciprocal(out=rec[:], in_=pacc[64:128, :]), [rpacc], [rrec])
                P.op(DVE, lambda e, pacc=pacc, i=i, base=base, pr=pr: e.tensor_tensor(out=MT[base:base + 64, pr, i * 512:(i + 1) * 512], in0=pacc[0:64, :], in1=rec[:], op=ALU.mult),
                     [rpacc, rrec], rMT[i * 4:(i + 1) * 4])


Builder.fox = _fox


def _proj_fm(self, W_, rW_, HT, rHT, b, pb, rpb):
    def mm(e):
        ins = None
        for kk in range(8):
            ins = e.matmul(pb[:], lhsT=W_[:, kk, :], rhs=HT[:, kk, b * 512:(b + 1) * 512], start=(kk == 0), stop=(kk == 7))
        return ins
    self.P.op(PE, mm, [rW_] + rHT[b * 4:(b + 1) * 4], [rpb])


def _proj_tm(self, W_ap, rW_, HT, rHT, t, pb_ap, rpb):
    def mm(e):
        ins = None
        for kk in range(8):
            ins = e.matmul(pb_ap, lhsT=HT[:, kk, t * 128:(t + 1) * 128], rhs=W_ap[:, kk, :], start=(kk == 0), stop=(kk == 7))
        return ins
    self.P.op(PE, mm, [rW_, rHT[t]], [rpb])


def _load_w(self, name, W_, rW_, cols=None, ncol=None):
    seg, rs = self.wseg(name)
    n = self.woff[name][1] // 8
    src = seg.rearrange("p (k c) -> p k c", k=8)
    if cols is not None:
        src = src[:, :, cols:cols + ncol]
    self.P.dma(SP, W_[:], src, reads=[rs], writes=[rW_])


Builder.proj_fm = _proj_fm
Builder.proj_tm = _proj_tm
Builder.load_w = _load_w


def _hgrn(self, HT, rHT, MT, rMT, st):
    P = self.P
    S, NT, NB = self.S, self.NT, self.NB
    sb, ps = self.sb, self.ps
    PP = self.PP
    Wq = sb(st, [128, 8, 128], BF16, "Wq"); rWq = Res()
    Wf = sb(st, [128, 8, 128], BF16, "Wf"); rWf = Res()
    Wg = sb(st, [128, 8, 128], BF16, "Wg"); rWg = Res()
    Wi = sb(st, [128, 8, 512], BF16, "Wi"); rWi = Res()
    vtok = sb(st, [128, NT, 512], BF16, "vtok"); rvt = [Res() for _ in range(NT)]
    qtT = sb(st, [128, S], BF16, "qtT"); rqt = [Res() for _ in range(NB)]
    ktT = sb(st, [128, S], BF16, "ktT"); rkt = [Res() for _ in range(NB)]
    qhT = sb(st, [128, S], BF16, "qhT"); rqh = [Res() for _ in range(NB)]
    class _V:
        def __init__(self, ap):
            self.ap = ap
        def __getitem__(self, k):
            return self.ap if (isinstance(k, slice) and k == slice(None)) else self.ap[k]
    A1 = _V(self.tmp32[0][:, 0:512]); rA1 = self.r_tmp32[0]
    A2 = _V(self.tmp32[0][:, 512:1024]); rA2 = self.r_tmp32[0]
    A3 = _V(self.gainB[0][:, 0:512]); rA3 = self.r_gainB[0]
    A4 = _V(self.gainB[0][:, 512:1024]); rA4 = self.r_gainB[0]
    E1 = _V(self.gainB[1][:, 0:512]); rE1 = self.r_gainB[1]
    E2 = _V(self.gainB[1][:, 512:1024]); rE2 = self.r_gainB[1]
    QS = sb(st, [128, 512], F32, "QS"); rQS = Res()
    RESET = sb(st, [128, 512], F32, "RESET"); rRS = Res()
    SC = sb(st, [128, NT, 3], F32, "SC"); rSC = Res()
    lbt = sb(st, [128, 4, 4], F32, "lbt"); rlb = Res()
    ktok = [sb(st, [128, 128], BF16, "ktok") for _ in range(2)]; rktok = [Res(), Res()]
    ATm = [sb(st, [128, 128], BF16, "ATm") for _ in range(2)]; rATm = [Res(), Res()]
    Sst = sb(st, [128, 128], F32, "Sst"); rS = Res()
    Sbf = sb(st, [128, 128], BF16, "Sbf"); rSbf = Res()
    yb = [sb(st, [128, 128], F32, "yb") for _ in range(2)]; ryb = [Res(), Res()]
    sg = [sb(st, [128, 128], F32, "sg") for _ in range(2)]; rsg = [Res(), Res()]
    ym = [sb(st, [128, 128], BF16, "ym") for _ in range(2)]; rym = [Res(), Res()]
    ssn = [sb(st, [128, 2], F32, "ssn") for _ in range(2)]; rssn = [Res(), Res()]
    pA = [ps(st, [128, 512], F32, "pA") for _ in range(2)]; rpA = [Res(), Res()]
    pv = ps(st, [128, 512], F32, "pv"); rpv = Res()
    pG = ps(st, [128, 512], F32, "pG"); rpG = Res()
    pKl = [pA[0][:, 0:128], pA[1][:, 0:128], pv[:, 0:128], pG[:, 0:128]]
    rpK = [rpA[0], rpA[1], rpv, rpG]
    l0o, _ = self.poff["lb0"]; l1o, _ = self.poff["lb1"]
    hgo, _ = self.poff["hng"]

    P.op(DVE, lambda e: e.tensor_tensor(out=lbt[:, :, 0], in0=PP[:, l1o:l1o + 4], in1=PP[:, l0o:l0o + 4], op=ALU.subtract), [self.rPP], [rlb])
    P.op(ACT, lambda e: e.activation(out=lbt[:, :, 0], in_=lbt[:, :, 0], func=AF.Exp, scale=-1.0), [rlb], [rlb])
    P.op(DVE, lambda e: e.tensor_scalar(out=lbt[:, :, 0], in0=lbt[:, :, 0], scalar1=1.0, scalar2=None, op0=ALU.add), [rlb], [rlb])
    P.op(DVE, lambda e: e.reciprocal(out=lbt[:, :, 0], in_=lbt[:, :, 0]), [rlb], [rlb])
    P.op(DVE, lambda e: e.tensor_scalar(out=lbt[:, :, 1], in0=lbt[:, :, 0], scalar1=-1.0, scalar2=1.0, op0=ALU.mult, op1=ALU.add), [rlb], [rlb])
    P.op(DVE, lambda e: e.tensor_scalar(out=lbt[:, :, 3], in0=lbt[:, :, 1], scalar1=0.5, scalar2=None, op0=ALU.mult), [rlb], [rlb])
    P.op(DVE, lambda e: e.tensor_tensor(out=lbt[:, :, 2], in0=lbt[:, :, 0], in1=lbt[:, :, 3], op=ALU.add), [rlb], [rlb])
    P.op(POOL, lambda e: e.memset(RESET[:], 1.0), [], [rRS])
    P.op(POOL, lambda e: e.memset(RESET[:].rearrange("p (c t) -> p c t", c=4)[:, :, 0:1], 0.0), [rRS], [rRS])
    self.load_w("in1_hi", Wi, rWi)
    for t in range(NT):
        self.proj_tm(Wi, rWi, HT, rHT, t, pv[:], rpv)
        P.op(ACT, lambda e, t=t: e.activation(out=vtok[:, t, :], in_=pv[:], func=AF.Copy), [rpv], [rvt[t]])

    def head_block(h, b):
        lb = lbt[:, h, 0:1]; oml = lbt[:, h, 1:2]
        self.proj_fm(Wf, rWf, HT, rHT, b, pA[0], rpA[0])
        self.proj_fm(Wq, rWq, HT, rHT, b, pA[1], rpA[1])
        P.op(ACT, lambda e: e.activation(out=A1[:], in_=pA[0][:], func=AF.Tanh, scale=0.5), [], [rA1, rpA[0]])
        P.op(ACT, lambda e: e.activation(out=QS[:], in_=pA[1][:], func=AF.Silu), [], [rQS, rpA[1]])
        P.op(DVE, lambda e: e.tensor_scalar(out=A1[:], in0=A1[:], scalar1=lbt[:, h, 3:4], scalar2=lbt[:, h, 2:3], op0=ALU.mult, op1=ALU.add), [rA1, rlb], [rA1])
        P.op(POOL, lambda e: e.tensor_scalar(out=A2[:], in0=A1[:], scalar1=-1.0, scalar2=1.0, op0=ALU.mult, op1=ALU.add), [rA1], [rA2])
        P.op(ACT, lambda e: e.activation(out=A1[:], in_=A1[:], func=AF.Ln), [rA1, rA2], [rA1])
        P.op(DVE, lambda e: e.tensor_tensor_scan(out=A3[:], data0=RESET[:], data1=A1[:], initial=0.0, op0=ALU.mult, op1=ALU.add), [rA1, rRS], [rA3])
        A3v = A3[:].rearrange("p (c t) -> p c t", c=4)
        A4v = A4[:].rearrange("p (c t) -> p c t", c=4)
        P.op(DVE, lambda e: e.tensor_tensor(out=A4v, in0=A3v, in1=A3v[:, :, 63:64].to_broadcast([128, 4, 128]), op=ALU.subtract), [rA3], [rA4])
        P.op(ACT, lambda e: e.activation(out=E1[:], in_=A4[:], func=AF.Exp), [rA4], [rE1])
        P.op(ACT, lambda e: e.activation(out=E2[:], in_=A4[:], func=AF.Exp, scale=-1.0), [rA4], [rE2])
        P.op(ACT, lambda e: e.activation(out=SC[:, b * 4:(b + 1) * 4, 0], in_=A3v[:, :, 127], func=AF.Exp), [rA3], [rSC])
        P.op(ACT, lambda e: e.activation(out=SC[:, b * 4:(b + 1) * 4, 1], in_=A4v[:, :, 127], func=AF.Exp), [rA4], [rSC])
        P.op(ACT, lambda e: e.activation(out=SC[:, b * 4:(b + 1) * 4, 2], in_=A3v[:, :, 63], func=AF.Exp), [rA3], [rSC])
        P.op(DVE, lambda e: e.tensor_tensor(out=qtT[:, b * 512:(b + 1) * 512], in0=QS[:], in1=E1[:], op=ALU.mult), [rQS, rE1], [rqt[b]])
        P.op(DVE, lambda e: e.tensor_tensor(out=ktT[:, b * 512:(b + 1) * 512], in0=A2[:], in1=E2[:], op=ALU.mult), [rA2, rE2], [rkt[b]])
        P.op(POOL, lambda e: e.tensor_tensor(out=qhT[:, b * 512:(b + 1) * 512].rearrange("p (c t) -> p c t", c=4),
                                             in0=qtT[:, b * 512:(b + 1) * 512].rearrange("p (c t) -> p c t", c=4),
                                             in1=SC[:, b * 4:(b + 1) * 4, 2:3].to_broadcast([128, 4, 128]), op=ALU.mult), [rqt[b], rSC], [rqh[b]])

    def head_chunk(h, n):
        k = n % 2
        b = n // 4
        cs = slice(n * 128, (n + 1) * 128)
        pt, rpt = self.pt[k], self.r_pt[k]
        P.op(PE, lambda e: e.transpose(pt[:, 0, :], ktT[:, cs], self.ident_bf[:]), [rkt[b]], [rpt])
        P.op(DVE, lambda e: e.tensor_copy(out=ktok[k][:], in_=pt[:, 0, :]), [rpt], [rktok[k]])
        P.op(PE, lambda e: e.matmul(pKl[0], lhsT=ktT[:, cs], rhs=qtT[:, cs], start=True, stop=True), [rkt[b], rqt[b]], [rpK[0]])
        P.op(DVE, lambda e: e.tensor_tensor(out=ATm[k][:], in0=pKl[0], in1=self.tri_incl[:], op=ALU.mult), [self.rconst], [rATm[k], rpK[0]])
        P.op(PE, lambda e: e.matmul(pKl[1], lhsT=ktok[k][:], rhs=vtok[:, n, h * 128:(h + 1) * 128], start=True, stop=True), [rktok[k], rvt[n]], [rpK[1]])
        def mmo(e):
            ins = None
            if n > 0:
                ins = e.matmul(pKl[2], lhsT=qhT[:, cs], rhs=Sbf[:], start=True, stop=False)
            ins = e.matmul(pKl[2], lhsT=ATm[k][:], rhs=vtok[:, n, h * 128:(h + 1) * 128], start=(n == 0), stop=True)
            return ins
        P.op(PE, mmo, [rqh[b], rSbf, rATm[k], rvt[n]], [rpK[2]])
        if n == 0:
            P.op(DVE, lambda e: e.tensor_scalar(out=Sst[:], in0=pKl[1], scalar1=SC[:, n, 1:2], scalar2=None, op0=ALU.mult), [rSC], [rS, rpK[1]])
        else:
            P.op(DVE, lambda e: e.tensor_scalar(out=Sst[:], in0=Sst[:], scalar1=SC[:, n, 0:1], scalar2=None, op0=ALU.mult), [rS, rSC], [rS])
            P.op(DVE, lambda e: e.scalar_tensor_tensor(out=Sst[:], in0=pKl[1], scalar=SC[:, n, 1:2], in1=Sst[:], op0=ALU.mult, op1=ALU.add), [rSC, rS], [rS, rpK[1]])
        if n < NT - 1:
            P.op(ACT, lambda e: e.activation(out=Sbf[:], in_=Sst[:], func=AF.Copy), [rS], [rSbf])
        self.proj_tm(Wg, rWg, HT, rHT, n, pKl[3], rpK[3])
        P.op(ACT, lambda e: e.activation(out=sg[k][:], in_=pKl[3], func=AF.Exp, scale=-1.0), [], [rsg[k], rpK[3]])
        P.op(ACT, lambda e: e.activation(out=sg[k][:], in_=sg[k][:], func=AF.Ln, bias=1.0), [rsg[k]], [rsg[k]])
        P.op(ACT, lambda e: e.activation(out=sg[k][:], in_=sg[k][:], func=AF.Exp, scale=-1.0), [rsg[k]], [rsg[k]])
        P.op(DVE, lambda e: e.tensor_tensor(out=sg[k][:], in0=pKl[3], in1=sg[k][:], op=ALU.mult), [rsg[k]], [rsg[k], rpK[3]])
        P.op(ACT, lambda e: e.activation(out=yb[k][:], in_=pKl[2], func=AF.Square, accum_out=ssn[k][:, 0:1]), [], [ryb[k], rssn[k], rpK[2]])
        self.rstd_from_ss(ssn[k][:, 0:1], ssn[k][:, 1:2], rssn[k], rssn[k], 128)
        P.op(DVE, lambda e: e.scalar_tensor_tensor(out=yb[k][:], in0=pKl[2], scalar=ssn[k][:, 1:2], in1=PP[:, hgo:hgo + 128], op0=ALU.mult, op1=ALU.mult), [rssn[k], ryb[k]], [ryb[k], rpK[2]])
        P.op(POOL, lambda e: e.tensor_tensor(out=ym[k][:], in0=yb[k][:], in1=sg[k][:], op=ALU.mult), [ryb[k], rsg[k]], [rym[k]])
        P.op(PE, lambda e: e.transpose(pt[:, 1, :], ym[k][:], self.ident_bf[:]), [rym[k]], [rpt])
        P.op(ACT, lambda e: e.activation(out=MT[:, h, cs], in_=pt[:, 1, :], func=AF.Copy), [rpt], [rMT[n]])

    for h in range(4):
        self.load_w(f"in1_hq{h}", Wq, rWq)
        self.load_w(f"in1_hf{h}", Wf, rWf)
        self.load_w("in1_hg", Wg, rWg, cols=h * 128, ncol=128)
        for b in range(NB):
            head_block(h, b)
        for n in range(NT):
            head_chunk(h, n)


Builder.hgrn = _hgrn


def _ssd(self, HT, rHT, MT, rMT, st):
    P = self.P
    S, NT, NB = self.S, self.NT, self.NB
    sb, ps = self.sb, self.ps
    PP = self.PP
    XT = sb(st, [128, 8, 512], BF16, "XT"); rXT = [Res() for _ in range(8)]
    U = [sb(st, [128, 515], F32, "U") for _ in range(2)]; rU = [Res(), Res()]
    Y = [sb(st, [128, 512], F32, "Y") for _ in range(2)]; rY = [Res(), Res()]
    HALO = sb(st, [128, 8, 3], F32, "HALO"); rH = [Res() for _ in range(8)]
    Wx = [sb(st, [128, 8, 128], BF16, "Wx") for _ in range(2)]; rWx = [Res(), Res()]
    Wz = sb(st, [128, 8, 512], BF16, "Wz"); rWz = Res()
    Wdt = sb(st, [128, 8, 8], BF16, "Wdt"); rWdt = Res()
    sgt = sb(st, [128, 8], F32, "sgt"); rsgt = Res()
    eA = sb(st, [128, 8], F32, "eA"); reA = Res()
    dt = sb(st, [128, 8], F32, "dt"); rdt = Res()
    av = sb(st, [128, 8], F32, "av"); rav = Res()
    sc = sb(st, [128, 3, 8], F32, "sc"); rsc = Res()
    tmp8 = sb(st, [128, 8], F32, "tmp8"); rtmp8 = Res()
    class _V3:
        def __init__(self, ap):
            self.ap = ap
        def __getitem__(self, k):
            return self.ap if (isinstance(k, slice) and k == slice(None)) else self.ap[k]
    TA8 = _V3(self.gainB[0][:].rearrange("p (h c) -> p h c", h=8)); rTA = self.r_gainB[0]
    gT = sb(st, [128, 8, 128], BF16, "gT"); rgT = Res()
    ATt = sb(st, [128, 8, 128], BF16, "ATt"); rAT = Res()
    Btok = sb(st, [128, 2, 128], BF16, "Btok"); rBt = Res()
    Xd = sb(st, [128, 8, 64], BF16, "Xd"); rXd = Res()
    Xdec = sb(st, [128, 8, 64], BF16, "Xdec"); rXdec = Res()
    xsD = sb(st, [128, 8, 64], F32, "xsD"); rxsD = Res()
    ytmp = sb(st, [128, 8, 64], F32, "ytmp"); rytmp = Res()
    yy = sb(st, [128, 512], F32, "yy"); ryy = Res()
    sgz = _V3(self.tmp32[0][:, 0:512]); rsgz = self.r_tmp32[0]
    sq = _V3(self.tmp32[0][:, 512:1024]); rsq = self.r_tmp32[0]
    ybf = sb(st, [128, 512], BF16, "ybf"); rybf = Res()
    ssg = sb(st, [128, 2, 2], F32, "ssg"); rssg = Res()
    Sst = sb(st, [128, 8, 64], F32, "Sst"); rS = Res()
    Sbf = sb(st, [128, 512], BF16, "Sbf"); rSbf = Res()
    sgt_c = sb(st, [128, 128], F32, "sgt_c"); rsgt_c = Res()
    pLG = ps(st, [128, 4, 128], F32, "pLG"); rpLG = Res()
    pCB = ps(st, [128, 512], F32, "pCB"); rpCB = Res()
    p2 = ps(st, [128, 512], F32, "p2"); rp2 = Res()
    p3 = ps(st, [128, 512], F32, "p3"); rp3 = Res()
    p4 = ps(st, [128, 512], F32, "p4"); rp4 = Res()
    pz = ps(st, [128, 512], F32, "pz"); rpz = Res()
    cwo, _ = self.poff["mcw"]; cbo, _ = self.poff["mcb"]
    alo, _ = self.poff["mAlog"]; dbo, _ = self.poff["mdtb"]; mDo, _ = self.poff["mD"]; ngo, _ = self.poff["mng"]

    P.op(POOL, lambda e: e.memset(sgt_c[:], 1.0), [], [rsgt_c])
    P.op(POOL, lambda e: e.affine_select(out=sgt_c[:], in_=sgt_c[:], pattern=[[-1, 128]], compare_op=ALU.is_gt, fill=0.0, base=0, channel_multiplier=1), [rsgt_c], [rsgt_c])
    P.op(POOL, lambda e: e.memset(HALO[:], 0.0), [], rH)
    P.op(ACT, lambda e: e.activation(out=eA[:], in_=PP[:, alo:alo + 8], func=AF.Exp), [self.rPP], [reA])
    self.load_w("in1_mz", Wz, rWz)
    self.load_w("in1_mdt", Wdt, rWdt)

    def conv_chunk(b, c):
        k = c % 2
        self.load_w(f"in1_mx{c}", Wx[k], rWx[k])
        pb, rpb = (p2, rp2) if k == 0 else (p3, rp3)
        self.proj_fm(Wx[k], rWx[k], HT, rHT, b, pb, rpb)
        u, ru, y, ry = U[k], rU[k], Y[k], rY[k]
        P.op(POOL, lambda e: e.tensor_copy(out=u[:, 0:3], in_=HALO[:, c, :]), [rH[c]], [ru])
        P.op(ACT, lambda e: e.activation(out=u[:, 3:515], in_=pb[:], func=AF.Copy), [], [ru, rpb])
        P.op(POOL, lambda e: e.tensor_copy(out=HALO[:, c, :], in_=u[:, 512:515]), [ru], [rH[c]])
        w = [PP[:, cwo + c * 4 + i:cwo + c * 4 + i + 1] for i in range(4)]
        bb = PP[:, cbo + c:cbo + c + 1]
        P.op(DVE, lambda e: e.tensor_scalar(out=y[:], in0=u[:, 0:512], scalar1=w[0], scalar2=bb, op0=ALU.mult, op1=ALU.add), [ru], [ry])
        for i in range(1, 4):
            P.op(DVE, lambda e, i=i: e.scalar_tensor_tensor(out=y[:], in0=u[:, i:i + 512], scalar=w[i], in1=y[:], op0=ALU.mult, op1=ALU.add), [ru, ry], [ry])
        P.op(ACT, lambda e: e.activation(out=XT[:, c, :], in_=y[:], func=AF.Silu), [ry], [rXT[c]])

    def chunk(n):
        tt = n % 4
        cs = slice(tt * 128, (tt + 1) * 128)
        gs = slice(n * 128, (n + 1) * 128)
        pt0, rpt0 = self.pt[0], self.r_pt[0]
        pt1, rpt1 = self.pt[1], self.r_pt[1]
        self.proj_tm(Wdt, rWdt, HT, rHT, n, pCB[:, 256:264], rpCB)
        P.op(DVE, lambda e: e.tensor_tensor(out=dt[:], in0=pCB[:, 256:264], in1=PP[:, dbo:dbo + 8], op=ALU.add), [self.rPP], [rdt, rpCB])
        P.op(ACT, lambda e: e.activation(out=dt[:], in_=dt[:], func=AF.Exp), [rdt], [rdt])
        P.op(ACT, lambda e: e.activation(out=dt[:], in_=dt[:], func=AF.Ln, bias=1.0), [rdt], [rdt])
        P.op(DVE, lambda e: e.scalar_tensor_tensor(out=av[:], in0=dt[:], scalar=-1.0, in1=eA[:], op0=ALU.mult, op1=ALU.mult), [rdt, reA], [rav])
        def mma(e):
            e.matmul(pCB[:, 264:272], lhsT=self.tri_incl[:], rhs=av[:], start=True, stop=True)
            return e.matmul(pCB[:, 272:280], lhsT=self.ones_f[:], rhs=av[:], start=True, stop=True)
        P.op(PE, mma, [rav, self.rconst], [rpCB])
        P.op(ACT, lambda e: e.activation(out=sc[:, 0, :], in_=pCB[:, 264:272], func=AF.Exp), [], [rsc, rpCB])
        P.op(ACT, lambda e: e.activation(out=sc[:, 2, :], in_=pCB[:, 272:280], func=AF.Exp), [], [rsc, rpCB])
        P.op(ACT, lambda e: e.activation(out=tmp8[:], in_=pCB[:, 264:272], func=AF.Copy), [], [rtmp8, rpCB])
        P.op(DVE, lambda e: e.tensor_tensor(out=tmp8[:], in0=pCB[:, 272:280], in1=tmp8[:], op=ALU.subtract), [], [rtmp8, rpCB])
        P.op(ACT, lambda e: e.activation(out=sc[:, 1, :], in_=tmp8[:], func=AF.Exp), [rtmp8], [rsc])
        P.op(DVE, lambda e: e.tensor_tensor(out=TA8[:], in0=self.tri_incl[:].unsqueeze(1).to_broadcast([128, 8, 128]),
                                            in1=av[:].unsqueeze(2).to_broadcast([128, 8, 128]), op=ALU.mult), [rav, self.rconst], [rTA])
        for half in range(2):
            def mml(e, half=half):
                ins = None
                for hh in range(4):
                    h = half * 4 + hh
                    e.matmul(pLG[:, hh, :], lhsT=sgt_c[:], rhs=TA8[:, h, :], start=True, stop=False)
                    ins = e.matmul(pLG[:, hh, :], lhsT=self.ident_f[:], rhs=self.neg_incl[:], start=False, stop=True)
                return ins
            P.op(PE, mml, [rTA, rsgt_c, self.rconst], [rpLG])
            P.op(ACT, lambda e, half=half: e.activation(out=gT[:, half * 4:(half + 1) * 4, :], in_=pLG[:], func=AF.Exp), [], [rgT, rpLG])
        def mmcb(e):
            ins = None
            for g in range(2):
                ins = e.matmul(pCB[:, g * 128:(g + 1) * 128], lhsT=XT[:, 4 + g, cs], rhs=XT[:, 6 + g, cs], start=True, stop=True)
            return ins
        P.op(PE, mmcb, [rXT[4], rXT[5], rXT[6], rXT[7]], [rpCB])
        P.op(DVE, lambda e: e.tensor_tensor(out=ATt[:].rearrange("p (g a) l -> p g a l", g=2),
                                            in0=pCB[:, 0:256].rearrange("p (g l) -> p g l", g=2).unsqueeze(2).to_broadcast([128, 2, 4, 128]),
                                            in1=gT[:].rearrange("p (g a) l -> p g a l", g=2), op=ALU.mult), [rgT], [rAT, rpCB])
        def trx(e):
            ins = None
            for c in range(4):
                ins = e.transpose(pt0[:, c, :], XT[:, c, cs], self.ident_bf[:])
            return ins
        P.op(PE, trx, [rXT[0], rXT[1], rXT[2], rXT[3]], [rpt0])
        def trb(e):
            ins = None
            for g in range(2):
                ins = e.transpose(pt1[:, g, :], XT[:, 4 + g, cs], self.ident_bf[:])
            return ins
        P.op(PE, trb, [rXT[4], rXT[5]], [rpt1])
        P.op(ACT, lambda e: e.activation(out=Btok[:], in_=pt1[:, 0:2, :], func=AF.Copy), [], [rBt, rpt1])
        xs_ps = pt0[:, 0:4, :].rearrange("p c (a q) -> p (c a) q", q=64)
        P.op(DVE, lambda e: e.tensor_tensor(out=Xd[:], in0=xs_ps, in1=dt[:].unsqueeze(2).to_broadcast([128, 8, 64]), op=ALU.mult), [rdt], [rXd, rpt0])
        P.op(DVE, lambda e: e.tensor_tensor(out=xsD[:], in0=xs_ps, in1=PP[:, mDo:mDo + 8].unsqueeze(2).to_broadcast([128, 8, 64]), op=ALU.mult), [self.rPP], [rxsD, rpt0])
        P.op(POOL, lambda e: e.tensor_tensor(out=Xdec[:], in0=Xd[:], in1=sc[:, 1, :].unsqueeze(2).to_broadcast([128, 8, 64]), op=ALU.mult), [rXd, rsc], [rXdec])
        if n > 0:
            def mm2(e):
                ins = None
                for g in range(2):
                    ins = e.matmul(p2[:, g * 256:(g + 1) * 256], lhsT=XT[:, 6 + g, cs], rhs=Sbf[:, g * 256:(g + 1) * 256], start=True, stop=True)
                return ins
            P.op(PE, mm2, [rXT[6], rXT[7], rSbf], [rp2])
        def mm3(e):
            ins = None
            for h in range(8):
                ins = e.matmul(p3[:, h * 64:(h + 1) * 64], lhsT=ATt[:, h, :], rhs=Xd[:, h, :], start=True, stop=True)
            return ins
        P.op(PE, mm3, [rAT, rXd], [rp3])
        if n > 0:
            P.op(DVE, lambda e: e.tensor_tensor(out=ytmp[:], in0=p2[:].rearrange("p (h q) -> p h q", q=64), in1=sc[:, 0, :].unsqueeze(2).to_broadcast([128, 8, 64]), op=ALU.mult), [rsc], [rytmp, rp2])
            P.op(DVE, lambda e: e.tensor_tensor(out=yy[:], in0=p3[:], in1=ytmp[:].rearrange("p h q -> p (h q)"), op=ALU.add), [rytmp], [ryy, rp3])
        else:
            P.op(DVE, lambda e: e.tensor_copy(out=yy[:], in_=p3[:]), [], [ryy, rp3])
        P.op(POOL, lambda e: e.tensor_tensor(out=yy[:], in0=yy[:], in1=xsD[:].rearrange("p h q -> p (h q)"), op=ALU.add), [ryy, rxsD], [ryy])
        if n < NT - 1:
            def mm4(e):
                ins = None
                for g in range(2):
                    ins = e.matmul(p4[:, g * 256:(g + 1) * 256], lhsT=Btok[:, g, :], rhs=Xdec[:, g * 4:(g + 1) * 4, :].rearrange("p a q -> p (a q)"), start=True, stop=True)
                return ins
            P.op(PE, mm4, [rBt, rXdec], [rp4])
            if n == 0:
                P.op(DVE, lambda e: e.tensor_copy(out=Sst[:].rearrange("p h q -> p (h q)"), in_=p4[:]), [], [rS, rp4])
            else:
                P.op(DVE, lambda e: e.tensor_tensor(out=Sst[:], in0=Sst[:], in1=sc[:, 2, :].unsqueeze(2).to_broadcast([128, 8, 64]), op=ALU.mult), [rS, rsc], [rS])
                P.op(DVE, lambda e: e.tensor_tensor(out=Sst[:].rearrange("p h q -> p (h q)"), in0=p4[:], in1=Sst[:].rearrange("p h q -> p (h q)"), op=ALU.add), [rS], [rS, rp4])
            P.op(ACT, lambda e: e.activation(out=Sbf[:], in_=Sst[:].rearrange("p h q -> p (h q)"), func=AF.Copy), [rS], [rSbf])
        self.proj_tm(Wz, rWz, HT, rHT, n, pz[:], rpz)
        P.op(ACT, lambda e: e.activation(out=sgz[:], in_=pz[:], func=AF.Silu), [], [rsgz, rpz])
        P.op(POOL, lambda e: e.tensor_tensor(out=yy[:], in0=yy[:], in1=sgz[:], op=ALU.mult), [ryy, rsgz], [ryy])
        P.op(POOL, lambda e: e.tensor_tensor(out=sq[:], in0=yy[:], in1=yy[:], op=ALU.mult), [ryy], [rsq])
        P.op(DVE, lambda e: e.tensor_reduce(out=ssg[:, 0, :], in_=sq[:].rearrange("p (g f) -> p g f", g=2), axis=AX.X, op=ALU.add), [rsq], [rssg])
        self.rstd_from_ss(ssg[:, 0, :], ssg[:, 1, :], rssg, rssg, 256)
        P.op(DVE, lambda e: e.tensor_tensor(out=yy[:].rearrange("p (g f) -> p g f", g=2), in0=yy[:].rearrange("p (g f) -> p g f", g=2),
                                            in1=ssg[:, 1, :].unsqueeze(2).to_broadcast([128, 2, 256]), op=ALU.mult), [ryy, rssg], [ryy])
        P.op(POOL, lambda e: e.tensor_tensor(out=ybf[:], in0=yy[:], in1=PP[:, ngo:ngo + 512], op=ALU.mult), [ryy, self.rPP], [rybf])
        def try_(e):
            ins = None
            for c in range(4):
                ins = e.transpose(pt1[:, 4 + c, :], ybf[:, c * 128:(c + 1) * 128], self.ident_bf[:])
            return ins
        P.op(PE, try_, [rybf], [rpt1])
        P.op(ACT, lambda e: e.activation(out=MT[:, 4:8, gs], in_=pt1[:, 4:8, :], func=AF.Copy), [], [rMT[n], rpt1])

    for b in range(NB):
        for c in range(8):
            conv_chunk(b, c)
        for tt in range(4):
            chunk(b * 4 + tt)


Builder.ssd = _ssd


def _gdn(self, HT, rHT, MT, rMT, st):
    P = self.P
    S, NT, NB = self.S, self.NT, self.NB
    sb, ps = self.sb, self.ps
    PP = self.PP
    v4 = lambda ap: ap.rearrange("p (h c) -> p h c", h=4)
    YT = sb(st, [128, 12, 512], BF16, "YT"); rYT = [Res() for _ in range(12)]
    U = [sb(st, [128, 515], F32, "U") for _ in range(2)]; rU = [Res(), Res()]
    Y = [sb(st, [128, 512], F32, "Y") for _ in range(2)]; rY = [Res(), Res()]
    HALO = sb(st, [128, 12, 3], F32, "HALO"); rH = [Res() for _ in range(12)]
    Wc = [sb(st, [128, 8, 128], BF16, "Wc") for _ in range(2)]; rWc = [Res(), Res()]
    Wgg = sb(st, [128, 8, 512], BF16, "Wgg"); rWgg = Res()
    Wgab = sb(st, [128, 8, 8], BF16, "Wgab"); rWgab = Res()
    sm2 = [sb(st, [128, 16, 4], F32, "sm") for _ in range(3)]; rsm2 = [Res(), Res(), Res()]
    eAg = sb(st, [128, 4], F32, "eAg"); reAg = Res()
    sqq = sb(st, [128, 4, 128], BF16, "sqq"); rsqq = Res()
    sqk = sb(st, [128, 4, 128], BF16, "sqk"); rsqk = Res()
    TTt = sb(st, [128, 512], F32, "TT"); TT = TTt[:]; rTT = Res()
    offd = sb(st, [128, 128], BF16, "offd"); roffd = Res()
    gNs = sb(st, [128, 4, 128], BF16, "gNs"); rgNs = Res()
    ATt = sb(st, [128, 4, 128], BF16, "ATt"); rAT = Res()
    Xf = sb(st, [128, 4, 256], F32, "Xf"); rXf = Res()
    kD = sb(st, [128, 4, 128], BF16, "kD"); rkD = Res()
    u0 = sb(st, [128, 4, 128], F32, "u0"); ru0 = Res()
    wv = sb(st, [128, 4, 128], BF16, "wv"); rwv = Res()
    wT = sb(st, [128, 4, 128], BF16, "wT"); rwT = Res()
    ubf = sb(st, [128, 4, 128], BF16, "ubf"); rubf = Res()
    Sst = sb(st, [128, 4, 128], F32, "Sst"); rS = Res()
    Sbf = sb(st, [128, 4, 128], BF16, "Sbf"); rSbf = Res()
    sgt_c = sb(st, [128, 128], F32, "sgt_c"); rsgt_c = Res()
    o_t = self.tmp32[0][:, 0:512]; otmp = self.tmp32[0][:, 512:1024]; ro = self.r_tmp32[0]
    sq_t = Y[0][:]; rsq = rY[0]
    sgg = Y[1][:]; rsgg = rY[1]
    TG4 = U[0][:, 0:512]; DRA4 = U[1][:, 0:512]; rTG = rU[0]; rDRA = rU[1]
    gA = self.hn[0][:, 0:512]; gN = self.hn[0][:, 512:1024]; rgm = self.r_hn[0]
    ybf = self.hn[1][:, 0:512]; rybf = self.r_hn[1]
    Pc = [(self.gainB[0][:, 0:512], self.gainB[0][:, 512:1024], self.r_gainB[0], Res()),
          (self.gainB[1][:, 0:512], self.gainB[1][:, 512:1024], self.r_gainB[1], Res())]
    X = self.X
    xs = self.gdn_xsub
    def sub(t):
        r = Res(); xs[t].append(r); r.rd.append(self.gdn_spill_ops[t]); return r
    bf = lambda t: X[:, t, :].bitcast(BF16)
    r4 = lambda ap: ap.rearrange("p (h c) -> p h c", h=4)
    B0 = dict(sqq=sqq, rsqq=rsqq, sqk=sqk, rsqk=rsqk, TG4=TG4, DRA4=DRA4, rTG=rTG, rDRA=rDRA, gA=gA, rgm=rgm,
              gNs=gNs, rgNs=rgNs, ATt=ATt, rAT=rAT, Pc=Pc, TT=TT, rTT=rTT, Xf=Xf, rXf=rXf, kD=kD, rkD=rkD,
              u0=u0, ru0=ru0, wv=wv, rwv=rwv, wT=wT, rwT=rwT)
    def xset(o):
        return dict(sqq=V(r4(bf(o + 5)[:, 0:512])), rsqq=sub(o + 5), sqk=V(r4(bf(o + 5)[:, 512:1024])), rsqk=sub(o + 5),
              gA=bf(o + 5)[:, 1024:1536], rgm=sub(o + 5), gNs=V(r4(bf(o + 5)[:, 1536:2048])), rgNs=sub(o + 5),
              TG4=X[:, o + 4, 0:512], DRA4=X[:, o + 4, 512:1024], rTG=sub(o + 4), rDRA=sub(o + 4),
              ATt=V(r4(bf(o + 6)[:, 0:512])), rAT=sub(o + 6), kD=V(r4(bf(o + 6)[:, 512:1024])), rkD=sub(o + 6),
              wv=V(r4(bf(o + 6)[:, 1024:1536])), rwv=sub(o + 6), wT=V(r4(bf(o + 6)[:, 1536:2048])), rwT=sub(o + 6),
              Pc=[(X[:, o + 0, 0:512], X[:, o + 0, 512:1024], sub(o + 0), sub(o + 0)), (X[:, o + 1, 0:512], X[:, o + 1, 512:1024], sub(o + 1), sub(o + 1))],
              TT=X[:, o + 2, 0:512], rTT=sub(o + 2), u0=V(r4(X[:, o + 2, 512:1024])), ru0=sub(o + 2),
              Xf=V(X[:, o + 3, :].rearrange("p (h c) -> p h c", h=4)), rXf=sub(o + 3))
    NSET = 1 + self.gdn_nx // 7
    if self.gdn_nx >= 15:
        for half in range(2):
            Wc.append(V(X[:, 14, :].bitcast(BF16)[:, half * 1024:(half + 1) * 1024].rearrange("p (k c) -> p k c", k=8)))
            rWc.append(sub(14))
    BUFS = [B0] + [xset(7 * i) for i in range(NSET - 1)]
    pa = ps(st, [128, 512], F32, "pa"); rpa = Res()
    pb = ps(st, [128, 512], F32, "pb"); rpb = Res()
    pcd = ps(st, [128, 1024], F32, "pcd"); rpcd = Res()
    pc_, pd_ = pcd[:, 0:512], pcd[:, 512:1024]
    pe = ps(st, [128, 512], F32, "pe"); rpe = Res()
    pg = ps(st, [128, 512], F32, "pg"); rpg = Res()
    cwo, _ = self.poff["gcw"]
    alo, _ = self.poff["gAlog"]; dbo, _ = self.poff["gdtb"]; ngo, _ = self.poff["gng"]
    GV, LNB, BETA, GC, GL, EG, EGLG, GLB, LNRK, RK, CR, CD, T1, T2, MSR, RSTD = range(16)
    P.op(POOL, lambda e: e.memset(offd[:], 1.0), [], [roffd])
    P.op(POOL, lambda e: e.affine_select(out=offd[:], in_=offd[:], pattern=[[-1, 128]], compare_op=ALU.not_equal, fill=0.0, base=0, channel_multiplier=1), [roffd], [roffd])

    P.op(POOL, lambda e: e.memset(sgt_c[:], 1.0), [], [rsgt_c])
    P.op(POOL, lambda e: e.affine_select(out=sgt_c[:], in_=sgt_c[:], pattern=[[-1, 128]], compare_op=ALU.is_gt, fill=0.0, base=0, channel_multiplier=1), [rsgt_c], [rsgt_c])
    P.op(POOL, lambda e: e.memset(HALO[:], 0.0), [], rH)
    P.op(ACT, lambda e: e.activation(out=eAg[:], in_=PP[:, alo:alo + 4], func=AF.Exp), [self.rPP], [reAg])
    self.load_w("in0_gg", Wgg, rWgg)
    self.load_w("in0_gab", Wgab, rWgab)

    def conv_chunk(b, c):
        k = c % 2
        kw = c % len(Wc)
        name = ("gq", "gk", "gv")[c // 4] + str(c % 4)
        self.load_w(f"in0_{name}", Wc[kw], rWc[kw])
        pbk, rpbk = (pa, rpa) if k == 0 else (pb, rpb)
        self.proj_fm(Wc[kw], rWc[kw], HT, rHT, b, pbk, rpbk)
        u, ru, y, ry = U[k], rU[k], Y[k], rY[k]
        P.op(POOL, lambda e: e.tensor_copy(out=u[:, 0:3], in_=HALO[:, c, :]), [rH[c]], [ru])
        P.op(ACT, lambda e: e.activation(out=u[:, 3:515], in_=pbk[:], func=AF.Copy), [], [ru, rpbk])
        P.op(POOL, lambda e: e.tensor_copy(out=HALO[:, c, :], in_=u[:, 512:515]), [ru], [rH[c]])
        w = [PP[:, cwo + c * 4 + i:cwo + c * 4 + i + 1] for i in range(4)]
        P.op(DVE, lambda e: e.tensor_scalar(out=y[:], in0=u[:, 0:512], scalar1=w[0], scalar2=None, op0=ALU.mult), [ru], [ry])
        for i in range(1, 4):
            P.op(DVE, lambda e, i=i: e.scalar_tensor_tensor(out=y[:], in0=u[:, i:i + 512], scalar=w[i], in1=y[:], op0=ALU.mult, op1=ALU.add), [ru, ry], [ry])
        P.op(ACT, lambda e: e.activation(out=YT[:, c, :], in_=y[:], func=AF.Silu), [ry], [rYT[c]])

    def chunk(n):
        sm, rsm = sm2[n % len(BUFS)], rsm2[n % len(BUFS)]
        Bf = BUFS[n % len(BUFS)]
        sqq, rsqq, sqk, rsqk = Bf["sqq"], Bf["rsqq"], Bf["sqk"], Bf["rsqk"]
        TG4, DRA4, rTG, rDRA = Bf["TG4"], Bf["DRA4"], Bf["rTG"], Bf["rDRA"]
        gA, rgm, gNs, rgNs = Bf["gA"], Bf["rgm"], Bf["gNs"], Bf["rgNs"]
        ATt, rAT, Pc, TT, rTT = Bf["ATt"], Bf["rAT"], Bf["Pc"], Bf["TT"], Bf["rTT"]
        Xf, rXf, kD, rkD = Bf["Xf"], Bf["rXf"], Bf["kD"], Bf["rkD"]
        u0, ru0, wv, rwv, wT, rwT = Bf["u0"], Bf["ru0"], Bf["wv"], Bf["rwv"], Bf["wT"], Bf["rwT"]

        def bc(row):
            return sm[:, row, :].unsqueeze(2).to_broadcast([128, 4, 128])
        tt = n % 4
        cs = slice(tt * 128, (tt + 1) * 128)
        gs = slice(n * 128, (n + 1) * 128)
        pt0, rpt0 = self.pt[0], self.r_pt[0]
        pt1, rpt1 = self.pt[1], self.r_pt[1]
        rq = [rYT[h] for h in range(4)]; rk = [rYT[4 + h] for h in range(4)]; rv = [rYT[8 + h] for h in range(4)]
        self.proj_tm(Wgab, rWgab, HT, rHT, n, pe[:, 0:8], rpe)
        P.op(DVE, lambda e: e.tensor_tensor(out=sm[:, GV, :], in0=pe[:, 0:4], in1=PP[:, dbo:dbo + 4], op=ALU.add), [self.rPP], [rsm, rpe])
        P.op(ACT, lambda e: e.activation(out=sm[:, GV, :], in_=sm[:, GV, :], func=AF.Exp), [rsm], [rsm])
        P.op(ACT, lambda e: e.activation(out=sm[:, GV, :], in_=sm[:, GV, :], func=AF.Ln, bias=1.0), [rsm], [rsm])
        P.op(DVE, lambda e: e.scalar_tensor_tensor(out=sm[:, GV, :], in0=sm[:, GV, :], scalar=-1.0, in1=eAg[:], op0=ALU.mult, op1=ALU.mult), [rsm, reAg], [rsm])
        P.op(ACT, lambda e: e.activation(out=sm[:, LNB, :], in_=pe[:, 4:8], func=AF.Exp, scale=-1.0), [], [rsm, rpe])
        P.op(ACT, lambda e: e.activation(out=sm[:, LNB, :], in_=sm[:, LNB, :], func=AF.Ln, bias=1.0), [rsm], [rsm])
        P.op(ACT, lambda e: e.activation(out=sm[:, BETA, :], in_=sm[:, LNB, :], func=AF.Exp, scale=-1.0), [rsm], [rsm])
        P.op(ACT, lambda e: e.activation(out=sqq[:], in_=YT[:, 0:4, cs], func=AF.Square), rq, [rsqq])
        P.op(ACT, lambda e: e.activation(out=sqk[:], in_=YT[:, 4:8, cs], func=AF.Square), rk, [rsqk])
        def mmg(e):
            e.matmul(pe[:, 8:12], lhsT=self.tri_incl[:], rhs=sm[:, GV, :], start=True, stop=True)
            ins = e.matmul(pe[:, 12:16], lhsT=self.ones_f[:], rhs=sm[:, GV, :], start=True, stop=True)
            for h in range(4):
                e.matmul(pe[:, 16 + h:17 + h], lhsT=sqk[:, h, :], rhs=self.ones_bf[:, 0:1], start=True, stop=True)
                ins = e.matmul(pe[:, 20 + h:21 + h], lhsT=sqq[:, h, :], rhs=self.ones_bf[:, 0:1], start=True, stop=True)
            return ins
        P.op(PE, mmg, [rsm, rsqq, rsqk, self.rconst], [rpe])
        P.op(ACT, lambda e: e.activation(out=sm[:, GC, :], in_=pe[:, 8:12], func=AF.Copy), [], [rsm, rpe])
        P.op(ACT, lambda e: e.activation(out=sm[:, EG, :], in_=pe[:, 8:12], func=AF.Exp), [], [rsm, rpe])
        P.op(ACT, lambda e: e.activation(out=sm[:, GLB, :], in_=pe[:, 12:16], func=AF.Exp), [], [rsm, rpe])
        P.op(DVE, lambda e: e.tensor_tensor(out=sm[:, GL, :], in0=pe[:, 12:16], in1=sm[:, GC, :], op=ALU.subtract), [rsm], [rsm, rpe])
        P.op(ACT, lambda e: e.activation(out=sm[:, EGLG, :], in_=sm[:, GL, :], func=AF.Exp), [rsm], [rsm])
        P.op(ACT, lambda e: e.activation(out=sm[:, LNRK, :], in_=pe[:, 16:20], func=AF.Ln, bias=self.eps_t[:, 0:1]), [], [rsm, rpe])
        P.op(DVE, lambda e: e.tensor_scalar(out=sm[:, LNRK, :], in0=sm[:, LNRK, :], scalar1=-0.5, scalar2=None, op0=ALU.mult), [rsm], [rsm])
        P.op(ACT, lambda e: e.activation(out=sm[:, RK, :], in_=sm[:, LNRK, :], func=AF.Exp), [rsm], [rsm])
        P.op(ACT, lambda e: e.activation(out=sm[:, CR, :], in_=sm[:, LNRK, :], func=AF.Exp, scale=-1.0), [rsm], [rsm])
        P.op(DVE, lambda e: e.tensor_tensor(out=sm[:, CD, :], in0=sm[:, RK, :], in1=sm[:, EGLG, :], op=ALU.mult), [rsm], [rsm])
        P.op(DVE, lambda e: e.tensor_tensor(out=sm[:, T1, :], in0=sm[:, RK, :], in1=sm[:, BETA, :], op=ALU.mult), [rsm], [rsm])
        P.op(DVE, lambda e: e.tensor_scalar(out=sm[:, T2, :], in0=pe[:, 20:24], scalar1=EPS, scalar2=128.0 * EPS, op0=ALU.add, op1=ALU.mult), [], [rsm, rpe])
        def mmgram(e):
            ins = None
            for h in range(4):
                e.matmul(pa[:, h * 128:(h + 1) * 128], lhsT=YT[:, 4 + h, cs], rhs=YT[:, 4 + h, cs], start=True, stop=True)
                ins = e.matmul(pb[:, h * 128:(h + 1) * 128], lhsT=YT[:, 4 + h, cs], rhs=YT[:, h, cs], start=True, stop=True)
            return ins
        P.op(PE, mmgram, rq + rk, [rpa, rpb])
        I4 = self.ident_f[:].unsqueeze(1).to_broadcast([128, 4, 128])
        P.op(DVE, lambda e: e.tensor_tensor(out=v4(TG4), in0=self.tri_incl[:].unsqueeze(1).to_broadcast([128, 4, 128]), in1=bc(GV), op=ALU.mult), [rsm, self.rconst], [rTG])
        P.op(DVE, lambda e: e.tensor_tensor(out=v4(DRA4), in0=I4, in1=bc(LNRK), op=ALU.mult), [rsm, self.rconst], [rDRA])
        def mmlg(e):
            ins = None
            for h in range(4):
                hs = slice(h * 128, (h + 1) * 128)
                e.matmul(pc_[:, hs], lhsT=sgt_c[:], rhs=TG4[:, hs], start=True, stop=False)
                e.matmul(pc_[:, hs], lhsT=DRA4[:, hs], rhs=self.ones_f[:], start=False, stop=False)
                ins = e.matmul(pc_[:, hs], lhsT=self.ident_f[:], rhs=self.neg_incl[:], start=False, stop=True)
            return ins
        P.op(PE, mmlg, [rTG, rDRA, rsgt_c, self.rconst], [rpcd])
        P.op(ACT, lambda e: e.activation(out=gA, in_=pc_, func=AF.Exp), [], [rgm, rpcd])
        P.op(DVE, lambda e: e.tensor_tensor(out=ATt[:].rearrange("p h c -> p (h c)"), in0=pb[:], in1=gA, op=ALU.mult), [rgm], [rAT, rpb])
        P.op(POOL, lambda e: e.tensor_tensor(out=gNs[:], in0=v4(gA), in1=bc(T1), op=ALU.mult), [rgm, rsm], [rgNs])
        P.op(POOL, lambda e: e.tensor_tensor(out=gNs[:], in0=gNs[:], in1=offd[:].unsqueeze(1).to_broadcast([128, 4, 128]), op=ALU.mult), [rgNs, roffd], [rgNs])
        P0, P0T, rP0, rP0T = Pc[0]
        P.op(DVE, lambda e: e.scalar_tensor_tensor(out=P0T, in0=pa[:], scalar=-1.0, in1=gNs[:].rearrange("p h c -> p (h c)"), op0=ALU.mult, op1=ALU.mult), [rgNs], [rP0T, rpa])
        def trp(e):
            ins = None
            for h in range(4):
                hs = slice(h * 128, (h + 1) * 128)
                ins = e.matmul(pb[:, hs], lhsT=P0T[:, hs], rhs=self.ident_f[:], start=True, stop=True)
            return ins
        P.op(PE, trp, [rP0T, self.rconst], [rpb])
        P.op(ACT, lambda e: e.activation(out=P0, in_=pb[:], func=AF.Copy), [], [rP0, rpb])
        def trkv(e):
            ins = None
            for h in range(4):
                e.transpose(pt0[:, h, :], YT[:, 4 + h, cs], self.ident_bf[:])
                ins = e.transpose(pt0[:, 4 + h, :], YT[:, 8 + h, cs], self.ident_bf[:])
            return ins
        P.op(PE, trkv, rk + rv, [rpt0])
        P.op(DVE, lambda e: e.tensor_tensor(out=Xf[:, :, 0:128], in0=pt0[:, 4:8, :], in1=bc(CR), op=ALU.mult), [rsm], [rXf, rpt0])
        P.op(DVE, lambda e: e.tensor_tensor(out=Xf[:, :, 128:256], in0=pt0[:, 0:4, :], in1=bc(EG), op=ALU.mult), [rsm], [rXf, rpt0])
        P.op(DVE, lambda e: e.tensor_tensor(out=kD[:], in0=pt0[:, 0:4, :], in1=bc(CD), op=ALU.mult), [rsm], [rkD, rpt0])
        P.op(POOL, lambda e: e.tensor_tensor(out=v4(TT), in0=v4(P0T), in1=self.ident_f[:].unsqueeze(1).to_broadcast([128, 4, 128]), op=ALU.add), [rP0T, self.rconst], [rTT])
        yield
        for j in range(6):
            Pj, PjT, rPj, rPjT = Pc[j % 2]
            Pn, PnT, rPn, rPnT = Pc[(j + 1) % 2]
            def mmsq(e, Pj=Pj, PjT=PjT):
                ins = None
                for h in range(4):
                    hs = slice(h * 128, (h + 1) * 128)
                    e.matmul(pa[:, hs], lhsT=PjT[:, hs], rhs=Pj[:, hs], start=True, stop=True)
                    ins = e.matmul(pb[:, hs], lhsT=Pj[:, hs], rhs=PjT[:, hs], start=True, stop=True)
                return ins
            P.op(PE, mmsq, [rPj, rPjT], [rpa, rpb])
            P.op(ACT, lambda e, Pn=Pn: e.activation(out=Pn, in_=pa[:], func=AF.Copy), [], [rPn, rpa])
            P.op(DVE, lambda e, PnT=PnT: e.tensor_copy(out=PnT, in_=pb[:]), [], [rPnT, rpb])
            def mmt(e, Pn=Pn):
                ins = None
                for h in range(4):
                    hs = slice(h * 128, (h + 1) * 128)
                    ins = e.matmul(pc_[:, hs], lhsT=Pn[:, hs], rhs=TT[:, hs], start=True, stop=True)
                return ins
            P.op(PE, mmt, [rPn, rTT], [rpcd])
            P.op(DVE, lambda e: e.tensor_tensor(out=TT, in0=pc_, in1=TT, op=ALU.add), [], [rTT, rpcd])
            yield
        def mmz(e):
            ins = None
            for h in range(4):
                ins = e.matmul(pcd[:, h * 256:(h + 1) * 256], lhsT=TT[:, h * 128:(h + 1) * 128], rhs=Xf[:, h, :], start=True, stop=True)
            return ins
        P.op(PE, mmz, [rTT, rXf], [rpcd])
        P.op(ACT, lambda e: e.activation(out=Xf[:].rearrange("p h c -> p (h c)"), in_=pcd[:], func=AF.Copy), [], [rXf, rpcd])
        Z, rZ = Xf, rXf
        P.op(DVE, lambda e: e.tensor_tensor(out=u0[:], in0=Z[:, :, 0:128], in1=bc(T1), op=ALU.mult), [rZ, rsm], [ru0])
        P.op(POOL, lambda e: e.tensor_tensor(out=wv[:], in0=Z[:, :, 128:256], in1=bc(T1), op=ALU.mult), [rZ, rsm], [rwv])
        def trw(e):
            ins = None
            for h in range(4):
                ins = e.transpose(pt1[:, 4 + h, :], wv[:, h, :], self.ident_bf[:])
            return ins
        P.op(PE, trw, [rwv], [rpt1])
        P.op(ACT, lambda e: e.activation(out=wT[:], in_=pt1[:, 4:8, :], func=AF.Copy), [], [rwT, rpt1])
        if n > 0:
            def mm1(e):
                ins = None
                for h in range(4):
                    ins = e.matmul(pa[:, h * 128:(h + 1) * 128], lhsT=wT[:, h, :], rhs=Sbf[:, h, :], start=True, stop=True)
                return ins
            P.op(PE, mm1, [rwT, rSbf], [rpa])
            P.op(DVE, lambda e: e.tensor_tensor(out=ubf[:].rearrange("p h c -> p (h c)"), in0=u0[:].rearrange("p h c -> p (h c)"), in1=pa[:], op=ALU.subtract), [ru0], [rubf, rpa])
            def mm2(e):
                ins = None
                for h in range(4):
                    ins = e.matmul(pb[:, h * 128:(h + 1) * 128], lhsT=YT[:, h, cs], rhs=Sbf[:, h, :], start=True, stop=True)
                return ins
            P.op(PE, mm2, rq + [rSbf], [rpb])
        else:
            P.op(DVE, lambda e: e.tensor_copy(out=ubf[:], in_=u0[:]), [ru0], [rubf])
        def mm3(e):
            ins = None
            for h in range(4):
                ins = e.matmul(pc_[:, h * 128:(h + 1) * 128], lhsT=ATt[:, h, :], rhs=ubf[:, h, :], start=True, stop=True)
            return ins
        P.op(PE, mm3, [rAT, rubf], [rpcd])
        if n > 0:
            P.op(DVE, lambda e: e.tensor_tensor(out=v4(otmp), in0=v4(pb[:]), in1=bc(EG), op=ALU.mult), [rsm], [ro, rpb])
            P.op(DVE, lambda e: e.tensor_tensor(out=o_t, in0=pc_, in1=otmp, op=ALU.add), [], [ro, rpcd])
        else:
            P.op(DVE, lambda e: e.tensor_copy(out=o_t, in_=pc_), [], [ro, rpcd])
        if n < NT - 1:
            def mm4(e):
                ins = None
                for h in range(4):
                    ins = e.matmul(pd_[:, h * 128:(h + 1) * 128], lhsT=kD[:, h, :], rhs=ubf[:, h, :], start=True, stop=True)
                return ins
            P.op(PE, mm4, [rkD, rubf], [rpcd])
            if n == 0:
                P.op(DVE, lambda e: e.tensor_copy(out=Sst[:].rearrange("p h c -> p (h c)"), in_=pd_), [], [rS, rpcd])
            else:
                P.op(DVE, lambda e: e.tensor_tensor(out=Sst[:], in0=Sst[:], in1=bc(GLB), op=ALU.mult), [rS, rsm], [rS])
                P.op(DVE, lambda e: e.tensor_tensor(out=Sst[:].rearrange("p h c -> p (h c)"), in0=pd_, in1=Sst[:].rearrange("p h c -> p (h c)"), op=ALU.add), [rS], [rS, rpcd])
            P.op(ACT, lambda e: e.activation(out=Sbf[:], in_=Sst[:], func=AF.Copy), [rS], [rSbf])
        P.op(POOL, lambda e: e.tensor_tensor(out=sq_t, in0=o_t, in1=o_t, op=ALU.mult), [ro], [rsq])
        P.op(DVE, lambda e: e.tensor_reduce(out=sm[:, MSR, :], in_=v4(sq_t), axis=AX.X, op=ALU.add), [rsq], [rsm])
        P.op(DVE, lambda e: e.scalar_tensor_tensor(out=sm[:, MSR, :], in0=sm[:, MSR, :], scalar=1.0 / 128, in1=sm[:, T2, :], op0=ALU.mult, op1=ALU.add), [rsm], [rsm])
        P.op(ACT, lambda e: e.activation(out=sm[:, RSTD, :], in_=sm[:, MSR, :], func=AF.Sqrt), [rsm], [rsm])
        P.op(DVE, lambda e: e.reciprocal(out=sm[:, RSTD, :], in_=sm[:, RSTD, :]), [rsm], [rsm])
        P.op(DVE, lambda e: e.tensor_tensor(out=v4(o_t), in0=v4(o_t), in1=bc(RSTD), op=ALU.mult), [ro, rsm], [ro])
        P.op(POOL, lambda e: e.tensor_tensor(out=v4(o_t), in0=v4(o_t), in1=PP[:, ngo:ngo + 128].unsqueeze(1).to_broadcast([128, 4, 128]), op=ALU.mult), [ro, self.rPP], [ro])
        self.proj_tm(Wgg, rWgg, HT, rHT, n, pg[:], rpg)
        P.op(ACT, lambda e: e.activation(out=sgg, in_=pg[:], func=AF.Silu), [], [rsgg, rpg])
        P.op(POOL, lambda e: e.tensor_tensor(out=ybf, in0=o_t, in1=sgg, op=ALU.mult), [ro, rsgg], [rybf])
        pgb = pg[:].bitcast(BF16)
        def try_(e):
            ins = None
            for c in range(4):
                ins = e.transpose(pgb[:, c * 128:(c + 1) * 128], ybf[:, c * 128:(c + 1) * 128], self.ident_bf[:])
            return ins
        P.op(PE, try_, [rybf], [rpg])
        P.op(ACT, lambda e: e.activation(out=MT[:, 4:8, gs], in_=pgb[:, 0:512].rearrange("p (c t) -> p c t", c=4), func=AF.Copy), [], [rMT[n], rpg])

    def run(gens):
        alive = list(gens)
        for gn in alive:
            next(gn)
        for j in range(6):
            for gn in alive:
                next(gn)
        for gn in alive:
            for _ in gn:
                pass

    for b in range(NB):
        for c in range(12):
            conv_chunk(b, c)
        run([chunk(b * 4 + 0), chunk(b * 4 + 1)])
        run([chunk(b * 4 + 2), chunk(b * 4 + 3)])


Builder.gdn = _gdn
```
